# Optimizing a Trainium2 kernel written in Bass

```python
import math
import jax, jax.numpy as jnp
from jax import lax
import numpy as np

D_MODEL = 1024
BATCH = 4
SEQ = 4096
DEPTH = 2
DEC_BATCH = 32
DEC_SEQ = 1
PAST_LEN = 8192
PAGE_SIZE = 128

N_A = DEPTH // 2
N_B = DEPTH - N_A
LRU_WIDTH = D_MODEL
N_LRU_BLOCKS = 4
LRU_BLOCK = LRU_WIDTH // N_LRU_BLOCKS
CONV_A = 4
LRU_C = 8.0
D_FF = 3 * D_MODEL
CONV_F = 3
HEAD_DIM = 64
HEADS_PER_GROUP = 8
GROUPS = ((128, 1), (512, 4), (2048, 16))
N_GROUPS = len(GROUPS)
Q_WIDTH = N_GROUPS * HEADS_PER_GROUP * HEAD_DIM
ATT_OUT = HEADS_PER_GROUP * HEAD_DIM
EPS = 1e-6

kernel_name = 'yoco_hawk_dilated_swa_step'


def rmsnorm(x, g):
    xf = x.astype(jnp.float32)
    y = xf * lax.rsqrt(jnp.mean(xf * xf, axis=-1, keepdims=True) + EPS)
    return (y * g.astype(jnp.float32)).astype(x.dtype)


def causal_dwconv(x, buf, w, b):
    K = w.shape[0]
    T = x.shape[1]
    xp = jnp.concatenate([buf.astype(x.dtype), x], axis=1)
    y = b + sum(w[k] * xp[:, k:k + T] for k in range(K))
    return y, xp[:, T:]


def block_diag(x, w, b):
    B_, T = x.shape[:2]
    xb = x.reshape(B_, T, N_LRU_BLOCKS, LRU_BLOCK)
    return jnp.einsum('btni,nij->btnj', xb, w.astype(jnp.float32)).reshape(B_, T, LRU_WIDTH) + b.astype(jnp.float32)


def rg_lru(x, h0, w_a, b_a, w_x, b_x, lam):
    xf = x.astype(jnp.float32)
    r = jax.nn.sigmoid(block_diag(xf, w_a, b_a))
    i = jax.nn.sigmoid(block_diag(xf, w_x, b_x))
    log_a = LRU_C * r * jax.nn.log_sigmoid(lam.astype(jnp.float32))
    a = jnp.exp(log_a)
    bt = jnp.sqrt(-jnp.expm1(2.0 * log_a)) * (i * xf)
    bt = bt.at[:, 0].add(a[:, 0] * h0.astype(jnp.float32))

    def combine(left, right):
        a1, b1 = left
        a2, b2 = right
        return a1 * a2, a2 * b1 + b2

    _, h = lax.associative_scan(combine, (a, bt), axis=1)
    return h.astype(x.dtype), h[:, -1].astype(x.dtype)


def recurrent_block(x, h0, conv_buf, w_in, conv_w, conv_b, w_a, b_a, w_x, b_x, lam, w_out):
    u = x @ w_in
    gate = jax.nn.gelu(u[..., :LRU_WIDTH])
    xc, new_buf = causal_dwconv(u[..., LRU_WIDTH:], conv_buf, conv_w, conv_b)
    h, h_last = rg_lru(xc, h0, w_a, b_a, w_x, b_x, lam)
    return (h * gate) @ w_out, h_last, new_buf


def conv_ffn(x, buf, w_up, conv_w, conv_b, w_down):
    u = x @ w_up
    g, new_buf = causal_dwconv(u[..., :D_FF], buf, conv_w, conv_b)
    return (jax.nn.gelu(g) * u[..., D_FF:]) @ w_down, new_buf


def dilated_window_prompt(q, k, v, window, dilation):
    B_, T, H, Dh = q.shape
    blk = window // dilation
    n = T // dilation
    nb = -(-n // blk)
    pad = nb * blk - n

    def to_blocks(t):
        t = t.reshape(B_, n, dilation, H, Dh).transpose(0, 2, 1, 3, 4)
        t = jnp.pad(t, ((0, 0), (0, 0), (0, pad), (0, 0), (0, 0)))
        return t.reshape(B_, dilation, nb, blk, H, Dh)

    def with_prev(t):
        prev = jnp.pad(t[:, :, :-1], ((0, 0), (0, 0), (1, 0), (0, 0), (0, 0), (0, 0)))
        return jnp.concatenate([prev, t], axis=3)

    qb = to_blocks(q).astype(jnp.float32)
    kk = with_prev(to_blocks(k)).astype(jnp.float32)
    vv = with_prev(to_blocks(v)).astype(jnp.float32)
    s = jnp.einsum('brnqhd,brnkhd->brnhqk', qb, kk) * (Dh ** -0.5)
    qi = jnp.arange(blk)[:, None]
    ki = jnp.arange(2 * blk)[None, :]
    dist = blk + qi - ki
    key_step = (jnp.arange(nb)[:, None, None] - 1) * blk + ki[None]
    valid = (dist >= 0) & (dist <= blk) & (key_step >= 0)
    s = jnp.where(valid[None, None, :, None], s, -jnp.inf)
    m = jnp.max(s, axis=-1, keepdims=True)
    p = jnp.exp(s - m)
    den = jnp.sum(p, axis=-1, keepdims=True)
    o = jnp.einsum('brnhqk,brnkhd->brnqhd', p, vv) / jnp.swapaxes(den, 3, 4)
    lse = jnp.swapaxes((m + jnp.log(den))[..., 0], 3, 4)
    o = o.reshape(B_, dilation, nb * blk, H, Dh)[:, :, :n].transpose(0, 2, 1, 3, 4).reshape(B_, T, H, Dh)
    lse = lse.reshape(B_, dilation, nb * blk, H)[:, :, :n].transpose(0, 2, 1, 3).reshape(B_, T, H)
    return o, lse


def dilated_window_sample(q, ke, ve, buf_len, window, dilation):
    S = q.shape[1]
    Dh = q.shape[-1]
    nk = window // dilation + 1
    idx = buf_len + jnp.arange(S)[:, None] - dilation * jnp.arange(nk)[None, :]
    valid = idx >= 0
    idxc = jnp.maximum(idx, 0)
    kg = ke[:, idxc].astype(jnp.float32)
    vg = ve[:, idxc].astype(jnp.float32)
    s = jnp.einsum('bshd,bsjhd->bshj', q.astype(jnp.float32), kg) * (Dh ** -0.5)
    s = jnp.where(valid[None, :, None, :], s, -jnp.inf)
    m = jnp.max(s, axis=-1, keepdims=True)
    p = jnp.exp(s - m)
    den = jnp.sum(p, axis=-1, keepdims=True)
    o = jnp.einsum('bshj,bsjhd->bshd', p, vg) / den
    return o, (m + jnp.log(den))[..., 0]


def trunk(x, lru_h, conv_a_buf, ffn_buf, kv_bufs,
          norm_mix, a_w_in, a_conv_w, a_conv_b, a_gate_a_w, a_gate_a_b, a_gate_x_w, a_gate_x_b,
          a_lambda, a_w_out, kv_norm, w_kv, b_w_q, b_w_o,
          norm_ffn, ffn_w_up, ffn_conv_w, ffn_conv_b, ffn_w_down, final_norm):
    Bsz, T = x.shape[:2]
    h = x
    new_lru, new_conva, new_ffn, kv_state = [], [], [], []
    ks, vs, buf_lens = [], [], []
    for l in range(DEPTH):
        if l < N_A:
            a_out, h_last, cbuf = recurrent_block(
                rmsnorm(h, norm_mix[l]), lru_h[l], conv_a_buf[l], a_w_in[l], a_conv_w[l], a_conv_b[l],
                a_gate_a_w[l], a_gate_a_b[l], a_gate_x_w[l], a_gate_x_b[l], a_lambda[l], a_w_out[l])
            h = h + a_out
            new_lru.append(h_last)
            new_conva.append(cbuf)
        else:
            if l == N_A:
                kv = rmsnorm(h, kv_norm) @ w_kv
                k_all = kv[..., :Q_WIDTH].reshape(Bsz, T, N_GROUPS, HEADS_PER_GROUP, HEAD_DIM)
                v_all = kv[..., Q_WIDTH:].reshape(Bsz, T, N_GROUPS, HEADS_PER_GROUP, HEAD_DIM)
                for g, (win, dil) in enumerate(GROUPS):
                    kg, vg = k_all[:, :, g], v_all[:, :, g]
                    if kv_bufs is None:
                        keep = min(win, T)
                        ks.append(kg)
                        vs.append(vg)
                        buf_lens.append(0)
                        kv_state += [kg[:, T - keep:], vg[:, T - keep:]]
                    else:
                        kb, vb = kv_bufs[2 * g], kv_bufs[2 * g + 1]
                        ke = jnp.concatenate([kb.astype(kg.dtype), kg], axis=1)
                        ve = jnp.concatenate([vb.astype(vg.dtype), vg], axis=1)
                        keep = min(win, ke.shape[1])
                        ks.append(ke)
                        vs.append(ve)
                        buf_lens.append(kb.shape[1])
                        kv_state += [ke[:, ke.shape[1] - keep:], ve[:, ve.shape[1] - keep:]]
            j = l - N_A
            q = (rmsnorm(h, norm_mix[l]) @ b_w_q[j]).reshape(Bsz, T, N_GROUPS, HEADS_PER_GROUP, HEAD_DIM)
            outs, lses = [], []
            for g, (win, dil) in enumerate(GROUPS):
                if kv_bufs is None:
                    o, lse = dilated_window_prompt(q[:, :, g], ks[g], vs[g], win, dil)
                else:
                    o, lse = dilated_window_sample(q[:, :, g], ks[g], vs[g], buf_lens[g], win, dil)
                outs.append(o)
                lses.append(lse)
            wgt = jax.nn.softmax(jnp.stack(lses, axis=0), axis=0)
            o = jnp.sum(wgt[..., None] * jnp.stack(outs, axis=0), axis=0)
            h = h + o.reshape(Bsz, T, ATT_OUT).astype(h.dtype) @ b_w_o[j]
        f_out, fbuf = conv_ffn(rmsnorm(h, norm_ffn[l]), ffn_buf[l], ffn_w_up[l], ffn_conv_w[l],
                               ffn_conv_b[l], ffn_w_down[l])
        h = h + f_out
        new_ffn.append(fbuf)
    y = rmsnorm(h, final_norm)
    return y, jnp.stack(new_lru), jnp.stack(new_conva), jnp.stack(new_ffn), kv_state


def setup_inputs(seed: int = 0) -> dict:
    key = jax.random.key(seed)
    ks = jax.random.split(key, 40)
    f32 = jnp.float32
    nrm = lambda k, shape, scale: jax.random.normal(k, shape, f32) * scale
    lens = [min(w, PAST_LEN) for (w, _) in GROUPS]
    u = jax.random.uniform(ks[13], (N_A, LRU_WIDTH), f32, 0.9, 0.999)
    return {
        'x_prompt': nrm(ks[0], (BATCH, SEQ, D_MODEL), 1.0),
        'x_sample': nrm(ks[1], (DEC_BATCH, DEC_SEQ, D_MODEL), 1.0),
        'state_lru_h': nrm(ks[2], (N_A, DEC_BATCH, LRU_WIDTH), 0.5),
        'state_conv_a': nrm(ks[3], (N_A, DEC_BATCH, CONV_A - 1, LRU_WIDTH), 0.5),
        'state_ffn_conv': nrm(ks[4], (DEPTH, DEC_BATCH, CONV_F - 1, D_FF), 0.5),
        'cache_k0': nrm(ks[5], (DEC_BATCH, lens[0], HEADS_PER_GROUP, HEAD_DIM), 1.0),
        'cache_v0': nrm(ks[6], (DEC_BATCH, lens[0], HEADS_PER_GROUP, HEAD_DIM), 1.0),
        'cache_k1': nrm(ks[7], (DEC_BATCH, lens[1], HEADS_PER_GROUP, HEAD_DIM), 1.0),
        'cache_v1': nrm(ks[8], (DEC_BATCH, lens[1], HEADS_PER_GROUP, HEAD_DIM), 1.0),
        'cache_k2': nrm(ks[9], (DEC_BATCH, lens[2], HEADS_PER_GROUP, HEAD_DIM), 1.0),
        'cache_v2': nrm(ks[10], (DEC_BATCH, lens[2], HEADS_PER_GROUP, HEAD_DIM), 1.0),
        'norm_mix': 1.0 + nrm(ks[11], (DEPTH, D_MODEL), 0.01),
        'a_w_in': nrm(ks[12], (N_A, D_MODEL, 2 * LRU_WIDTH), D_MODEL ** -0.5),
        'a_conv_w': nrm(ks[14], (N_A, CONV_A, LRU_WIDTH), CONV_A ** -0.5),
        'a_conv_b': nrm(ks[15], (N_A, LRU_WIDTH), 0.01),
        'a_gate_a_w': nrm(ks[16], (N_A, N_LRU_BLOCKS, LRU_BLOCK, LRU_BLOCK), LRU_BLOCK ** -0.5),
        'a_gate_a_b': nrm(ks[17], (N_A, LRU_WIDTH), 0.01),
        'a_gate_x_w': nrm(ks[18], (N_A, N_LRU_BLOCKS, LRU_BLOCK, LRU_BLOCK), LRU_BLOCK ** -0.5),
        'a_gate_x_b': nrm(ks[19], (N_A, LRU_WIDTH), 0.01),
        'a_lambda': jnp.log(u) - jnp.log1p(-u),
        'a_w_out': nrm(ks[20], (N_A, LRU_WIDTH, D_MODEL), LRU_WIDTH ** -0.5),
        'kv_norm': 1.0 + nrm(ks[21], (D_MODEL,), 0.01),
        'w_kv': nrm(ks[22], (D_MODEL, 2 * Q_WIDTH), D_MODEL ** -0.5),
        'b_w_q': nrm(ks[23], (N_B, D_MODEL, Q_WIDTH), D_MODEL ** -0.5),
        'b_w_o': nrm(ks[24], (N_B, ATT_OUT, D_MODEL), ATT_OUT ** -0.5),
        'norm_ffn': 1.0 + nrm(ks[25], (DEPTH, D_MODEL), 0.01),
        'ffn_w_up': nrm(ks[26], (DEPTH, D_MODEL, 2 * D_FF), D_MODEL ** -0.5),
        'ffn_conv_w': nrm(ks[27], (DEPTH, CONV_F, D_FF), CONV_F ** -0.5),
        'ffn_conv_b': nrm(ks[28], (DEPTH, D_FF), 0.01),
        'ffn_w_down': nrm(ks[29], (DEPTH, D_FF, D_MODEL), D_FF ** -0.5),
        'final_norm': 1.0 + nrm(ks[30], (D_MODEL,), 0.01),
    }


def reference(x_prompt, x_sample, state_lru_h, state_conv_a, state_ffn_conv,
              cache_k0, cache_v0, cache_k1, cache_v1, cache_k2, cache_v2,
              norm_mix, a_w_in, a_conv_w, a_conv_b, a_gate_a_w, a_gate_a_b, a_gate_x_w, a_gate_x_b,
              a_lambda, a_w_out, kv_norm, w_kv, b_w_q, b_w_o,
              norm_ffn, ffn_w_up, ffn_conv_w, ffn_conv_b, ffn_w_down, final_norm):
    dt = x_prompt.dtype
    zero_h = jnp.zeros((N_A, BATCH, LRU_WIDTH), dt)
    zero_ca = jnp.zeros((N_A, BATCH, CONV_A - 1, LRU_WIDTH), dt)
    zero_cf = jnp.zeros((DEPTH, BATCH, CONV_F - 1, D_FF), dt)
    y_prompt, p_lru_h, p_conv_a, p_ffn, p_kv = trunk(
        x_prompt, zero_h, zero_ca, zero_cf, None,
        norm_mix, a_w_in, a_conv_w, a_conv_b, a_gate_a_w, a_gate_a_b, a_gate_x_w, a_gate_x_b,
        a_lambda, a_w_out, kv_norm, w_kv, b_w_q, b_w_o,
        norm_ffn, ffn_w_up, ffn_conv_w, ffn_conv_b, ffn_w_down, final_norm)
    y_sample, s_lru_h, s_conv_a, s_ffn, s_kv = trunk(
        x_sample, state_lru_h, state_conv_a, state_ffn_conv,
        (cache_k0, cache_v0, cache_k1, cache_v1, cache_k2, cache_v2),
        norm_mix, a_w_in, a_conv_w, a_conv_b, a_gate_a_w, a_gate_a_b, a_gate_x_w, a_gate_x_b,
        a_lambda, a_w_out, kv_norm, w_kv, b_w_q, b_w_o,
        norm_ffn, ffn_w_up, ffn_conv_w, ffn_conv_b, ffn_w_down, final_norm)
    p_k0, p_v0, p_k1, p_v1, p_k2, p_v2 = p_kv
    s_k0, s_v0, s_k1, s_v1, s_k2, s_v2 = s_kv
    return (y_prompt, y_sample, p_lru_h, s_lru_h, p_conv_a, s_conv_a, p_ffn, s_ffn,
            p_k0, p_v0, s_k0, s_v0, p_k1, p_v1, s_k1, s_v1, p_k2, p_v2, s_k2, s_v2)
```

```python
import numpy as np
from contextlib import ExitStack
import concourse.bass as bass
import concourse.mybir as mybir
from concourse.bass_utils import run_bass_kernel_spmd

F32 = mybir.dt.float32
BF16 = mybir.dt.bfloat16
AF = mybir.ActivationFunctionType
ALU = mybir.AluOpType

ENGS = ("pe", "dve", "act", "pool", "sp")


class Buf:
    __slots__ = ("name", "last_w", "readers", "semkey")

    def __init__(self, name, semkey=None):
        self.name = name
        self.last_w = None
        self.readers = []
        self.semkey = semkey if semkey is not None else ("buf", name)


class Prog:
    def __init__(self):
        self.ops = []
        self.dma_count = {}
        self.total_keys = set()

    def _deps(self, reads, writes, opid):
        deps = set()
        for b in reads:
            if b.last_w is not None:
                deps.add(b.last_w)
        for b in writes:
            if b.last_w is not None:
                deps.add(b.last_w)
            lastr = {}
            for r in b.readers:
                o = self.ops[r]
                if o["dma"] is not None:
                    deps.add(r)
                else:
                    lastr[o["eng"]] = r
            deps.update(lastr.values())
        for b in reads:
            b.readers.append(opid)
        for b in writes:
            b.last_w = opid
            b.readers = []
        deps.discard(opid)
        return deps

    def op(self, eng, emit, reads=(), writes=()):
        opid = len(self.ops)
        deps = self._deps(reads, writes, opid)
        self.ops.append(dict(id=opid, eng=eng, emit=emit, deps=deps, dma=None))
        return opid

    def dma(self, queue, emit, reads=(), writes=(), sem=None, inc=16, serial=False):
        opid = len(self.ops)
        deps = self._deps(reads, writes, opid)
        key = sem.semkey
        if serial:
            if not hasattr(self, "last_serial"):
                self.last_serial = {}
            if key in self.last_serial:
                deps.add(self.last_serial[key])
            self.last_serial[key] = opid
        cnt = self.dma_count.get(key, 0) + inc
        self.dma_count[key] = cnt
        self.ops.append(dict(id=opid, eng=queue, emit=emit, deps=deps, dma=(key, cnt, inc)))
        return opid

    def barrier(self):
        last = {}
        for o in self.ops:
            if o["emit"] is None:
                continue
            if o["dma"] is not None and o["dma"][0] in self.total_keys:
                continue
            k = ("e", o["eng"]) if o["dma"] is None else ("d", o["dma"][0])
            last[k] = o["id"]
        deps = set(last.values())
        for e in ENGS:
            self.ops.append(dict(id=len(self.ops), eng=e, emit=None, deps=set(deps), dma=None))

    def emit(self, nc, stack, block, final_wait_all=True):
        ops = self.ops
        needed = set()
        for o in ops:
            for dep in o["deps"]:
                d = ops[dep]
                if o["eng"] == "pe" and d["eng"] == "pe" and d["dma"] is None and o["dma"] is None:
                    continue
                needed.add(dep)
        ms = {e: 0 for e in ENGS}
        for o in ops:
            if o["dma"] is None and o["id"] in needed:
                ms[o["eng"]] += 1
                o["ms"] = ms[o["eng"]]
        prog_sem = {e: stack.enter_context(nc.semaphore("prog_" + e)) for e in ENGS}
        dma_sem = {}
        for key in self.dma_count:
            dma_sem[key] = stack.enter_context(nc.semaphore("dma_%d" % len(dma_sem)))
        self.n_sems = len(prog_sem) + len(dma_sem)

        def token(dep, at_id):
            d = ops[dep]
            if d["dma"] is not None:
                key, cnt, _inc = d["dma"]
                if key in self.total_keys:
                    cnt = self.dma_count[key]
                return dma_sem[key], cnt, ("dma", key)
            return prog_sem[d["eng"]], d["ms"], ("eng", d["eng"])

        per_eng = {e: [o for o in ops if o["eng"] == e] for e in ENGS}
        final_tokens = []
        if final_wait_all:
            for key, cnt in self.dma_count.items():
                final_tokens.append((dma_sem[key], cnt))

        def run_engine(ename, eobj):
            waited = {}
            for o in per_eng[ename]:
                for dep in sorted(o["deps"]):
                    d = ops[dep]
                    if ename == "pe" and d["eng"] == "pe" and d["dma"] is None:
                        continue
                    sem, val, k = token(dep, o["id"])
                    if waited.get(k, 0) < val:
                        eobj.wait_ge(sem, val)
                        waited[k] = val
                if o["emit"] is None:
                    continue
                ins = o["emit"](eobj)
                if o["dma"] is not None:
                    ins.then_inc(dma_sem[o["dma"][0]], o["dma"][2])
                elif "ms" in o:
                    ins.then_inc(prog_sem[ename], 1)
            if ename == "sp":
                for sem, val in final_tokens:
                    eobj.wait_ge(sem, val)
                for e in ENGS:
                    if ms[e] > 0:
                        eobj.wait_ge(prog_sem[e], ms[e])

        @block.tensor
        def _(e):
            run_engine("pe", e)

        @block.vector
        def _(e):
            run_engine("dve", e)

        @block.scalar
        def _(e):
            run_engine("act", e)

        @block.gpsimd
        def _(e):
            run_engine("pool", e)

        @block.sync
        def _(e):
            run_engine("sp", e)


NT = 2048
NB = NT // 128
D = 1024
KC = 8
DFF = 3072
FC = 24
EPS = 1e-6
NEG = -30000.0
NS = 4
CV_NM0, CV_NM1, CV_NF0, CV_NF1, CV_KVN = 0, 8, 16, 24, 32
CV_CW, CV_CB, CV_BA, CV_BX, CV_LAM = 40, 72, 80, 88, 96
CV_CA = 104
CV_FIN = 112
CV_F = (128, 224)
NCV = 320
NSLOT = 21


def bc_last(ap, n):
    return bass.AP(ap.tensor, ap.offset, [list(x) for x in ap.ap] + [[0, n]])


class Ring:
    def __init__(self, items):
        self.items = items
        self.i = 0

    def next(self):
        it = self.items[self.i % len(self.items)]
        self.i += 1
        return it


class Region:
    def __init__(self, arena, lo, hi):
        self.arena, self.lo, self.hi, self.p = arena, lo, hi, lo

    def reset(self):
        self.p = self.lo

    def alloc(self, shape, dt):
        esz = 4 if dt == F32 else 2
        n = esz
        for s in shape[1:]:
            n *= s
        n = (n + 31) // 32 * 32
        off = self.p
        self.p += n
        assert self.p <= self.hi, ("region overflow", self.lo, self.hi, self.p, shape)
        v = self.arena[:, off // 4:(off + n) // 4]
        if dt != F32:
            v = v.bitcast(dt)
        tot = 1
        for s in shape[1:]:
            tot *= s
        v = v[:, 0:tot]
        if len(shape) == 3:
            v = v.rearrange("p (a b) -> p a b", a=shape[1])
        elif len(shape) == 4:
            v = v.rearrange("p (a b c) -> p a b c", a=shape[1], b=shape[2])
        elif len(shape) == 5:
            v = v.rearrange("p (a b c d) -> p a b c d", a=shape[1], b=shape[2], c=shape[3])
        return v


class Ctx:
    def __init__(self, nc):
        self.nc = nc
        self.P = Prog()
        self.uid = 0

    def sb(self, scope, name, shape, dt):
        if isinstance(scope, Region):
            return scope.alloc(list(shape), dt)
        return scope.enter_context(self.nc.sbuf_tensor(name, list(shape), dt))

    def newkey(self):
        k = ("pool", getattr(self, "keyi", 0))
        self.keyi = getattr(self, "keyi", 0) + 1
        return k

    def reset_keys(self):
        self.keyi = 0

    def barrier(self):
        self.P.barrier()
        self.reset_keys()

    def ring(self, scope, name, n, shape, dt):
        items = []
        for i in range(n):
            t = self.sb(scope, "%s%d" % (name, i), shape, dt)
            items.append((t, Buf("%s%d" % (name, i), semkey=self.newkey())))
        return Ring(items)

    def pe(self, fn, r=(), w=()):
        self.P.op("pe", fn, r, w)

    def dve(self, fn, r=(), w=()):
        self.P.op("dve", fn, r, w)

    def act(self, fn, r=(), w=()):
        self.P.op("act", fn, r, w)

    def pool(self, fn, r=(), w=()):
        self.P.op("pool", fn, r, w)

    def dma(self, fn, r=(), w=(), sem=None, q="sp", serial=False):
        self.P.dma(q, fn, r, w, sem=sem, serial=serial)


def unit_tok(g, u):
    if g == 0:
        return 128 * u, 1
    if g == 1:
        return 512 * (u % 4) + (u // 4), 4
    return u, 16


def unit_slot(g, u):
    if g == 2:
        return u
    if g == 1:
        return 16 + u // 4 if u % 4 == 3 else None
    return 20 if u == 15 else None


def unit_prev(g, u):
    if g == 0:
        return ("own", u - 1) if u > 0 else ("partner", 20)
    if g == 1:
        return ("own", u - 1) if u % 4 > 0 else ("partner", 16 + u // 4)
    return ("partner", u)


class StopBuild(Exception):
    pass


def build_program(upto=99, dbg=False):
    nc = bass.Bass("TRN2", target_bir_lowering=False)
    C = Ctx(nc)
    P = C.P

    dbg_bufs = {}

    def dump(name, ap, bufs, shape):
        if not dbg:
            return
        dd = nc.dram_tensor("d_" + name, list(shape), ap.dtype if hasattr(ap, "dtype") else F32, kind="ExternalOutput").ap()
        P.dma("sp", lambda e: e.dma_start(out=dd, in_=ap), reads=list(bufs), writes=[], sem=B_ser, serial=True)

    def chk(n):
        if upto == n:
            raise StopBuild()

    def din(name, shape, dt=F32):
        return nc.dram_tensor(name, list(shape), dt, kind="ExternalInput").ap()

    def dout(name, shape, dt=F32):
        return nc.dram_tensor(name, list(shape), dt, kind="ExternalOutput").ap()

    def dint(name, shape, dt):
        return nc.dram_tensor(name, list(shape), dt)

    xp = din("xp", [NT, D])
    xsm = din("xsm", [8, D])
    w_in = din("a_w_in", [D, 2 * D])
    a_conv_w = din("a_conv_w", [4, D])
    a_conv_b = din("a_conv_b", [D])
    w_ga = din("a_gate_a_w", [4, 256, 256])
    b_ga = din("a_gate_a_b", [D])
    w_gx = din("a_gate_x_w", [4, 256, 256])
    b_gx = din("a_gate_x_b", [D])
    a_lam = din("a_lambda", [D])
    w_out = din("a_w_out", [D, D])
    kv_norm = din("kv_norm", [D])
    w_kv = din("w_kv", [D, 3072])
    w_q = din("b_w_q", [D, 1536])
    w_o = din("b_w_o", [512, D])
    norm_mix = din("norm_mix", [2, D])
    norm_ffn = din("norm_ffn", [2, D])
    w_up = din("ffn_w_up", [2, D, 2 * DFF])
    f_conv_w = din("ffn_conv_w", [2, 3, DFF])
    f_conv_b = din("ffn_conv_b", [2, DFF])
    w_down = din("ffn_w_down", [2, DFF, D])
    final_norm = din("final_norm", [D])
    cmask = din("cmask", [128, 3, 128])
    pmask_d = din("pmask", [128, 1])

    s_lru0 = din("s_lru0", [NS, D])
    s_conva0 = din("s_conva0", [NS * 3, D])
    s_ffn0 = din("s_ffn0", [2, NS * 2, DFF])
    WG = (128, 512, 2048)
    ck = [din("ck%d" % g, [NS, WG[g], 512]) for g in range(3)]
    cv = [din("cv%d" % g, [NS, WG[g], 512]) for g in range(3)]
    ys_out = dout("ys", [NS, D])
    o_slru = dout("o_slru", [NS, D])
    o_sconva = dout("o_sconva", [NS * 3, D])
    o_sffn = dout("o_sffn", [2, NS * 2, DFF])
    sk = [dout("sk%d" % g, [NS, WG[g], 512]) for g in range(3)]
    sv = [dout("sv%d" % g, [NS, WG[g], 512]) for g in range(3)]
    y_out = dout("y", [NT, D])
    o_lru = dout("o_lru", [8, 128])
    o_conva = dout("o_conva", [3, D])
    o_ffn = dout("o_ffn", [2, 2, DFF])
    o_k = [dout("o_k%d" % g, [n, 512]) for g, n in enumerate((128, 512, 2048))]
    o_v = [dout("o_v%d" % g, [n, 512]) for g, n in enumerate((128, 512, 2048))]

    qt_scr = dint("qt_scr", [3, 4, 128, NT], BF16)
    kt_scr = dint("kt_scr", [3, 4, 128, NT], BF16)
    v_scr = dint("v_scr", [3, 16, 128, 512], BF16)
    xs_k = [dint("xs_k%d" % j, [28, 16384], BF16) for j in range(3)]
    xr_k = [dint("xr_k%d" % j, [56, 16384], BF16) for j in range(3)]
    xs_v = [dint("xs_v%d" % j, [28, 16384], BF16) for j in range(3)]
    xr_v = [dint("xr_v%d" % j, [56, 16384], BF16) for j in range(3)]
    qs_scr = dint("qs_scr", [NS, 1536], F32)
    e1s = dint("e1s", [128, 8], F32)
    e1r = dint("e1r", [256, 8], F32)
    e2s = dint("e2s", [128, 16], BF16)
    e2r = dint("e2r", [256, 16], BF16)
    e4s = dint("e4s", [128, 16], BF16)
    e4r = dint("e4r", [256, 16], BF16)
    B_qt, B_kt, B_v = Buf("qt_scr"), Buf("kt_scr"), Buf("v_scr")
    B_xsk, B_xrk, B_xsv, B_xrv = Buf("xs_k"), Buf("xr_k"), Buf("xs_v"), Buf("xr_v")
    B_e = {n: Buf(n) for n in ("e1s", "e1r", "e2s", "e2r", "e4s", "e4r")}
    B_cc = Buf("cc")
    B_ser = Buf("serial")
    B_out = Buf("outs")
    P.total_keys.add(B_out.semkey)
    PAIRS = [[0, 1], [2, 3], [4, 5], [6, 7]]

    def allgather(src, dst, bs, bd):
        P.dma("pool", lambda e: e.collective_compute("AllGather", ALU.bypass, replica_groups=PAIRS,
                                                     ins=[src.ap().opt()], outs=[dst.ap().opt()]),
              reads=[bs], writes=[bd], sem=B_cc, inc=1)

    with ExitStack() as top:
        ident_f = C.sb(top, "ident_f", [128, 128], F32)
        ident_b = C.sb(top, "ident_b", [128, 128], BF16)
        CV = C.sb(top, "CV", [128, NCV], F32)
        pmask = C.sb(top, "pmask_s", [128, 1], F32)
        zero1 = C.sb(top, "zero1", [128, 1], F32)
        XST = C.sb(top, "XST", [128, KC, 8], F32)
        RST = C.sb(top, "RST", [128, KC, NS], F32)
        H0S = C.sb(top, "H0S", [128, KC, NS], F32)
        S_H = C.sb(top, "S_H", [128, KC, NS], F32)
        CS = C.sb(top, "CS", [128, KC, NS * 3], F32)
        FS = C.sb(top, "FS", [128, 2, FC, NS * 2], F32)
        GELS = C.sb(top, "GELS", [128, FC, NS], F32)
        XCS = C.sb(top, "XCS", [128, KC, NS], F32)
        XCSB = C.sb(top, "XCSB", [128, KC, NS], BF16)
        HGS = C.sb(top, "HGS", [128, KC, NS], BF16)
        ones_f = C.sb(top, "ones_f", [128, 128], F32)
        sq_s = C.sb(top, "sq_s", [128, KC, NS], F32)
        st_s = C.sb(top, "st_s", [128, 4, NS], F32)
        tsm = C.sb(top, "tsm", [128, 8, NS], F32)
        B_XST, B_RST, B_H0S, B_SH, B_CS, B_FS, B_GELS, B_XCS, B_XCSB, B_HGS, B_ones, B_sq, B_sts = (Buf(n) for n in (
            "XST", "RST", "H0S", "S_H", "CS", "FS", "GELS", "XCS", "XCSB", "HGS", "ones_f", "sq_s", "st_s"))
        B_tsm = [Buf("tsm%d" % i) for i in range(8)]
        maskb = C.sb(top, "maskb", [128, 3, 128], BF16)
        esel = C.sb(top, "esel", [128, 2, 128], BF16)
        KB = 1024
        arena = C.sb(top, "arena", [128, 196 * KB // 4], F32)
        R0 = Region(arena, 0, 32 * KB)
        R1 = Region(arena, 32 * KB, 96 * KB)
        R2 = Region(arena, 96 * KB, 160 * KB)
        R3 = Region(arena, 160 * KB, 196 * KB)
        BIGB = R0.alloc([128, KC, NT], BF16)
        HL = C.sb(top, "HL", [128, 8], F32)
        PCL = C.sb(top, "PCL", [128, 8], F32)
        HIN = C.sb(top, "HIN", [128, 8], F32)
        PCA = C.sb(top, "PCA", [128, 8, 3], F32)
        PF = C.sb(top, "PF", [128, 2, FC, 2], F32)
        B_ident, B_CV, B_pm, B_z, B_gfin, B_maskb, B_esel = (Buf(n) for n in
            ("ident", "CV", "pmask", "zero1", "gfin", "maskb", "esel"))
        B_BIG = [Buf("BIGB%d" % i) for i in range(4)]
        B_HL, B_PCL, B_HIN, B_PCA, B_PF = Buf("HL"), Buf("PCL"), Buf("HIN"), Buf("PCA"), Buf("PF")

        psf = Ring([(top.enter_context(nc.psum_tensor("psf%d" % i, [128, 512], F32)), Buf("psf%d" % i))
                    for i in range(6)])
        psb = Ring([(top.enter_context(nc.psum_tensor("psb%d" % i, [128, 1024], BF16)), Buf("psb%d" % i))
                    for i in range(2)])

        block = top.enter_context(nc.Block())

        try:
            drip_list = []
            for g in (2, 1, 0):
                nrow = WG[g] - 1
                cuts = [(r0, 128) for r0 in range(0, nrow - 127, 128)]
                done = len(cuts) * 128
                cuts += [(done, 112), (done + 112, 15)]
                assert done + 127 == nrow
                for srct, dstt in ((ck[g], sk[g]), (cv[g], sv[g])):
                    for s in range(NS):
                        for (r0, n) in cuts:
                            drip_list.append((srct[s, 1 + r0:1 + r0 + n, :], dstt[s, r0:r0 + n, :]))
            drip_pos = [0]

            def drip(n=1):
                for _ in range(n):
                    if drip_pos[0] < len(drip_list):
                        sa, da = drip_list[drip_pos[0]]
                        drip_pos[0] += 1
                        P.dma("sp", lambda e, sa=sa, da=da: e.dma_start(out=da, in_=sa), reads=[], writes=[], sem=B_out)
            C.pool(lambda e: e.memset(ident_f[:], 0.0), w=[B_ident])
            C.pool(lambda e: e.affine_select(out=ident_f[:], in_=ident_f[:], pattern=[[-1, 128]],
                                             compare_op=ALU.not_equal, fill=1.0, base=0, channel_multiplier=1),
                   r=[B_ident], w=[B_ident])
            C.dve(lambda e: e.tensor_copy(out=ident_b[:], in_=ident_f[:]), r=[B_ident], w=[B_ident])
            C.pool(lambda e: e.memset(zero1[:], 0.0), w=[B_z])
            C.pool(lambda e: e.memset(esel[:], 0.0), w=[B_esel])
            C.pool(lambda e: e.memset(esel[:, 0, 0:64], 1.0), r=[B_esel], w=[B_esel])
            C.pool(lambda e: e.memset(esel[:, 1, 64:128], 1.0), r=[B_esel], w=[B_esel])
            C.dma(lambda e: e.dma_start(out=pmask[:], in_=pmask_d), w=[B_pm], sem=B_ser, serial=True)
            C.pool(lambda e: e.memset(ones_f[:], 1.0), w=[B_ones])
            if True:
                sc = R3
                mstage = C.sb(sc, "mstage", [128, 3, 128], F32)
                B_ms = Buf("mstage")
                C.dma(lambda e: e.dma_start(out=mstage[:], in_=cmask), w=[B_ms], sem=B_ser, serial=True)
                C.dve(lambda e: e.tensor_copy(out=maskb[:], in_=mstage[:]), r=[B_ms], w=[B_maskb])
                vst = [C.sb(sc, "vst%d" % i, [128, 128], F32) for i in range(3)]
                B_vst = [Buf("vst%d" % i) for i in range(3)]
                for i in range(3):
                    C.pool(lambda e, i=i: e.memset(vst[i][:], 0.0), w=[B_vst[i]])

                def ldv(i, row, src, nrows):
                    C.dma(lambda e: e.dma_start(out=vst[i][row:row + nrows, :], in_=src), r=[], w=[B_vst[i]], sem=B_ser, serial=True)
                ldv(0, CV_NM0, norm_mix.rearrange("l (c p) -> (l c) p", p=128), 16)
                ldv(0, CV_NF0, norm_ffn.rearrange("l (c p) -> (l c) p", p=128), 16)
                ldv(0, CV_KVN, kv_norm.rearrange("(c p) -> c p", p=128), 8)
                ldv(0, CV_CW, a_conv_w.rearrange("k (c p) -> (k c) p", p=128), 32)
                ldv(0, CV_CB, a_conv_b.rearrange("(c p) -> c p", p=128), 8)
                ldv(0, CV_BA, b_ga.rearrange("(c p) -> c p", p=128), 8)
                ldv(0, CV_BX, b_gx.rearrange("(c p) -> c p", p=128), 8)
                ldv(0, CV_LAM, a_lam.rearrange("(c p) -> c p", p=128), 8)
                ldv(0, CV_FIN, final_norm.rearrange("(c p) -> c p", p=128), 8)
                for l in range(2):
                    ldv(1 + l, 0, f_conv_w[l].rearrange("k (c p) -> (k c) p", p=128), 72)
                    ldv(1 + l, 72, f_conv_b[l].rearrange("(c p) -> c p", p=128), 24)
                for i, (c0, n) in enumerate(((0, 128), (128, 96), (224, 96))):
                    pt, bpt = psf.next()
                    C.pe(lambda e, i=i, pt=pt, n=n: e.transpose(pt[:, 0:n], vst[i][0:n, :], ident_f[0:n, 0:n]),
                         r=[B_vst[i], B_ident], w=[bpt])
                    C.dve(lambda e, pt=pt, c0=c0, n=n: e.tensor_copy(out=CV[:, c0:c0 + n], in_=pt[:, 0:n]),
                          r=[bpt], w=[B_CV])
                rin = C.ring(sc, "rin", 2, [NS * 3, D], F32)

                def rows_in(srcd, nrow, nchunk, dst_fn, bdst):
                    for c0 in range(0, nchunk, 8):
                        n = min(8, nchunk - c0)
                        stg, bstg = rin.next()
                        C.dma(lambda e, stg=stg, c0=c0, n=n: e.dma_start(out=stg[0:nrow, 0:n * 128],
                                                                        in_=srcd[:, c0 * 128:(c0 + n) * 128]), w=[bstg], sem=bstg)
                        ps, bps = psf.next()
                        for i in range(n):
                            C.pe(lambda e, ps=ps, stg=stg, i=i: e.transpose(ps[:, i * nrow:(i + 1) * nrow],
                                                                            stg[0:nrow, i * 128:(i + 1) * 128],
                                                                            ident_f[0:nrow, 0:nrow]), r=[bstg, B_ident], w=[bps])
                        C.dve(lambda e, ps=ps, c0=c0, n=n: e.tensor_copy(
                            out=dst_fn(c0, n), in_=ps[:, 0:n * nrow].rearrange("p (c r) -> p c r", r=nrow)), r=[bps], w=[bdst])
                rows_in(s_lru0, NS, KC, lambda c0, n: H0S[:, c0:c0 + n, :], B_H0S)
                rows_in(s_conva0, NS * 3, KC, lambda c0, n: CS[:, c0:c0 + n, :], B_CS)
                for l in range(2):
                    rows_in(s_ffn0[l], NS * 2, FC, lambda c0, n, l=l: FS[:, l, c0:c0 + n, :], B_FS)
                C.act(lambda e: e.activation(out=CV[:, CV_CA:CV_CA + 8], in_=CV[:, CV_LAM:CV_LAM + 8], func=AF.Sigmoid),
                      r=[B_CV], w=[B_CV])
                C.act(lambda e: e.activation(out=CV[:, CV_CA:CV_CA + 8], in_=CV[:, CV_CA:CV_CA + 8], func=AF.Ln),
                      r=[B_CV], w=[B_CV])
                C.dve(lambda e: e.tensor_scalar(out=CV[:, CV_CA:CV_CA + 8], in0=CV[:, CV_CA:CV_CA + 8], scalar1=8.0,
                                                scalar2=None, op0=ALU.mult), r=[B_CV], w=[B_CV])
                C.barrier()
            def norm_T(x_t, bx, n, dst_ap, bdst, scope_bufs):
                junk, bj, ss, bs_, xn, bxn = scope_bufs
                C.act(lambda e: e.activation(out=junk[0:n, :], in_=x_t[0:n, :], func=AF.Square, accum_out=ss[0:n, 0:1]),
                      r=[bx], w=[bj, bs_])
                C.act(lambda e: e.activation(out=ss[0:n, 1:2], in_=ss[0:n, 0:1], func=AF.Sqrt, scale=1.0 / D, bias=EPS),
                      r=[bs_], w=[bs_])
                C.dve(lambda e: e.reciprocal(out=ss[0:n, 2:3], in_=ss[0:n, 1:2]), r=[bs_], w=[bs_])
                C.dve(lambda e: e.tensor_scalar(out=xn[0:n, :], in0=x_t[0:n, :], scalar1=ss[0:n, 2:3], scalar2=None,
                                                op0=ALU.mult), r=[bx, bs_], w=[bxn])
                pt, bpt = psb.next()
                for c in range(KC):
                    C.pe(lambda e, c=c: e.transpose(pt[:, c * 128:c * 128 + n], xn[0:n, c * 128:(c + 1) * 128],
                                                    ident_b[0:n, 0:n]), r=[bxn, B_ident], w=[bpt])
                C.act(lambda e: e.copy(out=dst_ap, in_=pt[:].rearrange("p (c x) -> p c x", c=KC)[:, :, 0:n]),
                      r=[bpt], w=[bdst])

            def load_w(src3, kch, ncol, gain_col, stage, wbuf, eng="pool"):
                (sg, bsg), (wb, bwb) = stage.next(), wbuf.next()
                C.dma(lambda e: e.dma_start(out=sg[:, 0:kch, 0:ncol], in_=src3), w=[bsg], sem=bsg)
                drip(1)
                if gain_col is None:
                    P.op(eng, lambda e: e.tensor_copy(out=wb[:, 0:kch, 0:ncol], in_=sg[:, 0:kch, 0:ncol]), [bsg], [bwb])
                else:
                    P.op(eng, lambda e: e.tensor_tensor(out=wb[:, 0:kch, 0:ncol], in0=sg[:, 0:kch, 0:ncol],
                                                        in1=bc_last(CV[:, gain_col:gain_col + kch], ncol), op=ALU.mult),
                         [bsg, B_CV], [bwb])
                return wb, bwb

            def bc_mid(ap2, n):
                (s0_, n0_), (s1_, n1_) = ap2.ap
                return bass.AP(ap2.tensor, ap2.offset, [[s0_, n0_], [0, n], [s1_, n1_]])

            def sample_rstd():
                C.act(lambda e: e.activation(out=sq_s[:], in_=RST[:], func=AF.Square), r=[B_RST], w=[B_sq])
                ps, bps = psf.next()
                for c in range(KC):
                    C.pe(lambda e, ps=ps, c=c: e.matmul(ps[:, 0:NS], lhsT=ones_f[:], rhs=sq_s[:, c, :],
                                                        start=(c == 0), stop=(c == KC - 1)), r=[B_ones, B_sq], w=[bps])
                C.act(lambda e, ps=ps: e.activation(out=st_s[:, 0, :], in_=ps[:, 0:NS], func=AF.Sqrt, scale=1.0 / D, bias=EPS),
                      r=[bps], w=[B_sts])
                C.dve(lambda e: e.reciprocal(out=st_s[:, 1, :], in_=st_s[:, 0, :]), r=[B_sts], w=[B_sts])

            def sample_norm():
                sample_rstd()
                C.dve(lambda e: e.tensor_tensor(out=HNS[:, :, 0:NS], in0=RST[:], in1=bc_mid(st_s[:, 1, :], KC), op=ALU.mult),
                      r=[B_RST, B_sts], w=[B_HNS])

            def sample_proj(w_t, bw, nk, rhs_fn, brhs):
                ps, bps = psf.next()
                for fc in range(KC):
                    for k in range(nk):
                        C.pe(lambda e, ps=ps, fc=fc, k=k: e.matmul(ps[:, fc * NS:(fc + 1) * NS],
                                                                   lhsT=w_t[:, k, fc * 128:(fc + 1) * 128], rhs=rhs_fn(k),
                                                                   start=(k == 0), stop=(k == nk - 1)), r=[bw, brhs], w=[bps])
                return ps, bps

            def norm_pass():
                R3.reset()
                junk = R3.alloc([128, D], BF16); bj = Buf("junkN%d" % C.uid); C.uid += 1
                ssa = R3.alloc([128, 3, NB], F32); bssa = Buf("ssaN%d" % C.uid)
                xnr = C.ring(R3, "xnN", 3, [128, D], BF16)
                for tb in range(NB):
                    C.act(lambda e, tb=tb: e.activation(out=junk[:], in_=RES[:, tb, :], func=AF.Square,
                                                        accum_out=ssa[:, 0, tb:tb + 1]), r=[B_RES[tb]], w=[bj, bssa])
                C.act(lambda e: e.activation(out=ssa[:, 1, :], in_=ssa[:, 0, :], func=AF.Sqrt, scale=1.0 / D, bias=EPS),
                      r=[bssa], w=[bssa])
                C.dve(lambda e: e.reciprocal(out=ssa[:, 2, :], in_=ssa[:, 1, :]), r=[bssa], w=[bssa])
                for tb in range(NB):
                    xn, bxn = xnr.next()
                    C.dve(lambda e, xn=xn, tb=tb: e.tensor_scalar(out=xn[:], in0=RES[:, tb, :], scalar1=ssa[:, 2, tb:tb + 1],
                                                                  scalar2=None, op0=ALU.mult), r=[B_RES[tb], bssa], w=[bxn])
                    pt, bpt = psb.next()
                    for c in range(KC):
                        C.pe(lambda e, pt=pt, xn=xn, c=c: e.transpose(pt[:, c * 128:(c + 1) * 128], xn[:, c * 128:(c + 1) * 128],
                                                                      ident_b[:]), r=[bxn, B_ident], w=[bpt])
                    C.act(lambda e, pt=pt, tb=tb: e.copy(out=BIGB[:, :, tb * 128:(tb + 1) * 128],
                                                         in_=pt[:].rearrange("p (c x) -> p c x", c=KC)), r=[bpt], w=[B_BIG[tb // 4]])
                C.barrier()

            if True:
                pa = R1
                R1.reset(); R2.reset(); R3.reset()
                HN0 = C.sb(pa, "HN0", [128, KC, NT], BF16)
                B_HN0 = [Buf("HN0_%d" % i) for i in range(4)]
                PG = C.sb(pa, "PG", [128, KC, NT], BF16)
                B_PG = [Buf("PG%d" % i) for i in range(4)]
                HNS = C.sb(top, "HNS", [128, KC, 8], BF16)
                B_HNS = Buf("HNS")
                if True:
                    s0 = R2
                    xr_ = C.ring(s0, "xblk", 2, [128, D], F32)
                    junk = C.sb(s0, "junk", [128, D], BF16)
                    ssr = C.ring(s0, "ss", 2, [128, 4], F32)
                    xnr = C.ring(s0, "xn", 2, [128, D], BF16)
                    bj = Buf("junk")
                    for tb in range(NB):
                        (xb, bxb), (ss, bss), (xn, bxn) = xr_.next(), ssr.next(), xnr.next()
                        C.dma(lambda e, xb=xb, tb=tb: e.dma_start(out=xb[:], in_=xp[tb * 128:(tb + 1) * 128, :]),
                              w=[bxb], sem=bxb)
                        norm_T(xb, bxb, 128, HN0[:, :, tb * 128:(tb + 1) * 128], B_HN0[tb // 4],
                               (junk, bj, ss, bss, xn, bxn))
                    (xb, bxb), (ss, bss), (xn, bxn) = xr_.next(), ssr.next(), xnr.next()
                    C.dma(lambda e, xb=xb: e.dma_start(out=xb[0:8, :], in_=xsm), w=[bxb], sem=bxb)
                    norm_T(xb, bxb, 8, HNS[:, :, 0:8], B_HNS, (junk, bj, ss, bss, xn, bxn))
                    ps, bps = psf.next()
                    for c in range(KC):
                        C.pe(lambda e, ps=ps, xb=xb, c=c: e.transpose(ps[:, c * 8:(c + 1) * 8], xb[0:8, c * 128:(c + 1) * 128],
                                                                      ident_f[0:8, 0:8]), r=[bxb, B_ident], w=[bps])
                    C.dve(lambda e, ps=ps: e.tensor_copy(out=XST[:], in_=ps[:, 0:64].rearrange("p (c r) -> p c r", r=8)),
                          r=[bps], w=[B_XST])
                C.barrier()
                if True:
                    R2.reset(); R3.reset()
                    s1 = R2
                    wst = C.ring(R3, "wst", 3, [128, KC, 128], F32)
                    wbf = C.ring(R3, "wbf", 3, [128, KC, 128], BF16)
                    gst = C.ring(R3, "gst", 2, [128, 2, 256], F32)
                    gbf = C.ring(R3, "gbf", 2, [128, 2, 256], BF16)
                    USR = C.sb(R3, "USR", [128, 16, 8], F32)
                    UR = C.sb(R3, "UR", [128, 3 + NT + 5], F32)
                    B_UR = Buf("UR")
                    XC = C.sb(s1, "XC", [128, 2, NT], F32)
                    B_XC = [Buf("XC0"), Buf("XC1")]
                    XCB = C.sb(s1, "XCB", [128, 2, NT], BF16)
                    B_XCB = [Buf("XCB0"), Buf("XCB1")]
                    RAr = C.sb(s1, "RArow", [128, NT], F32); B_RAr = Buf("RArow")
                    IIr = C.sb(s1, "IIrow", [128, NT], F32); B_IIr = Buf("IIrow")
                    SSr = C.sb(s1, "SSrow", [128, NT], F32); B_SSr = Buf("SSrow")
                    hr = C.ring(s1, "hr", 3, [128, 512], F32)
                    pr = C.ring(s1, "pr", 3, [128, 512], F32)
                    gtr = C.ring(s1, "gt", 2, [128, 512], BF16)
                    B_USR = Buf("USR")
                    w_in3 = w_in.rearrange("(k p) n -> p k n", p=128)

                    def win_chunk(col0, evac):
                        wb, bwb = load_w(w_in3[:, :, col0:col0 + 128], KC, 128, CV_NM0, wst, wbf)
                        for tt in range(4):
                            ps, bps = psf.next()
                            for k in range(KC):
                                C.pe(lambda e, ps=ps, wb=wb, k=k, tt=tt: e.matmul(
                                    ps[:], lhsT=wb[:, k, :], rhs=HN0[:, k, tt * 512:(tt + 1) * 512],
                                    start=(k == 0), stop=(k == KC - 1)), r=[bwb, B_HN0[tt]], w=[bps])
                            evac(tt, ps, bps)
                        ps, bps = psf.next()
                        for k in range(KC):
                            C.pe(lambda e, ps=ps, wb=wb, k=k: e.matmul(ps[:, 0:8], lhsT=wb[:, k, :], rhs=HNS[:, k, :],
                                                                        start=(k == 0), stop=(k == KC - 1)),
                                 r=[bwb, B_HNS], w=[bps])
                        cidx = col0 // 128
                        C.act(lambda e, ps=ps, cidx=cidx: e.copy(out=USR[:, cidx, :], in_=ps[:, 0:8]), r=[bps], w=[B_USR])

                    for nb in range(4):
                        for j in range(2):
                            c = 2 * nb + j

                            def ev_rec(tt, ps, bps):
                                C.act(lambda e, ps=ps, tt=tt: e.copy(out=UR[:, 3 + tt * 512:3 + (tt + 1) * 512], in_=ps[:]),
                                      r=[bps], w=[B_UR])
                            win_chunk(D + c * 128, ev_rec)
                            C.act(lambda e, c=c: e.copy(out=UR[:, 0:3], in_=USR[:, 8 + c, 4:7]), r=[B_USR], w=[B_UR])
                            C.act(lambda e, c=c: e.copy(out=PCA[:, c, :], in_=UR[:, NT:NT + 3]), r=[B_UR], w=[B_PCA])
                            C.act(lambda e, c=c, j=j: e.activation(out=XC[:, j, :], in_=UR[:, 0:NT], func=AF.Identity,
                                                                    scale=CV[:, CV_CW + c:CV_CW + c + 1],
                                                                    bias=CV[:, CV_CB + c:CV_CB + c + 1]),
                                  r=[B_UR, B_CV], w=[B_XC[j]])
                            for k in range(1, 4):
                                C.dve(lambda e, c=c, j=j, k=k: e.scalar_tensor_tensor(
                                    out=XC[:, j, :], in0=UR[:, k:k + NT], scalar=CV[:, CV_CW + 8 * k + c:CV_CW + 8 * k + c + 1],
                                    in1=XC[:, j, :], op0=ALU.mult, op1=ALU.add), r=[B_UR, B_CV, B_XC[j]], w=[B_XC[j]])
                            C.pool(lambda e, j=j: e.tensor_copy(out=XCB[:, j, :], in_=XC[:, j, :]), r=[B_XC[j]], w=[B_XCB[j]])
                            CS4 = CS[:].rearrange("p c (s k) -> p c s k", k=3)
                            C.act(lambda e, c=c, CS4=CS4: e.activation(out=XCS[:, c, :], in_=CS4[:, c, :, 0], func=AF.Identity,
                                                                       scale=CV[:, CV_CW + c:CV_CW + c + 1],
                                                                       bias=CV[:, CV_CB + c:CV_CB + c + 1]),
                                  r=[B_CS, B_CV], w=[B_XCS])
                            for k in range(1, 4):
                                srck = CS4[:, c, :, k] if k < 3 else USR[:, 8 + c, 0:NS]
                                C.dve(lambda e, c=c, k=k, srck=srck: e.scalar_tensor_tensor(
                                    out=XCS[:, c, :], in0=srck, scalar=CV[:, CV_CW + 8 * k + c:CV_CW + 8 * k + c + 1],
                                    in1=XCS[:, c, :], op0=ALU.mult, op1=ALU.add), r=[B_CS, B_USR, B_CV, B_XCS], w=[B_XCS])
                            C.pool(lambda e, c=c: e.tensor_copy(out=XCSB[:, c, :], in_=XCS[:, c, :]), r=[B_XCS], w=[B_XCSB])
                            for k in range(3):
                                srck = CS4[:, c, :, k + 1] if k < 2 else USR[:, 8 + c, 0:NS]
                                C.dve(lambda e, c=c, k=k, srck=srck, CS4=CS4: e.tensor_copy(out=CS4[:, c, :, k], in_=srck),
                                      r=[B_CS, B_USR, B_XCS], w=[B_CS])
                        wa, bwa = load_w(w_ga[nb].rearrange("(k p) n -> p k n", p=128), 2, 256, None, gst, gbf)
                        wx, bwx = load_w(w_gx[nb].rearrange("(k p) n -> p k n", p=128), 2, 256, None, gst, gbf)
                        for j in range(2):
                            c = 2 * nb + j
                            gwb, bgwb = load_w(w_in3[:, :, c * 128:(c + 1) * 128], KC, 128, CV_NM0, wst, wbf)
                            for tt in range(4):
                                sl = slice(tt * 512, (tt + 1) * 512)
                                psr, bpsr = psf.next()
                                psi, bpsi = psf.next()
                                for k in range(2):
                                    C.pe(lambda e, psr=psr, wa=wa, k=k, j=j, sl=sl: e.matmul(
                                        psr[:], lhsT=wa[:, k, j * 128:(j + 1) * 128], rhs=XCB[:, k, sl],
                                        start=(k == 0), stop=(k == 1)), r=[bwa, B_XCB[k]], w=[bpsr])
                                for k in range(2):
                                    C.pe(lambda e, psi=psi, wx=wx, k=k, j=j, sl=sl: e.matmul(
                                        psi[:], lhsT=wx[:, k, j * 128:(j + 1) * 128], rhs=XCB[:, k, sl],
                                        start=(k == 0), stop=(k == 1)), r=[bwx, B_XCB[k]], w=[bpsi])
                                C.act(lambda e, psr=psr, c=c, sl=sl: e.activation(
                                    out=RAr[:, sl], in_=psr[:], func=AF.Sigmoid, bias=CV[:, CV_BA + c:CV_BA + c + 1]),
                                    r=[bpsr, B_CV], w=[B_RAr])
                                C.act(lambda e, psi=psi, c=c, sl=sl: e.activation(
                                    out=IIr[:, sl], in_=psi[:], func=AF.Sigmoid, bias=CV[:, CV_BX + c:CV_BX + c + 1]),
                                    r=[bpsi, B_CV], w=[B_IIr])
                            C.act(lambda e, c=c: e.activation(out=RAr[:], in_=RAr[:], func=AF.Exp, scale=CV[:, CV_CA + c:CV_CA + c + 1]),
                                  r=[B_RAr, B_CV], w=[B_RAr])
                            C.pool(lambda e: e.tensor_tensor(out=SSr[:], in0=RAr[:], in1=RAr[:], op=ALU.mult), r=[B_RAr], w=[B_SSr])
                            C.act(lambda e: e.activation(out=SSr[:], in_=SSr[:], func=AF.Sqrt, scale=-1.0, bias=1.0), r=[B_SSr], w=[B_SSr])
                            C.dve(lambda e: e.tensor_tensor(out=IIr[:], in0=IIr[:], in1=SSr[:], op=ALU.mult), r=[B_IIr, B_SSr], w=[B_IIr])
                            C.dve(lambda e, j=j: e.tensor_tensor(out=IIr[:], in0=IIr[:], in1=XC[:, j, :], op=ALU.mult),
                                  r=[B_IIr, B_XC[j]], w=[B_IIr])
                            hprev = pprev = None
                            for tt in range(4):
                                sl = slice(tt * 512, (tt + 1) * 512)
                                (h, bh), (pc, bpc) = hr.next(), pr.next()
                                hinit = 0.0 if hprev is None else hprev[0][:, 511:512]
                                pinit = 1.0 if pprev is None else pprev[0][:, 511:512]
                                C.dve(lambda e, h=h, sl=sl, hinit=hinit: e.tensor_tensor_scan(
                                    out=h[:], data0=RAr[:, sl], data1=IIr[:, sl], initial=hinit, op0=ALU.mult, op1=ALU.add),
                                    r=[B_RAr, B_IIr] + ([hprev[1]] if hprev else []), w=[bh])
                                C.dve(lambda e, pc=pc, sl=sl, pinit=pinit: e.tensor_tensor_scan(
                                    out=pc[:], data0=RAr[:, sl], data1=zero1[:, 0:1].broadcast_to([128, 512]), initial=pinit,
                                    op0=ALU.mult, op1=ALU.add), r=[B_RAr, B_z] + ([pprev[1]] if pprev else []), w=[bpc])
                                hprev, pprev = (h, bh), (pc, bpc)
                                psg, bpsg = psf.next()
                                for k in range(KC):
                                    C.pe(lambda e, psg=psg, gwb=gwb, k=k, sl=sl: e.matmul(
                                        psg[:], lhsT=gwb[:, k, :], rhs=HN0[:, k, sl], start=(k == 0), stop=(k == KC - 1)),
                                        r=[bgwb, B_HN0[tt]], w=[bpsg])
                                gt, bgt = gtr.next()
                                C.act(lambda e, gt=gt, psg=psg: e.activation(out=gt[:], in_=psg[:], func=AF.Gelu),
                                      r=[bpsg], w=[bgt])
                                C.dve(lambda e, h=h, gt=gt, c=c, sl=sl: e.tensor_tensor(
                                    out=BIGB[:, c, sl], in0=h[:], in1=gt[:], op=ALU.mult), r=[bh, bgt], w=[B_BIG[tt]])
                                C.pool(lambda e, pc=pc, gt=gt, c=c, sl=sl: e.tensor_tensor(
                                    out=PG[:, c, sl], in0=pc[:], in1=gt[:], op=ALU.mult), r=[bpc, bgt], w=[B_PG[tt]])
                            C.act(lambda e, h=hprev[0], c=c: e.copy(out=HL[:, c:c + 1], in_=h[:, 511:512]),
                                  r=[hprev[1]], w=[B_HL])
                            C.act(lambda e, pc=pprev[0], c=c: e.copy(out=PCL[:, c:c + 1], in_=pc[:, 511:512]),
                                  r=[pprev[1]], w=[B_PCL])
                            ps, bps = psf.next()
                            for k in range(KC):
                                C.pe(lambda e, ps=ps, gwb=gwb, k=k: e.matmul(ps[:, 0:8], lhsT=gwb[:, k, :], rhs=HNS[:, k, :],
                                                                              start=(k == 0), stop=(k == KC - 1)),
                                     r=[bgwb, B_HNS], w=[bps])
                            C.act(lambda e, ps=ps, c=c: e.copy(out=USR[:, c, :], in_=ps[:, 0:8]), r=[bps], w=[B_USR])
                            psr, bpsr = psf.next()
                            for gi, wgt, bwgt in ((0, wa, bwa), (1, wx, bwx)):
                                for k in range(2):
                                    C.pe(lambda e, psr=psr, wgt=wgt, k=k, j=j, gi=gi, nb=nb: e.matmul(
                                        psr[:, gi * NS:(gi + 1) * NS], lhsT=wgt[:, k, j * 128:(j + 1) * 128],
                                        rhs=XCSB[:, 2 * nb + k, :], start=(k == 0), stop=(k == 1)), r=[bwgt, B_XCSB], w=[bpsr])
                            bt0 = B_tsm[0]
                            C.act(lambda e, psr=psr, c=c: e.activation(out=tsm[:, 0, :], in_=psr[:, 0:NS], func=AF.Sigmoid,
                                                                       bias=CV[:, CV_BA + c:CV_BA + c + 1]), r=[bpsr, B_CV], w=[bt0])
                            C.act(lambda e, c=c: e.activation(out=tsm[:, 1, :], in_=tsm[:, 0, :], func=AF.Exp,
                                                              scale=CV[:, CV_CA + c:CV_CA + c + 1]), r=[bt0, B_CV], w=[bt0])
                            C.act(lambda e, psr=psr, c=c: e.activation(out=tsm[:, 2, :], in_=psr[:, NS:2 * NS], func=AF.Sigmoid,
                                                                       bias=CV[:, CV_BX + c:CV_BX + c + 1]), r=[bpsr, B_CV], w=[bt0])
                            C.act(lambda e: e.activation(out=tsm[:, 3, :], in_=tsm[:, 1, :], func=AF.Square), r=[bt0], w=[bt0])
                            C.act(lambda e: e.activation(out=tsm[:, 3, :], in_=tsm[:, 3, :], func=AF.Sqrt, scale=-1.0, bias=1.0),
                                  r=[bt0], w=[bt0])
                            C.dve(lambda e: e.tensor_tensor(out=tsm[:, 2, :], in0=tsm[:, 2, :], in1=tsm[:, 3, :], op=ALU.mult),
                                  r=[bt0], w=[bt0])
                            C.dve(lambda e, c=c: e.tensor_tensor(out=tsm[:, 2, :], in0=tsm[:, 2, :], in1=XCS[:, c, :], op=ALU.mult),
                                  r=[bt0, B_XCS], w=[bt0])
                            C.dve(lambda e, c=c: e.tensor_tensor(out=tsm[:, 1, :], in0=tsm[:, 1, :], in1=H0S[:, c, :], op=ALU.mult),
                                  r=[bt0, B_H0S], w=[bt0])
                            C.dve(lambda e, c=c: e.tensor_tensor(out=S_H[:, c, :], in0=tsm[:, 1, :], in1=tsm[:, 2, :], op=ALU.add),
                                  r=[bt0], w=[B_SH])
                            C.act(lambda e, c=c: e.activation(out=tsm[:, 4, :], in_=USR[:, c, 0:NS], func=AF.Gelu),
                                  r=[B_USR], w=[bt0])
                            C.dve(lambda e, c=c: e.tensor_tensor(out=HGS[:, c, :], in0=S_H[:, c, :], in1=tsm[:, 4, :], op=ALU.mult),
                                  r=[bt0, B_SH], w=[B_HGS])
                C.barrier()
                dump("hl", HL[:], [B_HL], [128, 8]); dump("pcl", PCL[:], [B_PCL], [128, 8])
                dump("pca", PCA[:].rearrange("p c k -> p (c k)"), [B_PCA], [128, 24])
                dump("xcs", XCS[:].rearrange("p c s -> p (c s)"), [B_XCS], [128, 32])
                dump("sh", S_H[:].rearrange("p c s -> p (c s)"), [B_SH], [128, 32])
                dump("h0s", H0S[:].rearrange("p c s -> p (c s)"), [B_H0S], [128, 32])
                chk(1)
                R2.reset(); R3.reset()
                RH = Region(arena, 32 * KB, 64 * KB)
                RES = R2.alloc([128, NB, D], F32)
                B_RES = [Buf("RES%d" % tb) for tb in range(NB)]
                HF = C.sb(top, "HF", [128, 8], F32)
                B_HF = Buf("HF")
                C.dma(lambda e: e.dma_start(out=e1s.ap(), in_=HL[:]), r=[B_HL], w=[B_e["e1s"]], sem=B_ser, serial=True)
                allgather(e1s, e1r, B_e["e1s"], B_e["e1r"])
                C.dma(lambda e: e.dma_start(out=HIN[:], in_=e1r.ap()[0:128, :]), r=[B_e["e1r"]], w=[B_HIN], sem=B_ser, serial=True)
                C.dve(lambda e: e.tensor_scalar(out=HIN[:], in0=HIN[:], scalar1=pmask[:, 0:1], scalar2=None, op0=ALU.mult),
                      r=[B_HIN, B_pm], w=[B_HIN])
                C.dve(lambda e: e.tensor_tensor(out=HF[:], in0=PCL[:], in1=HIN[:], op=ALU.mult), r=[B_PCL, B_HIN], w=[B_HF])
                C.dve(lambda e: e.tensor_tensor(out=HF[:], in0=HF[:], in1=HL[:], op=ALU.add), r=[B_HF, B_HL], w=[B_HF])
                dump("hf", HF[:], [B_HF], [128, 8])
                chk(2)
                wo_sb = RH.alloc([128, KC, D], BF16)
                B_wo = Buf("wo_sb")
                wost = C.ring(RH, "wost", 2, [128, D], F32)
                junk = RH.alloc([128, D], BF16)
                bj = Buf("junkB")
                for k in range(KC):
                    sg, bsg = wost.next()
                    C.dma(lambda e, sg=sg, k=k: e.dma_start(out=sg[:], in_=w_out[k * 128:(k + 1) * 128, :]), w=[bsg], sem=bsg)
                    C.pool(lambda e, sg=sg, k=k: e.tensor_copy(out=wo_sb[:, k, :], in_=sg[:]), r=[bsg], w=[B_wo])
                ps, bps = sample_proj(wo_sb, B_wo, KC, lambda k: HGS[:, k, :], B_HGS)
                C.dve(lambda e, ps=ps: e.tensor_tensor(out=RST[:], in0=XST[:, :, 0:NS],
                                                       in1=ps[:, 0:KC * NS].rearrange("p (c s) -> p c s", s=NS), op=ALU.add),
                      r=[bps, B_XST], w=[B_RST])
                sample_norm()
                hgr = C.ring(R3, "hg", 2, [128, KC, 512], BF16)
                xnr = C.ring(R3, "xnB", 2, [128, D], BF16)
                ssr = C.ring(R3, "ssB", 2, [128, 4], F32)
                for tt in range(4):
                    sl = slice(tt * 512, (tt + 1) * 512)
                    hg, bhg = hgr.next()
                    for c in range(KC):
                        C.dve(lambda e, hg=hg, c=c, sl=sl: e.scalar_tensor_tensor(
                            out=hg[:, c, :], in0=PG[:, c, sl], scalar=HIN[:, c:c + 1], in1=BIGB[:, c, sl],
                            op0=ALU.mult, op1=ALU.add), r=[B_PG[tt], B_HIN, B_BIG[tt]], w=[bhg])
                    for tbl in range(4):
                        tb = tt * 4 + tbl
                        C.dma(lambda e, tb=tb: e.dma_start(out=RES[:, tb, :], in_=xp[tb * 128:(tb + 1) * 128, :]),
                              w=[B_RES[tb]], sem=B_ser, serial=True)
                        for half in range(2):
                            ps, bps = psf.next()
                            for c in range(KC):
                                C.pe(lambda e, ps=ps, hg=hg, c=c, tbl=tbl, half=half: e.matmul(
                                    ps[:], lhsT=hg[:, c, tbl * 128:(tbl + 1) * 128], rhs=wo_sb[:, c, half * 512:(half + 1) * 512],
                                    start=(c == 0), stop=(c == KC - 1)), r=[bhg, B_wo], w=[bps])
                            C.dve(lambda e, ps=ps, tb=tb, half=half: e.tensor_tensor(
                                out=RES[:, tb, half * 512:(half + 1) * 512], in0=RES[:, tb, half * 512:(half + 1) * 512],
                                in1=ps[:], op=ALU.add), r=[bps, B_RES[tb]], w=[B_RES[tb]])
                C.barrier()
                norm_pass()
                dump("res1", RES, B_RES, [128, NB, D])
                chk(3)
                HALO = [C.sb(top, "HALO%d" % l, [128, KC, 2], BF16) for l in range(2)]
                B_HALO = [Buf("HALO0"), Buf("HALO1")]

                def halo_exchange(l, es, er, nes, ner):
                    C.dma(lambda e: e.dma_start(out=es.ap().rearrange("p (c x) -> p c x", c=KC), in_=BIGB[:, :, NT - 2:NT]),
                          r=[B_BIG[3]], w=[B_e[nes]], sem=B_ser, serial=True)
                    allgather(es, er, B_e[nes], B_e[ner])
                    C.dma(lambda e: e.dma_start(out=HALO[l][:], in_=er.ap()[0:128, :].rearrange("p (c x) -> p c x", c=KC)),
                          r=[B_e[ner]], w=[B_HALO[l]], sem=B_ser, serial=True)
                    C.dve(lambda e: e.tensor_scalar(out=HALO[l][:], in0=HALO[l][:], scalar1=pmask[:, 0:1], scalar2=None,
                                                    op0=ALU.mult), r=[B_HALO[l], B_pm], w=[B_HALO[l]])
                halo_exchange(0, e2s, e2r, "e2s", "e2r")
                C.barrier()
                dump("halo0", HALO[0][:].rearrange("p c x -> p (c x)"), [B_HALO[0]], [128, 16])
                chk(4)

            RA = Region(arena, 32 * KB, 96 * KB)
            acts_r = C.ring(top, "actsS", 2, [128, 4, NS], BF16)

            import os
            FFN_NG = int(os.environ.get("FFN_NG", FC // 4))
            FFN_HALO = int(os.environ.get("FFN_HALO", 1))
            FFN_DOWN = int(os.environ.get("FFN_DOWN", 1))

            def ffn(l):
                RA.reset(); R3.reset()
                actr = C.ring(RA, "act", 2, [128, 4, NT], BF16)
                FS4 = FS[:].rearrange("p l c (s k) -> p l c s k", k=2)
                wdnr = C.ring(RA, "wdn", 2, [128, 4, D], BF16)
                acc = RA.alloc([128, NT], F32); b_acc = Buf("acc%d" % l)
                gel = RA.alloc([128, NT], BF16); b_gel = Buf("gel%d" % l)
                wdst = C.ring(RA, "wdst", 1, [128, 1, D], F32)
                gsbr = C.ring(R3, "gsb", 2, [128, 2 + NT + 6], F32)
                wst = C.ring(R3, "wstF", 3, [128, KC, 128], F32)
                wbf = C.ring(R3, "wbfF", 3, [128, KC, 128], BF16)
                wup3 = w_up[l].rearrange("(k p) n -> p k n", p=128)
                gcol = CV_NF0 + 8 * l
                cvf = CV_F[l]
                pend_v = None

                def do_v(ci, act_t, bact, acts, bacts):
                    wv, bwv = load_w(wup3[:, :, DFF + ci * 128:DFF + (ci + 1) * 128], KC, 128, gcol, wst, wbf)
                    pss, bpss = psf.next()
                    for k in range(KC):
                        C.pe(lambda e, pss=pss, wv=wv, k=k: e.matmul(pss[:, 0:NS], lhsT=wv[:, k, :], rhs=HNS[:, k, 0:NS],
                                                                      start=(k == 0), stop=(k == KC - 1)), r=[bwv, B_HNS], w=[bpss])
                    C.dve(lambda e, pss=pss, ci=ci, acts=acts: e.tensor_tensor(out=acts[:, ci % 4, :], in0=GELS[:, ci, :],
                                                                             in1=pss[:, 0:NS], op=ALU.mult),
                          r=[bpss, B_GELS], w=[bacts])
                    for tt in range(4):
                        sl = slice(tt * 512, (tt + 1) * 512)
                        ps, bps = psf.next()
                        for k in range(KC):
                            C.pe(lambda e, ps=ps, wv=wv, k=k, sl=sl: e.matmul(ps[:], lhsT=wv[:, k, :], rhs=BIGB[:, k, sl],
                                                                            start=(k == 0), stop=(k == KC - 1)),
                                 r=[bwv, B_BIG[tt]], w=[bps])
                        C.dve(lambda e, ps=ps, ci=ci, sl=sl, act_t=act_t: e.tensor_tensor(
                            out=act_t[:, ci % 4, sl], in0=gel[:, sl], in1=ps[:], op=ALU.mult), r=[bps, b_gel], w=[bact])

                def do_down(G, act_t, bact, wdn, bwdn, acts, bacts):
                    ps, bps = sample_proj(wdn, bwdn, 4, lambda k: acts[:, k, :], bacts)
                    C.dve(lambda e, ps=ps: e.tensor_tensor(out=RST[:], in0=RST[:],
                                                           in1=ps[:, 0:KC * NS].rearrange("p (c s) -> p c s", s=NS), op=ALU.add),
                          r=[bps, B_RST], w=[B_RST])
                    for tb in range(NB if FFN_DOWN else 0):
                        for half in range(2):
                            ps, bps = psf.next()
                            for ci in range(4):
                                C.pe(lambda e, ps=ps, ci=ci, tb=tb, half=half, act_t=act_t, wdn=wdn: e.matmul(
                                    ps[:], lhsT=act_t[:, ci, tb * 128:(tb + 1) * 128], rhs=wdn[:, ci, half * 512:(half + 1) * 512],
                                    start=(ci == 0), stop=(ci == 3)), r=[bact, bwdn], w=[bps])
                            C.dve(lambda e, ps=ps, tb=tb, half=half: e.tensor_tensor(
                                out=RES[:, tb, half * 512:(half + 1) * 512], in0=RES[:, tb, half * 512:(half + 1) * 512],
                                in1=ps[:], op=ALU.add), r=[bps, B_RES[tb]], w=[B_RES[tb]])

                pend_down = None
                for G in range(FFN_NG):
                    (act_t, bact), (wdn, bwdn), (acts, bacts) = actr.next(), wdnr.next(), acts_r.next()
                    for cl in range(4):
                        ci = G * 4 + cl
                        wg, bwg = load_w(wup3[:, :, ci * 128:(ci + 1) * 128], KC, 128, gcol, wst, wbf)
                        gsb, bgsb = gsbr.next()
                        for tt in range(4):
                            sl = slice(tt * 512, (tt + 1) * 512)
                            ps, bps = psf.next()
                            for k in range(KC):
                                C.pe(lambda e, ps=ps, wg=wg, k=k, sl=sl: e.matmul(ps[:], lhsT=wg[:, k, :], rhs=BIGB[:, k, sl],
                                                                                start=(k == 0), stop=(k == KC - 1)),
                                     r=[bwg, B_BIG[tt]], w=[bps])
                            C.act(lambda e, ps=ps, gsb=gsb, tt=tt: e.copy(out=gsb[:, 2 + tt * 512:2 + (tt + 1) * 512], in_=ps[:]),
                                  r=[bps], w=[bgsb])
                        ps, bps = psf.next()
                        for k in range(KC if FFN_HALO else 0):
                            C.pe(lambda e, ps=ps, wg=wg, k=k: e.matmul(ps[:, 0:2], lhsT=wg[:, k, :], rhs=HALO[l][:, k, :],
                                                                        start=(k == 0), stop=(k == KC - 1)),
                                 r=[bwg, B_HALO[l]], w=[bps])
                        C.act(lambda e, ps=ps, gsb=gsb: e.copy(out=gsb[:, 0:2], in_=ps[:, 0:2]), r=[bps], w=[bgsb])
                        C.act(lambda e, gsb=gsb, ci=ci: e.copy(out=PF[:, l, ci, :], in_=gsb[:, NT:NT + 2]), r=[bgsb], w=[B_PF])
                        pss, bpss = psf.next()
                        for k in range(KC):
                            C.pe(lambda e, pss=pss, wg=wg, k=k: e.matmul(pss[:, 0:NS], lhsT=wg[:, k, :], rhs=HNS[:, k, 0:NS],
                                                                          start=(k == 0), stop=(k == KC - 1)), r=[bwg, B_HNS], w=[bpss])
                        bt1 = B_tsm[1]
                        C.act(lambda e, pss=pss: e.copy(out=tsm[:, 6, :], in_=pss[:, 0:NS]), r=[bpss], w=[bt1])
                        C.act(lambda e, ci=ci: e.activation(out=tsm[:, 5, :], in_=FS4[:, l, ci, :, 0], func=AF.Identity,
                                                            scale=CV[:, cvf + ci:cvf + ci + 1],
                                                            bias=CV[:, cvf + 72 + ci:cvf + 72 + ci + 1]), r=[B_FS, B_CV, bt1], w=[bt1])
                        C.dve(lambda e, ci=ci: e.scalar_tensor_tensor(out=tsm[:, 5, :], in0=FS4[:, l, ci, :, 1],
                                                                      scalar=CV[:, cvf + 24 + ci:cvf + 24 + ci + 1],
                                                                      in1=tsm[:, 5, :], op0=ALU.mult, op1=ALU.add),
                              r=[B_FS, B_CV, bt1], w=[bt1])
                        C.dve(lambda e, ci=ci: e.scalar_tensor_tensor(out=tsm[:, 5, :], in0=tsm[:, 6, :],
                                                                      scalar=CV[:, cvf + 48 + ci:cvf + 48 + ci + 1],
                                                                      in1=tsm[:, 5, :], op0=ALU.mult, op1=ALU.add),
                              r=[B_CV, bt1], w=[bt1])
                        C.act(lambda e, ci=ci: e.activation(out=GELS[:, ci, :], in_=tsm[:, 5, :], func=AF.Gelu), r=[bt1], w=[B_GELS])
                        C.dve(lambda e, ci=ci: e.tensor_copy(out=FS4[:, l, ci, :, 0], in_=FS4[:, l, ci, :, 1]), r=[B_FS, bt1], w=[B_FS])
                        C.dve(lambda e, ci=ci: e.tensor_copy(out=FS4[:, l, ci, :, 1], in_=tsm[:, 6, :]), r=[B_FS, bt1], w=[B_FS])
                        if pend_v is not None:
                            do_v(*pend_v)
                        C.act(lambda e, gsb=gsb, ci=ci: e.activation(
                            out=acc[:], in_=gsb[:, 0:NT], func=AF.Identity, scale=CV[:, cvf + ci:cvf + ci + 1],
                            bias=CV[:, cvf + 72 + ci:cvf + 72 + ci + 1]), r=[bgsb, B_CV, b_gel], w=[b_acc])
                        for k in (1, 2):
                            C.dve(lambda e, gsb=gsb, ci=ci, k=k: e.scalar_tensor_tensor(
                                out=acc[:], in0=gsb[:, k:k + NT], scalar=CV[:, cvf + 24 * k + ci:cvf + 24 * k + ci + 1],
                                in1=acc[:], op0=ALU.mult, op1=ALU.add), r=[bgsb, B_CV, b_acc], w=[b_acc])
                        C.act(lambda e: e.activation(out=gel[:], in_=acc[:], func=AF.Gelu), r=[b_acc], w=[b_gel])
                        pend_v = (ci, act_t, bact, acts, bacts)
                        sg, bsg = wdst.next()
                        C.dma(lambda e, sg=sg, ci=ci: e.dma_start(out=sg[:, 0, :], in_=w_down[l][ci * 128:(ci + 1) * 128, :]),
                              w=[bsg], sem=bsg)
                        drip(1)
                        C.pool(lambda e, sg=sg, wdn=wdn, cl=cl: e.tensor_copy(out=wdn[:, cl, :], in_=sg[:, 0, :]),
                               r=[bsg], w=[bwdn])
                    if pend_down is not None:
                        do_down(*pend_down)
                    pend_down = None
                    do_v(*pend_v)
                    pend_v = None
                    pend_down = (G, act_t, bact, wdn, bwdn, acts, bacts)
                do_down(*pend_down)
                C.barrier()

            ffn(0)
            dump("pf", PF[:].rearrange("p l c k -> p (l c k)"), [B_PF], [128, 2 * FC * 2])
            dump("res2", RES, B_RES, [128, NB, D])
            chk(5)
            def sls(start, stride, n=128):
                return slice(start, start + (n - 1) * stride + 1, stride)

            norm_pass()
            RA.reset(); R3.reset()
            WKV = RA.alloc([128, KC, 3072], BF16)
            B_WKVs = [Buf("WKV%d" % i) for i in range(6)]
            B_WKV = None
            wst = C.ring(R3, "wstD", 3, [128, KC, 128], F32)
            wbf = C.ring(R3, "wbfD", 3, [128, KC, 128], BF16)
            rowr = C.ring(R3, "rowD", 2, [128, NT], BF16)
            vbr = C.ring(R3, "vbD", 2, [128, 512], BF16)
            vfr = C.ring(R3, "vfD", 2, [128, 512], F32)
            w_kv3 = w_kv.rearrange("(k p) n -> p k n", p=128)
            w_q3 = w_q.rearrange("(k p) n -> p k n", p=128)
            sample_norm()
            sstg = C.ring(R3, "sstg", 2, [NS, 512], F32)

            def sample_rows(rhs_fn, brhs, ncol, dst):
                ps, bps = psf.next()
                for k in range(KC):
                    C.pe(lambda e, ps=ps, k=k: e.matmul(ps[0:NS, 0:ncol], lhsT=HNS[:, k, 0:NS], rhs=rhs_fn(k),
                                                        start=(k == 0), stop=(k == KC - 1)), r=[B_HNS, brhs], w=[bps])
                stg, bstg = sstg.next()
                C.act(lambda e, ps=ps, stg=stg: e.copy(out=stg[0:NS, 0:ncol], in_=ps[0:NS, 0:ncol]), r=[bps], w=[bstg])
                C.dma(lambda e, stg=stg: e.dma_start(out=dst, in_=stg[0:NS, 0:ncol]), r=[bstg], sem=bstg)
            for s in range(24):
                sg, bsg = wst.next()
                C.dma(lambda e, sg=sg, s=s: e.dma_start(out=sg[:], in_=w_kv3[:, :, s * 128:(s + 1) * 128]), w=[bsg], sem=bsg)
                drip(1)
                C.pool(lambda e, sg=sg, s=s: e.tensor_tensor(out=WKV[:, :, s * 128:(s + 1) * 128], in0=sg[:],
                                                             in1=bc_last(CV[:, CV_KVN:CV_KVN + KC], 128), op=ALU.mult),
                       r=[bsg, B_CV], w=[B_WKVs[s // 4]])
            def kview(t):
                return t.ap()[0:28].rearrange("(e c) (p x) -> e c p x", c=4, x=128)

            def vview(t):
                return t.ap()[0:28].rearrange("(e q) (k x) -> e (q k) x", q=4, x=512)
            xs_k4 = [kview(t) for t in xs_k]
            xs_v3 = [vview(t) for t in xs_v]

            def featmajor(g, cc, lhs_fn, bw, scr, is_k):
                row, brow = rowr.next()
                for tt in range(4):
                    ps, bps = psf.next()
                    for k in range(KC):
                        C.pe(lambda e, ps=ps, k=k, tt=tt: e.matmul(ps[:], lhsT=lhs_fn(k), rhs=BIGB[:, k, tt * 512:(tt + 1) * 512],
                                                                   start=(k == 0), stop=(k == KC - 1)), r=[bw, B_BIG[tt]], w=[bps])
                    if g == 0:
                        o_ap, i_ap = row[:, tt * 512:(tt + 1) * 512], ps[:]
                    elif g == 1:
                        o_ap = row[:].rearrange("p (r x) -> p r x", r=4)[:, :, tt * 128:(tt + 1) * 128]
                        i_ap = ps[:].rearrange("p (q r) -> p r q", r=4)
                    else:
                        o_ap = row[:].rearrange("p (r x) -> p r x", r=16)[:, :, tt * 32:(tt + 1) * 32]
                        i_ap = ps[:].rearrange("p (q r) -> p r q", r=16)
                    if tt % 2 == 0:
                        C.act(lambda e, o_ap=o_ap, i_ap=i_ap: e.copy(out=o_ap, in_=i_ap), r=[bps], w=[brow])
                    else:
                        C.dve(lambda e, o_ap=o_ap, i_ap=i_ap: e.tensor_copy(out=o_ap, in_=i_ap), r=[bps], w=[brow])
                C.dma(lambda e, row=row: e.dma_start(out=scr.ap()[g, cc], in_=row[:]), r=[brow], sem=brow)
                if is_k:
                    if g == 2:
                        for j, (e0, e1) in enumerate(((0, 7), (7, 14), (14, 16))):
                            C.dma(lambda e, row=row, j=j, e0=e0, e1=e1: e.dma_start(
                                out=xs_k4[j][0:e1 - e0, cc].rearrange("e p x -> p e x"),
                                in_=row[:].rearrange("p (e x) -> p e x", e=16)[:, e0:e1, :]), r=[brow], sem=brow)
                    elif g == 1:
                        C.dma(lambda e, row=row: e.dma_start(
                            out=xs_k4[2][2:6, cc].rearrange("e p x -> p e x"),
                            in_=row[:].rearrange("p (r n x) -> p r n x", r=4, n=4)[:, :, 3, :]), r=[brow], sem=brow)
                    else:
                        C.dma(lambda e, row=row: e.dma_start(out=xs_k4[2][6, cc], in_=row[:, 1920:2048]), r=[brow], sem=brow)

            D_K = int(os.environ.get("D_K", 3)); D_Q = int(os.environ.get("D_Q", 3)); D_TOK = int(os.environ.get("D_TOK", 3))
            D_CC = int(os.environ.get("D_CC", 1))
            for g in range(D_K):
                for cc in range(4):
                    col0 = g * 512 + cc * 128
                    featmajor(g, cc, lambda k, col0=col0: WKV[:, k, col0:col0 + 128], B_WKVs[g], kt_scr, True)
            for s6 in range(6):
                g_ = s6 % 3
                dst = (sk if s6 < 3 else sv)[g_][:, WG[g_] - 1, :]
                sample_rows(lambda k, s6=s6: WKV[:, k, s6 * 512:(s6 + 1) * 512], B_WKVs[s6], 512, dst)

            for g in range(D_TOK):
                for u in range(16):
                    t0_, stp = unit_tok(g, u)
                    slot = unit_slot(g, u)
                    if not int(os.environ.get("TOK_SLOT", 1)):
                        slot = None
                    if g == 0:
                        orow = (lambda o: o[0:128, :])
                    elif g == 1:
                        orow = (lambda o, u=u: o.rearrange("(i r) d -> r i d", r=4)[u // 4])
                    else:
                        orow = (lambda o, u=u: o.rearrange("(i r) d -> r i d", r=16)[u])
                    for which in ("v", "k"):
                        if which == "k" and slot is None:
                            continue
                        cbase = 1536 + g * 512 if which == "v" else g * 512
                        ps, bps = psf.next()
                        for k in range(KC):
                            C.pe(lambda e, ps=ps, k=k, t0_=t0_, stp=stp, cbase=cbase: e.matmul(
                                ps[:], lhsT=BIGB[:, k, sls(t0_, stp)], rhs=WKV[:, k, cbase:cbase + 512],
                                start=(k == 0), stop=(k == KC - 1)), r=B_BIG + [B_WKVs[cbase // 512]], w=[bps])
                        vf = bvf = None
                        if slot is not None:
                            vf, bvf = vfr.next()
                            C.dve(lambda e, vf=vf, ps=ps: e.tensor_copy(out=vf[:], in_=ps[:]), r=[bps], w=[bvf])
                            dst = orow((o_v if which == "v" else o_k)[g])
                            C.dma(lambda e, vf=vf, dst=dst: e.dma_start(out=dst, in_=vf[:]), r=[bvf], sem=bvf)
                        if which == "v":
                            vb, bvb = vbr.next()
                            if vf is None:
                                C.act(lambda e, vb=vb, ps=ps: e.copy(out=vb[:], in_=ps[:]), r=[bps], w=[bvb])
                            else:
                                C.act(lambda e, vb=vb, vf=vf: e.copy(out=vb[:], in_=vf[:]), r=[bvf], w=[bvb])
                            C.dma(lambda e, vb=vb, g=g, u=u: e.dma_start(out=v_scr.ap()[g, u], in_=vb[:]), r=[bvb], sem=bvb)
                            if slot is not None:
                                C.dma(lambda e, vb=vb, slot=slot: e.dma_start(out=xs_v3[slot // 7][slot % 7], in_=vb[:]), r=[bvb], sem=bvb)
            C.barrier()
            if D_CC:
                for j in range(3):
                    allgather(xs_k[j], xr_k[j], B_xsk, B_xrk)
                    allgather(xs_v[j], xr_v[j], B_xsv, B_xrv)
            for g in range(D_Q):
                for cc in range(4):
                    col0 = g * 512 + cc * 128
                    wq, bwq = load_w(w_q3[:, :, col0:col0 + 128], KC, 128, CV_NM1, wst, wbf, eng="dve")
                    featmajor(g, cc, lambda k, wq=wq: wq[:, k, :], bwq, qt_scr, False)
                    sample_rows(lambda k, wq=wq: wq[:, k, :], bwq, 128, qs_scr.ap()[:, col0:col0 + 128])
            C.barrier()
            chk(6)

            R0.reset(); RA.reset(); R3.reset()
            ACC = RA.alloc([128, 4, 2, NT], F32)
            B_ACC = [Buf("ACC%d" % cc) for cc in range(4)]
            OT = R3.alloc([128, 4, NT], BF16)
            B_OT = [Buf("OT%d" % i) for i in range(4)]
            WO = R3.alloc([128, 4, D], BF16)
            B_WO = Buf("WO")
            wost = C.ring(R3, "wostE", 2, [128, D], F32)
            for cc in range(4):
                sg, bsg = wost.next()
                C.dma(lambda e, sg=sg, cc=cc: e.dma_start(out=sg[:], in_=w_o[cc * 128:(cc + 1) * 128, :]), w=[bsg], sem=bsg)
                C.pool(lambda e, sg=sg, cc=cc: e.tensor_copy(out=WO[:, cc, :], in_=sg[:]), r=[bsg], w=[B_WO])
            NU = 3
            qzr = C.ring(R0, "QZ", NU, [128, 4, 2, 128], BF16)
            ktr = C.ring(R0, "KT", NU, [128, 4, 2, 128], BF16)
            vrr = C.ring(R0, "VR", NU, [128, 2, 4, 2, 64], BF16)
            vzr = C.ring(R0, "VZ", NU, [128, 2, 4, 2, 128], BF16)
            ptr_ = C.ring(R0, "PT", 2, [128, 512], BF16)
            for (t, bt) in qzr.items + vzr.items:
                C.pool(lambda e, t=t: e.memset(t[:], 0.0), w=[bt])
            xr_k4 = [kview(t) for t in xr_k]
            xr_v3 = [vview(t) for t in xr_v]
            for g in range(3):
                for u in range(16):
                    t0_, stp = unit_tok(g, u)
                    pk, pi = unit_prev(g, u)
                    (QZ, bqz), (KT, bkt), (VR, bvr), (VZ, bvz) = qzr.next(), ktr.next(), vrr.next(), vzr.next()
                    cols = slice(u * 128, (u + 1) * 128)
                    for par in range(2):
                        C.dma(lambda e, QZ=QZ, par=par, cols=cols, g=g: e.dma_start(
                            out=QZ[par * 64:(par + 1) * 64, :, par, :],
                            in_=qt_scr.ap()[g, :, par * 64:(par + 1) * 64, cols].rearrange("c p x -> p c x")),
                            w=[bqz], sem=bqz)
                    C.dma(lambda e, KT=KT, cols=cols, g=g: e.dma_start(
                        out=KT[:, :, 1, :], in_=kt_scr.ap()[g, :, :, cols].rearrange("c p x -> p c x")), w=[bkt], sem=bkt)
                    C.dma(lambda e, VR=VR, g=g, u=u: e.dma_start(out=VR[:, 1].rearrange("p c t d -> p (c t d)"),
                                                                 in_=v_scr.ap()[g, u]), w=[bvr], sem=bvr)
                    if pk == "own":
                        pcols = slice(pi * 128, (pi + 1) * 128)
                        ksrc = kt_scr.ap()[g, :, :, pcols].rearrange("c p x -> p c x")
                        vsrc = v_scr.ap()[g, pi]
                        mprev = 0
                    else:
                        ksrc = xr_k4[pi // 7][pi % 7].rearrange("c p x -> p c x")
                        vsrc = xr_v3[pi // 7][pi % 7]
                        mprev = 1
                    C.dma(lambda e, KT=KT, ksrc=ksrc: e.dma_start(out=KT[:, :, 0, :], in_=ksrc), w=[bkt], sem=bkt)
                    C.dma(lambda e, VR=VR, vsrc=vsrc: e.dma_start(out=VR[:, 0].rearrange("p c t d -> p (c t d)"), in_=vsrc),
                          w=[bvr], sem=bvr)
                    for par in range(2):
                        C.pool(lambda e, VZ=VZ, VR=VR, par=par: e.tensor_copy(
                            out=VZ[:, :, :, par, par * 64:(par + 1) * 64], in_=VR[:, :, :, par, :]), r=[bvr], w=[bvz])
                    for cc in range(4):
                        pss, bpss = psf.next()
                        for blk, mi in ((0, mprev), (1, 2)):
                            C.pe(lambda e, pss=pss, KT=KT, QZ=QZ, cc=cc, blk=blk: e.matmul(
                                pss[:, blk * 256:(blk + 1) * 256], lhsT=KT[:, cc, blk, :],
                                rhs=QZ[:, cc, :, :].rearrange("p a q -> p (a q)"), start=True, stop=False), r=[bkt, bqz], w=[bpss])
                            C.pe(lambda e, pss=pss, blk=blk, mi=mi: e.matmul(
                                pss[:, blk * 256:(blk + 1) * 256], lhsT=ident_b[:], rhs=bc_mid(maskb[:, mi, :], 2),
                                start=False, stop=True), r=[B_ident, B_maskb], w=[bpss])
                        PT, bpt = ptr_.next()
                        C.act(lambda e, PT=PT, pss=pss: e.activation(out=PT[:], in_=pss[:], func=AF.Exp, scale=0.125),
                              r=[bpss], w=[bpt])
                        pso, bpso = psf.next()
                        idx = 0
                        for par in range(2):
                            for blk in range(2):
                                C.pe(lambda e, pso=pso, VZ=VZ, cc=cc, par=par, blk=blk, PT=PT, idx=idx: e.matmul(
                                    pso[:, 0:128], lhsT=VZ[:, blk, cc, par, :], rhs=PT[:, (blk * 2 + par) * 128:(blk * 2 + par + 1) * 128],
                                    start=(idx == 0), stop=(idx == 3)), r=[bvz, bpt], w=[bpso])
                                idx += 1
                        idx = 0
                        for par in range(2):
                            for blk in range(2):
                                C.pe(lambda e, pso=pso, par=par, blk=blk, PT=PT, idx=idx: e.matmul(
                                    pso[:, 128:256], lhsT=esel[:, par, :], rhs=PT[:, (blk * 2 + par) * 128:(blk * 2 + par + 1) * 128],
                                    start=(idx == 0), stop=(idx == 3)), r=[B_esel, bpt], w=[bpso])
                                idx += 1
                        dsta = ACC[:, cc, :, sls(t0_, stp)]
                        srca = pso[:, 0:256].rearrange("p (a x) -> p a x", a=2)
                        if g == 0:
                            C.dve(lambda e, dsta=dsta, srca=srca: e.tensor_copy(out=dsta, in_=srca), r=[bpso], w=[B_ACC[cc]])
                        else:
                            C.dve(lambda e, dsta=dsta, srca=srca: e.tensor_tensor(out=dsta, in0=dsta, in1=srca, op=ALU.add),
                                  r=[bpso, B_ACC[cc]], w=[B_ACC[cc]])
            tmpr = C.ring(R3, "tE", 2, [128, 512], F32)
            for tt in range(4):
                sl = slice(tt * 512, (tt + 1) * 512)
                for cc in range(4):
                    tm, btm = tmpr.next()
                    C.act(lambda e, tm=tm, cc=cc, sl=sl: e.activation(out=tm[:], in_=ACC[:, cc, 1, sl], func=AF.Ln),
                          r=[B_ACC[cc]], w=[btm])
                    C.act(lambda e, tm=tm: e.activation(out=tm[:], in_=tm[:], func=AF.Exp, scale=-1.0), r=[btm], w=[btm])
                    C.dve(lambda e, tm=tm, cc=cc, sl=sl: e.tensor_tensor(out=OT[:, cc, sl], in0=ACC[:, cc, 0, sl], in1=tm[:],
                                                                        op=ALU.mult), r=[btm, B_ACC[cc]], w=[B_OT[tt]])
                for tbl in range(4):
                    tb = tt * 4 + tbl
                    for half in range(2):
                        ps, bps = psf.next()
                        for cc in range(4):
                            C.pe(lambda e, ps=ps, cc=cc, tb=tb, half=half: e.matmul(
                                ps[:], lhsT=OT[:, cc, tb * 128:(tb + 1) * 128], rhs=WO[:, cc, half * 512:(half + 1) * 512],
                                start=(cc == 0), stop=(cc == 3)), r=[B_OT[tt], B_WO], w=[bps])
                        C.dve(lambda e, ps=ps, tb=tb, half=half: e.tensor_tensor(
                            out=RES[:, tb, half * 512:(half + 1) * 512], in0=RES[:, tb, half * 512:(half + 1) * 512],
                            in1=ps[:], op=ALU.add), r=[bps, B_RES[tb]], w=[B_RES[tb]])
            C.barrier()
            R0.reset()
            kcr = C.ring(R0, "kcS", 2, [128, 512], F32)
            vcr = C.ring(R0, "vcS", 2, [128, 512], F32)
            qbr = C.ring(R0, "qbS", 2, [128, 512], F32)
            knr = C.ring(R0, "knS", 2, [1, 2, 512], F32)
            prod = R0.alloc([128, 512], F32); b_prod = Buf("prodS")
            OACC = R0.alloc([1, NS, 512], F32); b_oacc = Buf("OACC")
            smr = C.ring(R0, "smS", 2, [128, 4, 8], F32)
            DACC = C.sb(top, "DACC", [1, NS * 8], F32); b_dacc = Buf("DACC")
            OTS = C.sb(top, "OTS", [128, 4, NS], BF16); b_ots = Buf("OTS")
            DIL = (1, 4, 16)
            for s in range(NS):
                for g in range(3):
                    (Kc, bkc), (Vc, bvc), (qb, bqb), (kn, bkn), (sm, bsm) = kcr.next(), vcr.next(), qbr.next(), knr.next(), smr.next()
                    C.dma(lambda e, Kc=Kc, s=s, g=g: e.dma_start(out=Kc[:], in_=ck[g][s].rearrange("(i d) x -> d i x", d=DIL[g])[0]),
                          w=[bkc], sem=bkc)
                    C.dma(lambda e, Vc=Vc, s=s, g=g: e.dma_start(out=Vc[:], in_=cv[g][s].rearrange("(i d) x -> d i x", d=DIL[g])[0]),
                          w=[bvc], sem=bvc)
                    C.dma(lambda e, qb=qb, s=s, g=g: e.dma_start(out=qb[:], in_=qs_scr.ap()[s, g * 512:(g + 1) * 512].partition_broadcast(128)),
                          w=[bqb], sem=bqb)
                    C.dma(lambda e, kn=kn, s=s, g=g: e.dma_start(out=kn[0:1, 0, :], in_=sk[g][s, WG[g] - 1:WG[g], :]), w=[bkn], sem=bkn)
                    C.dma(lambda e, kn=kn, s=s, g=g: e.dma_start(out=kn[0:1, 1, :], in_=sv[g][s, WG[g] - 1:WG[g], :]), w=[bkn], sem=bkn)
                    C.dve(lambda e, Kc=Kc, qb=qb: e.tensor_tensor(out=prod[:], in0=Kc[:], in1=qb[:], op=ALU.mult), r=[bkc, bqb], w=[b_prod])
                    C.dve(lambda e, sm=sm: e.tensor_reduce(out=sm[:, 0, :], in_=prod[:].rearrange("p (h d) -> p h d", d=64),
                                                           axis=mybir.AxisListType.X, op=ALU.add), r=[b_prod], w=[bsm])
                    C.act(lambda e, sm=sm: e.activation(out=sm[:, 1, :], in_=sm[:, 0, :], func=AF.Exp, scale=0.125), r=[bsm], w=[bsm])
                    C.dve(lambda e, kn=kn, qb=qb: e.tensor_tensor(out=prod[0:1, :], in0=kn[0:1, 0, :], in1=qb[0:1, :], op=ALU.mult),
                          r=[bkn, bqb, b_prod], w=[b_prod])
                    C.dve(lambda e, sm=sm: e.tensor_reduce(out=sm[0:1, 2, :], in_=prod[0:1, :].rearrange("p (h d) -> p h d", d=64),
                                                           axis=mybir.AxisListType.X, op=ALU.add), r=[b_prod, bsm], w=[bsm])
                    C.act(lambda e, sm=sm: e.activation(out=sm[0:1, 3, :], in_=sm[0:1, 2, :], func=AF.Exp, scale=0.125), r=[bsm], w=[bsm])
                    pso, bpso = psf.next()
                    for h in range(8):
                        C.pe(lambda e, pso=pso, sm=sm, Vc=Vc, h=h: e.matmul(pso[0:1, h * 64:(h + 1) * 64], lhsT=sm[:, 1, h:h + 1],
                                                                           rhs=Vc[:, h * 64:(h + 1) * 64], start=True, stop=False),
                             r=[bsm, bvc], w=[bpso])
                        C.pe(lambda e, pso=pso, sm=sm, kn=kn, h=h: e.matmul(pso[0:1, h * 64:(h + 1) * 64], lhsT=sm[0:1, 3, h:h + 1],
                                                                           rhs=kn[0:1, 1, h * 64:(h + 1) * 64], start=False, stop=True),
                             r=[bsm, bkn], w=[bpso])
                    psd, bpsd = psf.next()
                    C.pe(lambda e, psd=psd, sm=sm: e.matmul(psd[0:1, 0:8], lhsT=ones_f[:, 0:1], rhs=sm[:, 1, :], start=True, stop=False),
                         r=[bsm, B_ones], w=[bpsd])
                    C.pe(lambda e, psd=psd, sm=sm: e.matmul(psd[0:1, 0:8], lhsT=ones_f[0:1, 0:1], rhs=sm[0:1, 3, :], start=False, stop=True),
                         r=[bsm, B_ones], w=[bpsd])
                    if g == 0:
                        C.dve(lambda e, pso=pso, s=s: e.tensor_copy(out=OACC[0:1, s, :], in_=pso[0:1, :]), r=[bpso], w=[b_oacc])
                        C.dve(lambda e, psd=psd, s=s: e.tensor_copy(out=DACC[0:1, s * 8:(s + 1) * 8], in_=psd[0:1, 0:8]), r=[bpsd], w=[b_dacc])
                    else:
                        C.dve(lambda e, pso=pso, s=s: e.tensor_tensor(out=OACC[0:1, s, :], in0=OACC[0:1, s, :], in1=pso[0:1, :], op=ALU.add),
                              r=[bpso, b_oacc], w=[b_oacc])
                        C.dve(lambda e, psd=psd, s=s: e.tensor_tensor(out=DACC[0:1, s * 8:(s + 1) * 8], in0=DACC[0:1, s * 8:(s + 1) * 8],
                                                                      in1=psd[0:1, 0:8], op=ALU.add), r=[bpsd, b_dacc], w=[b_dacc])
            C.dve(lambda e: e.reciprocal(out=DACC[:], in_=DACC[:]), r=[b_dacc], w=[b_dacc])
            OA3 = OACC[0:1].rearrange("p s (h d) -> p (s h) d", d=64)
            C.dve(lambda e: e.tensor_tensor(out=OA3, in0=OA3, in1=bc_last(DACC[0:1, :], 64), op=ALU.mult), r=[b_oacc, b_dacc], w=[b_oacc])
            pst, bpst = psf.next()
            for cc in range(4):
                for s in range(NS):
                    C.pe(lambda e, pst=pst, cc=cc, s=s: e.transpose(pst[:, cc * NS + s:cc * NS + s + 1], OACC[0:1, s, cc * 128:(cc + 1) * 128],
                                                                    ident_f[0:1, 0:1]), r=[b_oacc, B_ident], w=[bpst])
            C.act(lambda e, pst=pst: e.copy(out=OTS[:], in_=pst[:, 0:4 * NS].rearrange("p (c s) -> p c s", s=NS)), r=[bpst], w=[b_ots])
            ps, bps = sample_proj(WO, B_WO, 4, lambda k: OTS[:, k, :], b_ots)
            C.dve(lambda e, ps=ps: e.tensor_tensor(out=RST[:], in0=RST[:], in1=ps[:, 0:KC * NS].rearrange("p (c s) -> p c s", s=NS),
                                                   op=ALU.add), r=[bps, B_RST], w=[B_RST])
            C.barrier()
            chk(7)
            norm_pass()
            sample_norm()
            halo_exchange(1, e4s, e4r, "e4s", "e4r")
            C.barrier()
            ffn(1)
            R3.reset()
            gfin = R3.alloc([128, D], F32)
            C.dma(lambda e: e.dma_start(out=gfin[:], in_=final_norm.partition_broadcast(128)), w=[B_gfin], sem=B_ser, serial=True)
            yr = C.ring(R3, "yt", 2, [128, D], F32)
            ssr = C.ring(R3, "ssY", 2, [128, 4], F32)
            junk = R3.alloc([128, D], BF16); bj = Buf("junkY")
            ssa = R3.alloc([128, 3, NB], F32); bssa = Buf("ssaY")
            for tb in range(NB):
                C.act(lambda e, tb=tb: e.activation(out=junk[:], in_=RES[:, tb, :], func=AF.Square, accum_out=ssa[:, 0, tb:tb + 1]),
                      r=[B_RES[tb]], w=[bj, bssa])
            C.act(lambda e: e.activation(out=ssa[:, 1, :], in_=ssa[:, 0, :], func=AF.Sqrt, scale=1.0 / D, bias=EPS), r=[bssa], w=[bssa])
            C.dve(lambda e: e.reciprocal(out=ssa[:, 2, :], in_=ssa[:, 1, :]), r=[bssa], w=[bssa])
            for tb in range(NB):
                yt, byt = yr.next()
                C.dve(lambda e, yt=yt, tb=tb: e.scalar_tensor_tensor(
                    out=yt[:], in0=RES[:, tb, :], scalar=ssa[:, 2, tb:tb + 1], in1=gfin[:], op0=ALU.mult, op1=ALU.mult),
                    r=[B_RES[tb], bssa, B_gfin], w=[byt])
                C.dma(lambda e, yt=yt, tb=tb: e.dma_start(out=y_out[tb * 128:(tb + 1) * 128, :], in_=yt[:]), r=[byt], sem=byt)
            drip(len(drip_list))
            R3.reset()
            so = R3.alloc([64, 3072], F32); bso = Buf("so")

            def rows_out(src_fn, nchunk, nrow, dst):
                nonlocal_ps = psf.next()
                ps, bps = nonlocal_ps
                done = 0
                while done < nchunk:
                    n = min(4, nchunk - done)
                    ps, bps = psf.next()
                    for i in range(n):
                        C.pe(lambda e, ps=ps, i=i, c=done + i: e.transpose(ps[0:nrow, i * 128:(i + 1) * 128], src_fn(c), ident_f[:]),
                             r=[B_ident, B_HF, B_PCA, B_PF, B_sq, B_SH, B_CS, B_FS], w=[bps])
                    C.dve(lambda e, ps=ps, n=n, done=done: e.tensor_copy(out=so[0:nrow, done * 128:(done + n) * 128],
                                                                        in_=ps[0:nrow, 0:n * 128]), r=[bps], w=[bso])
                    done += n
                C.dma(lambda e: e.dma_start(out=dst, in_=so[0:nrow, 0:nchunk * 128]), r=[bso], sem=B_ser, serial=True)
            sample_rstd()
            C.dve(lambda e: e.tensor_tensor(out=sq_s[:], in0=RST[:], in1=bc_mid(st_s[:, 1, :], KC), op=ALU.mult),
                  r=[B_RST, B_sts, B_sq], w=[B_sq])
            C.dve(lambda e: e.tensor_tensor(out=sq_s[:], in0=sq_s[:], in1=bc_last(CV[:, CV_FIN:CV_FIN + KC], NS), op=ALU.mult),
                  r=[B_sq, B_CV], w=[B_sq])
            rows_out(lambda c: sq_s[:, c, :], 8, NS, ys_out)
            rows_out(lambda c: S_H[:, c, :], 8, NS, o_slru)
            rows_out(lambda c: CS[:, c, :], 8, NS * 3, o_sconva)
            for l in range(2):
                rows_out(lambda c, l=l: FS[:, l, c, :], FC, NS * 2, o_sffn[l])
            rows_out(lambda c: HF[:, c:c + 1], 8, 1, o_lru.rearrange("c p -> (c p)").rearrange("(o n) -> o n", o=1))
            rows_out(lambda c: PCA[:, c, :], 8, 3, o_conva)
            for l in range(2):
                rows_out(lambda c, l=l: PF[:, l, c, :], FC, 2, o_ffn[l])
        except StopBuild:
            pass
        P.emit(nc, top, block)
    return nc


WEIGHT_NAMES = ["a_w_in", "a_conv_w", "a_conv_b", "a_gate_a_w", "a_gate_a_b", "a_gate_x_w", "a_gate_x_b", "a_lambda",
                "a_w_out", "kv_norm", "w_kv", "b_w_q", "b_w_o", "norm_mix", "norm_ffn", "ffn_w_up", "ffn_conv_w",
                "ffn_conv_b", "ffn_w_down", "final_norm"]
KEEP_LEAD = ("norm_mix", "norm_ffn", "ffn_w_up", "ffn_conv_w", "ffn_conv_b", "ffn_w_down")


def make_in_maps(inp):
    f32 = np.float32
    shared = {}
    for n in WEIGHT_NAMES:
        a = np.asarray(inp[n], dtype=f32)
        if a.shape[0] == 1 and n not in KEEP_LEAD:
            a = a[0]
        shared[n] = np.ascontiguousarray(a)
    kq = np.arange(128)
    m_prev = np.where(kq[:, None] >= kq[None, :], 0.0, NEG).astype(f32)
    m_cur = np.where(kq[:, None] <= kq[None, :], 0.0, NEG).astype(f32)
    xpr = np.asarray(inp["x_prompt"], dtype=f32)
    xsa = np.asarray(inp["x_sample"], dtype=f32)
    lru = np.asarray(inp["state_lru_h"], dtype=f32)
    cva = np.asarray(inp["state_conv_a"], dtype=f32)
    ffs = np.asarray(inp["state_ffn_conv"], dtype=f32)
    maps = []
    for c in range(8):
        b, half = c // 2, c % 2
        sl = slice(NS * c, NS * (c + 1))
        m = dict(shared)
        m["xp"] = np.ascontiguousarray(xpr[b, half * NT:(half + 1) * NT])
        xsm = np.zeros((8, D), f32)
        xsm[0:NS] = xsa[sl, 0]
        if half == 1:
            xsm[4:7] = xpr[b, NT - 3:NT]
        m["xsm"] = xsm
        cm = np.empty((128, 3, 128), f32)
        cm[:, 0] = m_prev
        cm[:, 1] = m_prev if half == 1 else NEG
        cm[:, 2] = m_cur
        m["cmask"] = cm
        m["pmask"] = np.full((128, 1), float(half), f32)
        m["s_lru0"] = np.ascontiguousarray(lru[0, sl])
        m["s_conva0"] = np.ascontiguousarray(cva[0, sl].reshape(NS * 3, D))
        m["s_ffn0"] = np.ascontiguousarray(ffs[:, sl].reshape(2, NS * 2, DFF))
        for g, (kn, vn) in enumerate((("cache_k0", "cache_v0"), ("cache_k1", "cache_v1"), ("cache_k2", "cache_v2"))):
            m["ck%d" % g] = np.ascontiguousarray(np.asarray(inp[kn], dtype=f32)[sl].reshape(NS, -1, 512))
            m["cv%d" % g] = np.ascontiguousarray(np.asarray(inp[vn], dtype=f32)[sl].reshape(NS, -1, 512))
        maps.append(m)
    return maps


_NC_CACHE = {}


def kernel(**inputs):
    if "nc" not in _NC_CACHE:
        _NC_CACHE["nc"] = build_program()
    nc = _NC_CACHE["nc"]
    maps = make_in_maps(inputs)
    res = run_bass_kernel_spmd(nc, maps, core_ids=list(range(8)))
    R = res.results
    f32 = np.float32

    def cat(name, axis=0):
        return np.concatenate([np.asarray(R[c][name], dtype=f32) for c in range(8)], axis=axis)
    y_prompt = np.stack([np.concatenate([R[2 * b]["y"], R[2 * b + 1]["y"]], axis=0) for b in range(4)]).astype(f32)
    y_sample = cat("ys").reshape(32, 1, D)
    p_lru = np.stack([R[2 * b + 1]["o_lru"].reshape(D) for b in range(4)])[None].astype(f32)
    s_lru = cat("o_slru").reshape(1, 32, D)
    p_conva = np.stack([R[2 * b + 1]["o_conva"] for b in range(4)])[None].astype(f32)
    s_conva = cat("o_sconva").reshape(1, 32, 3, D)
    p_ffn = np.stack([R[2 * b + 1]["o_ffn"] for b in range(4)], axis=1).astype(f32)
    s_ffn = np.concatenate([np.asarray(R[c]["o_sffn"], dtype=f32).reshape(2, NS, 2, DFF) for c in range(8)], axis=1)
    outs = [y_prompt, y_sample, p_lru, s_lru, p_conva, s_conva, p_ffn, s_ffn]
    for g in range(3):
        pk = np.stack([R[2 * b + 1]["o_k%d" % g].reshape(-1, 8, 64) for b in range(4)]).astype(f32)
        pv = np.stack([R[2 * b + 1]["o_v%d" % g].reshape(-1, 8, 64) for b in range(4)]).astype(f32)
        skg = cat("sk%d" % g).reshape(32, -1, 8, 64)
        svg = cat("sv%d" % g).reshape(32, -1, 8, 64)
        outs += [pk, pv, skg, svg]
    return tuple(np.ascontiguousarray(o, dtype=f32) for o in outs)
```

```python
import numpy as np
from contextlib import ExitStack
import concourse.bass as bass
import concourse.mybir as mybir
from concourse.bass_utils import run_bass_kernel_spmd

F32 = mybir.dt.float32
BF16 = mybir.dt.bfloat16
AF = mybir.ActivationFunctionType
ALU = mybir.AluOpType

ENGS = ("pe", "dve", "act", "pool", "sp")


class Buf:
    __slots__ = ("name", "last_w", "readers", "semkey")

    def __init__(self, name, semkey=None):
        self.name = name
        self.last_w = None
        self.readers = []
        self.semkey = semkey if semkey is not None else ("buf", name)


class Prog:
    def __init__(self):
        self.ops = []
        self.dma_count = {}
        self.total_keys = set()

    def _deps(self, reads, writes, opid):
        deps = set()
        for b in reads:
            if b.last_w is not None:
                deps.add(b.last_w)
        for b in writes:
            if b.last_w is not None:
                deps.add(b.last_w)
            lastr = {}
            for r in b.readers:
                o = self.ops[r]
                if o["dma"] is not None:
                    deps.add(r)
                else:
                    lastr[o["eng"]] = r
            deps.update(lastr.values())
        for b in reads:
            b.readers.append(opid)
        for b in writes:
            b.last_w = opid
            b.readers = []
        deps.discard(opid)
        return deps

    def op(self, eng, emit, reads=(), writes=()):
        opid = len(self.ops)
        deps = self._deps(reads, writes, opid)
        self.ops.append(dict(id=opid, eng=eng, emit=emit, deps=deps, dma=None))
        return opid

    def dma(self, queue, emit, reads=(), writes=(), sem=None, inc=16, serial=False):
        opid = len(self.ops)
        deps = self._deps(reads, writes, opid)
        key = sem.semkey
        if serial:
            if not hasattr(self, "last_serial"):
                self.last_serial = {}
            if key in self.last_serial:
                deps.add(self.last_serial[key])
            self.last_serial[key] = opid
        cnt = self.dma_count.get(key, 0) + inc
        self.dma_count[key] = cnt
        self.ops.append(dict(id=opid, eng=queue, emit=emit, deps=deps, dma=(key, cnt, inc)))
        return opid

    def barrier(self):
        last = {}
        for o in self.ops:
            if o["emit"] is None:
                continue
            if o["dma"] is not None and o["dma"][0] in self.total_keys:
                continue
            k = ("e", o["eng"]) if o["dma"] is None else ("d", o["dma"][0])
            last[k] = o["id"]
        deps = set(last.values())
        for e in ENGS:
            self.ops.append(dict(id=len(self.ops), eng=e, emit=None, deps=set(deps), dma=None))

    def emit(self, nc, stack, block, final_wait_all=True):
        ops = self.ops
        needed = set()
        for o in ops:
            for dep in o["deps"]:
                d = ops[dep]
                if o["eng"] == "pe" and d["eng"] == "pe" and d["dma"] is None and o["dma"] is None:
                    continue
                needed.add(dep)
        ms = {e: 0 for e in ENGS}
        for o in ops:
            if o["dma"] is None and o["id"] in needed:
                ms[o["eng"]] += 1
                o["ms"] = ms[o["eng"]]
        prog_sem = {e: stack.enter_context(nc.semaphore("prog_" + e)) for e in ENGS}
        dma_sem = {}
        for key in self.dma_count:
            dma_sem[key] = stack.enter_context(nc.semaphore("dma_%d" % len(dma_sem)))
        self.n_sems = len(prog_sem) + len(dma_sem)

        def token(dep, at_id):
            d = ops[dep]
            if d["dma"] is not None:
                key, cnt, _inc = d["dma"]
                if key in self.total_keys:
                    cnt = self.dma_count[key]
                return dma_sem[key], cnt, ("dma", key)
            return prog_sem[d["eng"]], d["ms"], ("eng", d["eng"])

        per_eng = {e: [o for o in ops if o["eng"] == e] for e in ENGS}
        final_tokens = []
        if final_wait_all:
            for key, cnt in self.dma_count.items():
                final_tokens.append((dma_sem[key], cnt))

        def run_engine(ename, eobj):
            waited = {}
            for o in per_eng[ename]:
                for dep in sorted(o["deps"]):
                    d = ops[dep]
                    if ename == "pe" and d["eng"] == "pe" and d["dma"] is None:
                        continue
                    sem, val, k = token(dep, o["id"])
                    if waited.get(k, 0) < val:
                        eobj.wait_ge(sem, val)
                        waited[k] = val
                if o["emit"] is None:
                    continue
                ins = o["emit"](eobj)
                if o["dma"] is not None:
                    ins.then_inc(dma_sem[o["dma"][0]], o["dma"][2])
                elif "ms" in o:
                    ins.then_inc(prog_sem[ename], 1)
            if ename == "sp":
                for sem, val in final_tokens:
                    eobj.wait_ge(sem, val)
                for e in ENGS:
                    if ms[e] > 0:
                        eobj.wait_ge(prog_sem[e], ms[e])

        @block.tensor
        def _(e):
            run_engine("pe", e)

        @block.vector
        def _(e):
            run_engine("dve", e)

        @block.scalar
        def _(e):
            run_engine("act", e)

        @block.gpsimd
        def _(e):
            run_engine("pool", e)

        @block.sync
        def _(e):
            run_engine("sp", e)


NT = 2048
NB = NT // 128
D = 1024
KC = 8
DFF = 3072
FC = 24
EPS = 1e-6
NEG = -30000.0
NS = 4
CV_NM0, CV_NM1, CV_NF0, CV_NF1, CV_KVN = 0, 8, 16, 24, 32
CV_CW, CV_CB, CV_BA, CV_BX, CV_LAM = 40, 72, 80, 88, 96
CV_CA = 104
CV_FIN = 112
CV_F = (128, 224)
NCV = 320
NSLOT = 21


def bc_last(ap, n):
    return bass.AP(ap.tensor, ap.offset, [list(x) for x in ap.ap] + [[0, n]])


class Ring:
    def __init__(self, items):
        self.items = items
        self.i = 0

    def next(self):
        it = self.items[self.i % len(self.items)]
        self.i += 1
        return it


class Region:
    def __init__(self, arena, lo, hi):
        self.arena, self.lo, self.hi, self.p = arena, lo, hi, lo

    def reset(self):
        self.p = self.lo

    def alloc(self, shape, dt):
        esz = 4 if dt == F32 else 2
        n = esz
        for s in shape[1:]:
            n *= s
        n = (n + 31) // 32 * 32
        off = self.p
        self.p += n
        assert self.p <= self.hi, ("region overflow", self.lo, self.hi, self.p, shape)
        v = self.arena[:, off // 4:(off + n) // 4]
        if dt != F32:
            v = v.bitcast(dt)
        tot = 1
        for s in shape[1:]:
            tot *= s
        v = v[:, 0:tot]
        if len(shape) == 3:
            v = v.rearrange("p (a b) -> p a b", a=shape[1])
        elif len(shape) == 4:
            v = v.rearrange("p (a b c) -> p a b c", a=shape[1], b=shape[2])
        elif len(shape) == 5:
            v = v.rearrange("p (a b c d) -> p a b c d", a=shape[1], b=shape[2], c=shape[3])
        return v


class Ctx:
    def __init__(self, nc):
        self.nc = nc
        self.P = Prog()
        self.uid = 0

    def sb(self, scope, name, shape, dt):
        if isinstance(scope, Region):
            return scope.alloc(list(shape), dt)
        return scope.enter_context(self.nc.sbuf_tensor(name, list(shape), dt))

    def newkey(self):
        k = ("pool", getattr(self, "keyi", 0))
        self.keyi = getattr(self, "keyi", 0) + 1
        return k

    def reset_keys(self):
        self.keyi = 0

    def barrier(self):
        self.P.barrier()
        self.reset_keys()

    def ring(self, scope, name, n, shape, dt):
        items = []
        for i in range(n):
            t = self.sb(scope, "%s%d" % (name, i), shape, dt)
            items.append((t, Buf("%s%d" % (name, i), semkey=self.newkey())))
        return Ring(items)

    def pe(self, fn, r=(), w=()):
        self.P.op("pe", fn, r, w)

    def dve(self, fn, r=(), w=()):
        self.P.op("dve", fn, r, w)

    def act(self, fn, r=(), w=()):
        self.P.op("act", fn, r, w)

    def pool(self, fn, r=(), w=()):
        self.P.op("pool", fn, r, w)

    def dma(self, fn, r=(), w=(), sem=None, q="sp", serial=False):
        self.P.dma(q, fn, r, w, sem=sem, serial=serial)


def unit_tok(g, u):
    if g == 0:
        return 128 * u, 1
    if g == 1:
        return 512 * (u % 4) + (u // 4), 4
    return u, 16


def unit_slot(g, u):
    if g == 2:
        return u
    if g == 1:
        return 16 + u // 4 if u % 4 == 3 else None
    return 20 if u == 15 else None


def unit_prev(g, u):
    if g == 0:
        return ("own", u - 1) if u > 0 else ("partner", 20)
    if g == 1:
        return ("own", u - 1) if u % 4 > 0 else ("partner", 16 + u // 4)
    return ("partner", u)


class StopBuild(Exception):
    pass


def build_program(upto=99, dbg=False):
    nc = bass.Bass("TRN2", target_bir_lowering=False)
    C = Ctx(nc)
    P = C.P

    dbg_bufs = {}

    def dump(name, ap, bufs, shape):
        if not dbg:
            return
        dd = nc.dram_tensor("d_" + name, list(shape), ap.dtype if hasattr(ap, "dtype") else F32, kind="ExternalOutput").ap()
        P.dma("sp", lambda e: e.dma_start(out=dd, in_=ap), reads=list(bufs), writes=[], sem=B_ser, serial=True)

    def chk(n):
        if upto == n:
            raise StopBuild()

    def din(name, shape, dt=F32):
        return nc.dram_tensor(name, list(shape), dt, kind="ExternalInput").ap()

    def dout(name, shape, dt=F32):
        return nc.dram_tensor(name, list(shape), dt, kind="ExternalOutput").ap()

    def dint(name, shape, dt):
        return nc.dram_tensor(name, list(shape), dt)

    xp = din("xp", [NT, D])
    xsm = din("xsm", [8, D])
    w_in = din("a_w_in", [D, 2 * D])
    a_conv_w = din("a_conv_w", [4, D])
    a_conv_b = din("a_conv_b", [D])
    w_ga = din("a_gate_a_w", [4, 256, 256])
    b_ga = din("a_gate_a_b", [D])
    w_gx = din("a_gate_x_w", [4, 256, 256])
    b_gx = din("a_gate_x_b", [D])
    a_lam = din("a_lambda", [D])
    w_out = din("a_w_out", [D, D])
    kv_norm = din("kv_norm", [D])
    w_kv = din("w_kv", [D, 3072])
    w_q = din("b_w_q", [D, 1536])
    w_o = din("b_w_o", [512, D])
    norm_mix = din("norm_mix", [2, D])
    norm_ffn = din("norm_ffn", [2, D])
    w_up = din("ffn_w_up", [2, D, 2 * DFF])
    f_conv_w = din("ffn_conv_w", [2, 3, DFF])
    f_conv_b = din("ffn_conv_b", [2, DFF])
    w_down = din("ffn_w_down", [2, DFF, D])
    final_norm = din("final_norm", [D])
    cmask = din("cmask", [128, 3, 128])
    pmask_d = din("pmask", [128, 1])

    s_lru0 = din("s_lru0", [NS, D])
    s_conva0 = din("s_conva0", [NS * 3, D])
    s_ffn0 = din("s_ffn0", [2, NS * 2, DFF])
    WG = (128, 512, 2048)
    ck = [din("ck%d" % g, [NS, WG[g], 512]) for g in range(3)]
    cv = [din("cv%d" % g, [NS, WG[g], 512]) for g in range(3)]
    ys_out = dout("ys", [NS, D])
    o_slru = dout("o_slru", [NS, D])
    o_sconva = dout("o_sconva", [NS * 3, D])
    o_sffn = dout("o_sffn", [2, NS * 2, DFF])
    sk = [dout("sk%d" % g, [NS, WG[g], 512]) for g in range(3)]
    sv = [dout("sv%d" % g, [NS, WG[g], 512]) for g in range(3)]
    y_out = dout("y", [NT, D])
    o_lru = dout("o_lru", [8, 128])
    o_conva = dout("o_conva", [3, D])
    o_ffn = dout("o_ffn", [2, 2, DFF])
    o_k = [dout("o_k%d" % g, [n, 512]) for g, n in enumerate((128, 512, 2048))]
    o_v = [dout("o_v%d" % g, [n, 512]) for g, n in enumerate((128, 512, 2048))]

    qt_scr = dint("qt_scr", [3, 4, 128, NT], BF16)
    kt_scr = dint("kt_scr", [3, 4, 128, NT], BF16)
    v_scr = dint("v_scr", [3, 16, 128, 512], BF16)
    xs_k = [dint("xs_k%d" % j, [28, 16384], BF16) for j in range(3)]
    xr_k = [dint("xr_k%d" % j, [56, 16384], BF16) for j in range(3)]
    xs_v = [dint("xs_v%d" % j, [28, 16384], BF16) for j in range(3)]
    xr_v = [dint("xr_v%d" % j, [56, 16384], BF16) for j in range(3)]
    qs_scr = dint("qs_scr", [NS, 1536], F32)
    e1s = dint("e1s", [128, 8], F32)
    e1r = dint("e1r", [256, 8], F32)
    e2s = dint("e2s", [128, 16], BF16)
    e2r = dint("e2r", [256, 16], BF16)
    e4s = dint("e4s", [128, 16], BF16)
    e4r = dint("e4r", [256, 16], BF16)
    B_qt, B_kt, B_v = Buf("qt_scr"), Buf("kt_scr"), Buf("v_scr")
    B_xsk, B_xrk, B_xsv, B_xrv = Buf("xs_k"), Buf("xr_k"), Buf("xs_v"), Buf("xr_v")
    B_e = {n: Buf(n) for n in ("e1s", "e1r", "e2s", "e2r", "e4s", "e4r")}
    B_cc = Buf("cc")
    B_ser = Buf("serial")
    B_out = Buf("outs")
    P.total_keys.add(B_out.semkey)
    PAIRS = [[0, 1], [2, 3], [4, 5], [6, 7]]

    def allgather(src, dst, bs, bd):
        P.dma("pool", lambda e: e.collective_compute("AllGather", ALU.bypass, replica_groups=PAIRS,
                                                     ins=[src.ap().opt()], outs=[dst.ap().opt()]),
              reads=[bs], writes=[bd], sem=B_cc, inc=1)

    with ExitStack() as top:
        ident_f = C.sb(top, "ident_f", [128, 128], F32)
        ident_b = C.sb(top, "ident_b", [128, 128], BF16)
        CV = C.sb(top, "CV", [128, NCV], F32)
        pmask = C.sb(top, "pmask_s", [128, 1], F32)
        zero1 = C.sb(top, "zero1", [128, 1], F32)
        XST = C.sb(top, "XST", [128, KC, 8], F32)
        RST = C.sb(top, "RST", [128, KC, NS], F32)
        H0S = C.sb(top, "H0S", [128, KC, NS], F32)
        S_H = C.sb(top, "S_H", [128, KC, NS], F32)
        CS = C.sb(top, "CS", [128, KC, NS * 3], F32)
        FS = C.sb(top, "FS", [128, 2, FC, NS * 2], F32)
        GELS = C.sb(top, "GELS", [128, FC, NS], F32)
        XCS = C.sb(top, "XCS", [128, KC, NS], F32)
        XCSB = C.sb(top, "XCSB", [128, KC, NS], BF16)
        HGS = C.sb(top, "HGS", [128, KC, NS], BF16)
        ones_f = C.sb(top, "ones_f", [128, 128], F32)
        sq_s = C.sb(top, "sq_s", [128, KC, NS], F32)
        st_s = C.sb(top, "st_s", [128, 4, NS], F32)
        tsm = C.sb(top, "tsm", [128, 8, NS], F32)
        B_XST, B_RST, B_H0S, B_SH, B_CS, B_FS, B_GELS, B_XCS, B_XCSB, B_HGS, B_ones, B_sq, B_sts = (Buf(n) for n in (
            "XST", "RST", "H0S", "S_H", "CS", "FS", "GELS", "XCS", "XCSB", "HGS", "ones_f", "sq_s", "st_s"))
        B_tsm = [Buf("tsm%d" % i) for i in range(8)]
        maskb = C.sb(top, "maskb", [128, 3, 128], BF16)
        esel = C.sb(top, "esel", [128, 2, 128], BF16)
        KB = 1024
        arena = C.sb(top, "arena", [128, 196 * KB // 4], F32)
        R0 = Region(arena, 0, 32 * KB)
        R1 = Region(arena, 32 * KB, 96 * KB)
        R2 = Region(arena, 96 * KB, 160 * KB)
        R3 = Region(arena, 160 * KB, 196 * KB)
        BIGB = R0.alloc([128, KC, NT], BF16)
        HL = C.sb(top, "HL", [128, 8], F32)
        PCL = C.sb(top, "PCL", [128, 8], F32)
        HIN = C.sb(top, "HIN", [128, 8], F32)
        PCA = C.sb(top, "PCA", [128, 8, 3], F32)
        PF = C.sb(top, "PF", [128, 2, FC, 2], F32)
        B_ident, B_CV, B_pm, B_z, B_gfin, B_maskb, B_esel = (Buf(n) for n in
            ("ident", "CV", "pmask", "zero1", "gfin", "maskb", "esel"))
        B_BIG = [Buf("BIGB%d" % i) for i in range(4)]
        B_HL, B_PCL, B_HIN, B_PCA, B_PF = Buf("HL"), Buf("PCL"), Buf("HIN"), Buf("PCA"), Buf("PF")

        psf = Ring([(top.enter_context(nc.psum_tensor("psf%d" % i, [128, 512], F32)), Buf("psf%d" % i))
                    for i in range(6)])
        psb = Ring([(top.enter_context(nc.psum_tensor("psb%d" % i, [128, 1024], BF16)), Buf("psb%d" % i))
                    for i in range(2)])

        block = top.enter_context(nc.Block())

        try:
            drip_list = []
            for g in (2, 1, 0):
                nrow = WG[g] - 1
                cuts = [(r0, 128) for r0 in range(0, nrow - 127, 128)]
                done = len(cuts) * 128
                cuts += [(done, 112), (done + 112, 15)]
                assert done + 127 == nrow
                for srct, dstt in ((ck[g], sk[g]), (cv[g], sv[g])):
                    for s in range(NS):
                        for (r0, n) in cuts:
                            drip_list.append((srct[s, 1 + r0:1 + r0 + n, :], dstt[s, r0:r0 + n, :]))
            drip_pos = [0]

            def drip(n=1):
                for _ in range(n):
                    if drip_pos[0] < len(drip_list):
                        sa, da = drip_list[drip_pos[0]]
                        drip_pos[0] += 1
                        P.dma("sp", lambda e, sa=sa, da=da: e.dma_start(out=da, in_=sa), reads=[], writes=[], sem=B_out)
            C.pool(lambda e: e.memset(ident_f[:], 0.0), w=[B_ident])
            C.pool(lambda e: e.affine_select(out=ident_f[:], in_=ident_f[:], pattern=[[-1, 128]],
                                             compare_op=ALU.not_equal, fill=1.0, base=0, channel_multiplier=1),
                   r=[B_ident], w=[B_ident])
            C.dve(lambda e: e.tensor_copy(out=ident_b[:], in_=ident_f[:]), r=[B_ident], w=[B_ident])
            C.pool(lambda e: e.memset(zero1[:], 0.0), w=[B_z])
            C.pool(lambda e: e.memset(esel[:], 0.0), w=[B_esel])
            C.pool(lambda e: e.memset(esel[:, 0, 0:64], 1.0), r=[B_esel], w=[B_esel])
            C.pool(lambda e: e.memset(esel[:, 1, 64:128], 1.0), r=[B_esel], w=[B_esel])
            C.dma(lambda e: e.dma_start(out=pmask[:], in_=pmask_d), w=[B_pm], sem=B_ser, serial=True)
            C.pool(lambda e: e.memset(ones_f[:], 1.0), w=[B_ones])
            if True:
                sc = R3
                mstage = C.sb(sc, "mstage", [128, 3, 128], F32)
                B_ms = Buf("mstage")
                C.dma(lambda e: e.dma_start(out=mstage[:], in_=cmask), w=[B_ms], sem=B_ser, serial=True)
                C.dve(lambda e: e.tensor_copy(out=maskb[:], in_=mstage[:]), r=[B_ms], w=[B_maskb])
                vst = [C.sb(sc, "vst%d" % i, [128, 128], F32) for i in range(3)]
                B_vst = [Buf("vst%d" % i) for i in range(3)]
                for i in range(3):
                    C.pool(lambda e, i=i: e.memset(vst[i][:], 0.0), w=[B_vst[i]])

                def ldv(i, row, src, nrows):
                    C.dma(lambda e: e.dma_start(out=vst[i][row:row + nrows, :], in_=src), r=[], w=[B_vst[i]], sem=B_ser, serial=True)
                ldv(0, CV_NM0, norm_mix.rearrange("l (c p) -> (l c) p", p=128), 16)
                ldv(0, CV_NF0, norm_ffn.rearrange("l (c p) -> (l c) p", p=128), 16)
                ldv(0, CV_KVN, kv_norm.rearrange("(c p) -> c p", p=128), 8)
                ldv(0, CV_CW, a_conv_w.rearrange("k (c p) -> (k c) p", p=128), 32)
                ldv(0, CV_CB, a_conv_b.rearrange("(c p) -> c p", p=128), 8)
                ldv(0, CV_BA, b_ga.rearrange("(c p) -> c p", p=128), 8)
                ldv(0, CV_BX, b_gx.rearrange("(c p) -> c p", p=128), 8)
                ldv(0, CV_LAM, a_lam.rearrange("(c p) -> c p", p=128), 8)
                ldv(0, CV_FIN, final_norm.rearrange("(c p) -> c p", p=128), 8)
                for l in range(2):
                    ldv(1 + l, 0, f_conv_w[l].rearrange("k (c p) -> (k c) p", p=128), 72)
                    ldv(1 + l, 72, f_conv_b[l].rearrange("(c p) -> c p", p=128), 24)
                for i, (c0, n) in enumerate(((0, 128), (128, 96), (224, 96))):
                    pt, bpt = psf.next()
                    C.pe(lambda e, i=i, pt=pt, n=n: e.transpose(pt[:, 0:n], vst[i][0:n, :], ident_f[0:n, 0:n]),
                         r=[B_vst[i], B_ident], w=[bpt])
                    C.dve(lambda e, pt=pt, c0=c0, n=n: e.tensor_copy(out=CV[:, c0:c0 + n], in_=pt[:, 0:n]),
                          r=[bpt], w=[B_CV])
                rin = C.ring(sc, "rin", 2, [NS * 3, D], F32)

                def rows_in(srcd, nrow, nchunk, dst_fn, bdst):
                    for c0 in range(0, nchunk, 8):
                        n = min(8, nchunk - c0)
                        stg, bstg = rin.next()
                        C.dma(lambda e, stg=stg, c0=c0, n=n: e.dma_start(out=stg[0:nrow, 0:n * 128],
                                                                        in_=srcd[:, c0 * 128:(c0 + n) * 128]), w=[bstg], sem=bstg)
                        ps, bps = psf.next()
                        for i in range(n):
                            C.pe(lambda e, ps=ps, stg=stg, i=i: e.transpose(ps[:, i * nrow:(i + 1) * nrow],
                                                                            stg[0:nrow, i * 128:(i + 1) * 128],
                                                                            ident_f[0:nrow, 0:nrow]), r=[bstg, B_ident], w=[bps])
                        C.dve(lambda e, ps=ps, c0=c0, n=n: e.tensor_copy(
                            out=dst_fn(c0, n), in_=ps[:, 0:n * nrow].rearrange("p (c r) -> p c r", r=nrow)), r=[bps], w=[bdst])
                rows_in(s_lru0, NS, KC, lambda c0, n: H0S[:, c0:c0 + n, :], B_H0S)
                rows_in(s_conva0, NS * 3, KC, lambda c0, n: CS[:, c0:c0 + n, :], B_CS)
                for l in range(2):
                    rows_in(s_ffn0[l], NS * 2, FC, lambda c0, n, l=l: FS[:, l, c0:c0 + n, :], B_FS)
                C.act(lambda e: e.activation(out=CV[:, CV_CA:CV_CA + 8], in_=CV[:, CV_LAM:CV_LAM + 8], func=AF.Sigmoid),
                      r=[B_CV], w=[B_CV])
                C.act(lambda e: e.activation(out=CV[:, CV_CA:CV_CA + 8], in_=CV[:, CV_CA:CV_CA + 8], func=AF.Ln),
                      r=[B_CV], w=[B_CV])
                C.dve(lambda e: e.tensor_scalar(out=CV[:, CV_CA:CV_CA + 8], in0=CV[:, CV_CA:CV_CA + 8], scalar1=8.0,
                                                scalar2=None, op0=ALU.mult), r=[B_CV], w=[B_CV])
                C.barrier()
            def norm_T(x_t, bx, n, dst_ap, bdst, scope_bufs):
                junk, bj, ss, bs_, xn, bxn = scope_bufs
                C.act(lambda e: e.activation(out=junk[0:n, :], in_=x_t[0:n, :], func=AF.Square, accum_out=ss[0:n, 0:1]),
                      r=[bx], w=[bj, bs_])
                C.act(lambda e: e.activation(out=ss[0:n, 1:2], in_=ss[0:n, 0:1], func=AF.Sqrt, scale=1.0 / D, bias=EPS),
                      r=[bs_], w=[bs_])
                C.dve(lambda e: e.reciprocal(out=ss[0:n, 2:3], in_=ss[0:n, 1:2]), r=[bs_], w=[bs_])
                C.dve(lambda e: e.tensor_scalar(out=xn[0:n, :], in0=x_t[0:n, :], scalar1=ss[0:n, 2:3], scalar2=None,
                                                op0=ALU.mult), r=[bx, bs_], w=[bxn])
                pt, bpt = psb.next()
                for c in range(KC):
                    C.pe(lambda e, c=c: e.transpose(pt[:, c * 128:c * 128 + n], xn[0:n, c * 128:(c + 1) * 128],
                                                    ident_b[0:n, 0:n]), r=[bxn, B_ident], w=[bpt])
                C.act(lambda e: e.copy(out=dst_ap, in_=pt[:].rearrange("p (c x) -> p c x", c=KC)[:, :, 0:n]),
                      r=[bpt], w=[bdst])

            def load_w(src3, kch, ncol, gain_col, stage, wbuf, eng="pool"):
                (sg, bsg), (wb, bwb) = stage.next(), wbuf.next()
                C.dma(lambda e: e.dma_start(out=sg[:, 0:kch, 0:ncol], in_=src3), w=[bsg], sem=bsg)
                drip(1)
                if gain_col is None:
                    P.op(eng, lambda e: e.tensor_copy(out=wb[:, 0:kch, 0:ncol], in_=sg[:, 0:kch, 0:ncol]), [bsg], [bwb])
                else:
                    P.op(eng, lambda e: e.tensor_tensor(out=wb[:, 0:kch, 0:ncol], in0=sg[:, 0:kch, 0:ncol],
                                                        in1=bc_last(CV[:, gain_col:gain_col + kch], ncol), op=ALU.mult),
                         [bsg, B_CV], [bwb])
                return wb, bwb

            def bc_mid(ap2, n):
                (s0_, n0_), (s1_, n1_) = ap2.ap
                return bass.AP(ap2.tensor, ap2.offset, [[s0_, n0_], [0, n], [s1_, n1_]])

            def sample_rstd():
                C.act(lambda e: e.activation(out=sq_s[:], in_=RST[:], func=AF.Square), r=[B_RST], w=[B_sq])
                ps, bps = psf.next()
                for c in range(KC):
                    C.pe(lambda e, ps=ps, c=c: e.matmul(ps[:, 0:NS], lhsT=ones_f[:], rhs=sq_s[:, c, :],
                                                        start=(c == 0), stop=(c == KC - 1)), r=[B_ones, B_sq], w=[bps])
                C.act(lambda e, ps=ps: e.activation(out=st_s[:, 0, :], in_=ps[:, 0:NS], func=AF.Sqrt, scale=1.0 / D, bias=EPS),
                      r=[bps], w=[B_sts])
                C.dve(lambda e: e.reciprocal(out=st_s[:, 1, :], in_=st_s[:, 0, :]), r=[B_sts], w=[B_sts])

            def sample_norm():
                sample_rstd()
                C.dve(lambda e: e.tensor_tensor(out=HNS[:, :, 0:NS], in0=RST[:], in1=bc_mid(st_s[:, 1, :], KC), op=ALU.mult),
                      r=[B_RST, B_sts], w=[B_HNS])

            def sample_proj(w_t, bw, nk, rhs_fn, brhs):
                ps, bps = psf.next()
                for fc in range(KC):
                    for k in range(nk):
                        C.pe(lambda e, ps=ps, fc=fc, k=k: e.matmul(ps[:, fc * NS:(fc + 1) * NS],
                                                                   lhsT=w_t[:, k, fc * 128:(fc + 1) * 128], rhs=rhs_fn(k),
                                                                   start=(k == 0), stop=(k == nk - 1)), r=[bw, brhs], w=[bps])
                return ps, bps

            def norm_pass():
                R3.reset()
                junk = R3.alloc([128, D], BF16); bj = Buf("junkN%d" % C.uid); C.uid += 1
                ssa = R3.alloc([128, 3, NB], F32); bssa = Buf("ssaN%d" % C.uid)
                xnr = C.ring(R3, "xnN", 3, [128, D], BF16)
                for tb in range(NB):
                    C.act(lambda e, tb=tb: e.activation(out=junk[:], in_=RES[:, tb, :], func=AF.Square,
                                                        accum_out=ssa[:, 0, tb:tb + 1]), r=[B_RES[tb]], w=[bj, bssa])
                C.act(lambda e: e.activation(out=ssa[:, 1, :], in_=ssa[:, 0, :], func=AF.Sqrt, scale=1.0 / D, bias=EPS),
                      r=[bssa], w=[bssa])
                C.dve(lambda e: e.reciprocal(out=ssa[:, 2, :], in_=ssa[:, 1, :]), r=[bssa], w=[bssa])
                for tb in range(NB):
                    xn, bxn = xnr.next()
                    C.dve(lambda e, xn=xn, tb=tb: e.tensor_scalar(out=xn[:], in0=RES[:, tb, :], scalar1=ssa[:, 2, tb:tb + 1],
                                                                  scalar2=None, op0=ALU.mult), r=[B_RES[tb], bssa], w=[bxn])
                    pt, bpt = psb.next()
                    for c in range(KC):
                        C.pe(lambda e, pt=pt, xn=xn, c=c: e.transpose(pt[:, c * 128:(c + 1) * 128], xn[:, c * 128:(c + 1) * 128],
                                                                      ident_b[:]), r=[bxn, B_ident], w=[bpt])
                    C.act(lambda e, pt=pt, tb=tb: e.copy(out=BIGB[:, :, tb * 128:(tb + 1) * 128],
                                                         in_=pt[:].rearrange("p (c x) -> p c x", c=KC)), r=[bpt], w=[B_BIG[tb // 4]])
                C.barrier()

            if True:
                pa = R1
                R1.reset(); R2.reset(); R3.reset()
                HN0 = C.sb(pa, "HN0", [128, KC, NT], BF16)
                B_HN0 = [Buf("HN0_%d" % i) for i in range(4)]
                PG = C.sb(pa, "PG", [128, KC, NT], BF16)
                B_PG = [Buf("PG%d" % i) for i in range(4)]
                HNS = C.sb(top, "HNS", [128, KC, 8], BF16)
                B_HNS = Buf("HNS")
                if True:
                    s0 = R2
                    xr_ = C.ring(s0, "xblk", 2, [128, D], F32)
                    junk = C.sb(s0, "junk", [128, D], BF16)
                    ssr = C.ring(s0, "ss", 2, [128, 4], F32)
                    xnr = C.ring(s0, "xn", 2, [128, D], BF16)
                    bj = Buf("junk")
                    for tb in range(NB):
                        (xb, bxb), (ss, bss), (xn, bxn) = xr_.next(), ssr.next(), xnr.next()
                        C.dma(lambda e, xb=xb, tb=tb: e.dma_start(out=xb[:], in_=xp[tb * 128:(tb + 1) * 128, :]),
                              w=[bxb], sem=bxb)
                        norm_T(xb, bxb, 128, HN0[:, :, tb * 128:(tb + 1) * 128], B_HN0[tb // 4],
                               (junk, bj, ss, bss, xn, bxn))
                    (xb, bxb), (ss, bss), (xn, bxn) = xr_.next(), ssr.next(), xnr.next()
                    C.dma(lambda e, xb=xb: e.dma_start(out=xb[0:8, :], in_=xsm), w=[bxb], sem=bxb)
                    norm_T(xb, bxb, 8, HNS[:, :, 0:8], B_HNS, (junk, bj, ss, bss, xn, bxn))
                    ps, bps = psf.next()
                    for c in range(KC):
                        C.pe(lambda e, ps=ps, xb=xb, c=c: e.transpose(ps[:, c * 8:(c + 1) * 8], xb[0:8, c * 128:(c + 1) * 128],
                                                                      ident_f[0:8, 0:8]), r=[bxb, B_ident], w=[bps])
                    C.dve(lambda e, ps=ps: e.tensor_copy(out=XST[:], in_=ps[:, 0:64].rearrange("p (c r) -> p c r", r=8)),
                          r=[bps], w=[B_XST])
                C.barrier()
                if True:
                    R2.reset(); R3.reset()
                    s1 = R2
                    wst = C.ring(R3, "wst", 3, [128, KC, 128], F32)
                    wbf = C.ring(R3, "wbf", 3, [128, KC, 128], BF16)
                    gst = C.ring(R3, "gst", 2, [128, 2, 256], F32)
                    gbf = C.ring(R3, "gbf", 2, [128, 2, 256], BF16)
                    USR = C.sb(R3, "USR", [128, 16, 8], F32)
                    UR = C.sb(R3, "UR", [128, 3 + NT + 5], F32)
                    B_UR = Buf("UR")
                    XC = C.sb(s1, "XC", [128, 2, NT], F32)
                    B_XC = [Buf("XC0"), Buf("XC1")]
                    XCB = C.sb(s1, "XCB", [128, 2, NT], BF16)
                    B_XCB = [Buf("XCB0"), Buf("XCB1")]
                    RAr = C.sb(s1, "RArow", [128, NT], F32); B_RAr = Buf("RArow")
                    IIr = C.sb(s1, "IIrow", [128, NT], F32); B_IIr = Buf("IIrow")
                    SSr = C.sb(s1, "SSrow", [128, NT], F32); B_SSr = Buf("SSrow")
                    hr = C.ring(s1, "hr", 3, [128, 512], F32)
                    pr = C.ring(s1, "pr", 3, [128, 512], F32)
                    gtr = C.ring(s1, "gt", 2, [128, 512], BF16)
                    B_USR = Buf("USR")
                    w_in3 = w_in.rearrange("(k p) n -> p k n", p=128)

                    def win_chunk(col0, evac):
                        wb, bwb = load_w(w_in3[:, :, col0:col0 + 128], KC, 128, CV_NM0, wst, wbf)
                        for tt in range(4):
                            ps, bps = psf.next()
                            for k in range(KC):
                                C.pe(lambda e, ps=ps, wb=wb, k=k, tt=tt: e.matmul(
                                    ps[:], lhsT=wb[:, k, :], rhs=HN0[:, k, tt * 512:(tt + 1) * 512],
                                    start=(k == 0), stop=(k == KC - 1)), r=[bwb, B_HN0[tt]], w=[bps])
                            evac(tt, ps, bps)
                        ps, bps = psf.next()
                        for k in range(KC):
                            C.pe(lambda e, ps=ps, wb=wb, k=k: e.matmul(ps[:, 0:8], lhsT=wb[:, k, :], rhs=HNS[:, k, :],
                                                                        start=(k == 0), stop=(k == KC - 1)),
                                 r=[bwb, B_HNS], w=[bps])
                        cidx = col0 // 128
                        C.act(lambda e, ps=ps, cidx=cidx: e.copy(out=USR[:, cidx, :], in_=ps[:, 0:8]), r=[bps], w=[B_USR])

                    for nb in range(4):
                        for j in range(2):
                            c = 2 * nb + j

                            def ev_rec(tt, ps, bps):
                                C.act(lambda e, ps=ps, tt=tt: e.copy(out=UR[:, 3 + tt * 512:3 + (tt + 1) * 512], in_=ps[:]),
                                      r=[bps], w=[B_UR])
                            win_chunk(D + c * 128, ev_rec)
                            C.act(lambda e, c=c: e.copy(out=UR[:, 0:3], in_=USR[:, 8 + c, 4:7]), r=[B_USR], w=[B_UR])
                            C.act(lambda e, c=c: e.copy(out=PCA[:, c, :], in_=UR[:, NT:NT + 3]), r=[B_UR], w=[B_PCA])
                            C.act(lambda e, c=c, j=j: e.activation(out=XC[:, j, :], in_=UR[:, 0:NT], func=AF.Identity,
                                                                    scale=CV[:, CV_CW + c:CV_CW + c + 1],
                                                                    bias=CV[:, CV_CB + c:CV_CB + c + 1]),
                                  r=[B_UR, B_CV], w=[B_XC[j]])
                            for k in range(1, 4):
                                C.dve(lambda e, c=c, j=j, k=k: e.scalar_tensor_tensor(
                                    out=XC[:, j, :], in0=UR[:, k:k + NT], scalar=CV[:, CV_CW + 8 * k + c:CV_CW + 8 * k + c + 1],
                                    in1=XC[:, j, :], op0=ALU.mult, op1=ALU.add), r=[B_UR, B_CV, B_XC[j]], w=[B_XC[j]])
                            C.pool(lambda e, j=j: e.tensor_copy(out=XCB[:, j, :], in_=XC[:, j, :]), r=[B_XC[j]], w=[B_XCB[j]])
                            CS4 = CS[:].rearrange("p c (s k) -> p c s k", k=3)
                            C.act(lambda e, c=c, CS4=CS4: e.activation(out=XCS[:, c, :], in_=CS4[:, c, :, 0], func=AF.Identity,
                                                                       scale=CV[:, CV_CW + c:CV_CW + c + 1],
                                                                       bias=CV[:, CV_CB + c:CV_CB + c + 1]),
                                  r=[B_CS, B_CV], w=[B_XCS])
                            for k in range(1, 4):
                                srck = CS4[:, c, :, k] if k < 3 else USR[:, 8 + c, 0:NS]
                                C.dve(lambda e, c=c, k=k, srck=srck: e.scalar_tensor_tensor(
                                    out=XCS[:, c, :], in0=srck, scalar=CV[:, CV_CW + 8 * k + c:CV_CW + 8 * k + c + 1],
                                    in1=XCS[:, c, :], op0=ALU.mult, op1=ALU.add), r=[B_CS, B_USR, B_CV, B_XCS], w=[B_XCS])
                            C.pool(lambda e, c=c: e.tensor_copy(out=XCSB[:, c, :], in_=XCS[:, c, :]), r=[B_XCS], w=[B_XCSB])
                            for k in range(3):
                                srck = CS4[:, c, :, k + 1] if k < 2 else USR[:, 8 + c, 0:NS]
                                C.dve(lambda e, c=c, k=k, srck=srck, CS4=CS4: e.tensor_copy(out=CS4[:, c, :, k], in_=srck),
                                      r=[B_CS, B_USR, B_XCS], w=[B_CS])
                        wa, bwa = load_w(w_ga[nb].rearrange("(k p) n -> p k n", p=128), 2, 256, None, gst, gbf)
                        wx, bwx = load_w(w_gx[nb].rearrange("(k p) n -> p k n", p=128), 2, 256, None, gst, gbf)
                        for j in range(2):
                            c = 2 * nb + j
                            gwb, bgwb = load_w(w_in3[:, :, c * 128:(c + 1) * 128], KC, 128, CV_NM0, wst, wbf)
                            for tt in range(4):
                                sl = slice(tt * 512, (tt + 1) * 512)
                                psr, bpsr = psf.next()
                                psi, bpsi = psf.next()
                                for k in range(2):
                                    C.pe(lambda e, psr=psr, wa=wa, k=k, j=j, sl=sl: e.matmul(
                                        psr[:], lhsT=wa[:, k, j * 128:(j + 1) * 128], rhs=XCB[:, k, sl],
                                        start=(k == 0), stop=(k == 1)), r=[bwa, B_XCB[k]], w=[bpsr])
                                for k in range(2):
                                    C.pe(lambda e, psi=psi, wx=wx, k=k, j=j, sl=sl: e.matmul(
                                        psi[:], lhsT=wx[:, k, j * 128:(j + 1) * 128], rhs=XCB[:, k, sl],
                                        start=(k == 0), stop=(k == 1)), r=[bwx, B_XCB[k]], w=[bpsi])
                                C.act(lambda e, psr=psr, c=c, sl=sl: e.activation(
                                    out=RAr[:, sl], in_=psr[:], func=AF.Sigmoid, bias=CV[:, CV_BA + c:CV_BA + c + 1]),
                                    r=[bpsr, B_CV], w=[B_RAr])
                                C.act(lambda e, psi=psi, c=c, sl=sl: e.activation(
                                    out=IIr[:, sl], in_=psi[:], func=AF.Sigmoid, bias=CV[:, CV_BX + c:CV_BX + c + 1]),
                                    r=[bpsi, B_CV], w=[B_IIr])
                            C.act(lambda e, c=c: e.activation(out=RAr[:], in_=RAr[:], func=AF.Exp, scale=CV[:, CV_CA + c:CV_CA + c + 1]),
                                  r=[B_RAr, B_CV], w=[B_RAr])
                            C.pool(lambda e: e.tensor_tensor(out=SSr[:], in0=RAr[:], in1=RAr[:], op=ALU.mult), r=[B_RAr], w=[B_SSr])
                            C.act(lambda e: e.activation(out=SSr[:], in_=SSr[:], func=AF.Sqrt, scale=-1.0, bias=1.0), r=[B_SSr], w=[B_SSr])
                            C.dve(lambda e: e.tensor_tensor(out=IIr[:], in0=IIr[:], in1=SSr[:], op=ALU.mult), r=[B_IIr, B_SSr], w=[B_IIr])
                            C.dve(lambda e, j=j: e.tensor_tensor(out=IIr[:], in0=IIr[:], in1=XC[:, j, :], op=ALU.mult),
                                  r=[B_IIr, B_XC[j]], w=[B_IIr])
                            hprev = pprev = None
                            for tt in range(4):
                                sl = slice(tt * 512, (tt + 1) * 512)
                                (h, bh), (pc, bpc) = hr.next(), pr.next()
                                hinit = 0.0 if hprev is None else hprev[0][:, 511:512]
                                pinit = 1.0 if pprev is None else pprev[0][:, 511:512]
                                C.dve(lambda e, h=h, sl=sl, hinit=hinit: e.tensor_tensor_scan(
                                    out=h[:], data0=RAr[:, sl], data1=IIr[:, sl], initial=hinit, op0=ALU.mult, op1=ALU.add),
                                    r=[B_RAr, B_IIr] + ([hprev[1]] if hprev else []), w=[bh])
                                C.dve(lambda e, pc=pc, sl=sl, pinit=pinit: e.tensor_tensor_scan(
                                    out=pc[:], data0=RAr[:, sl], data1=zero1[:, 0:1].broadcast_to([128, 512]), initial=pinit,
                                    op0=ALU.mult, op1=ALU.add), r=[B_RAr, B_z] + ([pprev[1]] if pprev else []), w=[bpc])
                                hprev, pprev = (h, bh), (pc, bpc)
                                psg, bpsg = psf.next()
                                for k in range(KC):
                                    C.pe(lambda e, psg=psg, gwb=gwb, k=k, sl=sl: e.matmul(
                                        psg[:], lhsT=gwb[:, k, :], rhs=HN0[:, k, sl], start=(k == 0), stop=(k == KC - 1)),
                                        r=[bgwb, B_HN0[tt]], w=[bpsg])
                                gt, bgt = gtr.next()
                                C.act(lambda e, gt=gt, psg=psg: e.activation(out=gt[:], in_=psg[:], func=AF.Gelu),
                                      r=[bpsg], w=[bgt])
                                C.dve(lambda e, h=h, gt=gt, c=c, sl=sl: e.tensor_tensor(
                                    out=BIGB[:, c, sl], in0=h[:], in1=gt[:], op=ALU.mult), r=[bh, bgt], w=[B_BIG[tt]])
                                C.pool(lambda e, pc=pc, gt=gt, c=c, sl=sl: e.tensor_tensor(
                                    out=PG[:, c, sl], in0=pc[:], in1=gt[:], op=ALU.mult), r=[bpc, bgt], w=[B_PG[tt]])
                            C.act(lambda e, h=hprev[0], c=c: e.copy(out=HL[:, c:c + 1], in_=h[:, 511:512]),
                                  r=[hprev[1]], w=[B_HL])
                            C.act(lambda e, pc=pprev[0], c=c: e.copy(out=PCL[:, c:c + 1], in_=pc[:, 511:512]),
                                  r=[pprev[1]], w=[B_PCL])
                            ps, bps = psf.next()
                            for k in range(KC):
                                C.pe(lambda e, ps=ps, gwb=gwb, k=k: e.matmul(ps[:, 0:8], lhsT=gwb[:, k, :], rhs=HNS[:, k, :],
                                                                              start=(k == 0), stop=(k == KC - 1)),
                                     r=[bgwb, B_HNS], w=[bps])
                            C.act(lambda e, ps=ps, c=c: e.copy(out=USR[:, c, :], in_=ps[:, 0:8]), r=[bps], w=[B_USR])
                            psr, bpsr = psf.next()
                            for gi, wgt, bwgt in ((0, wa, bwa), (1, wx, bwx)):
                                for k in range(2):
                                    C.pe(lambda e, psr=psr, wgt=wgt, k=k, j=j, gi=gi, nb=nb: e.matmul(
                                        psr[:, gi * NS:(gi + 1) * NS], lhsT=wgt[:, k, j * 128:(j + 1) * 128],
                                        rhs=XCSB[:, 2 * nb + k, :], start=(k == 0), stop=(k == 1)), r=[bwgt, B_XCSB], w=[bpsr])
                            bt0 = B_tsm[0]
                            C.act(lambda e, psr=psr, c=c: e.activation(out=tsm[:, 0, :], in_=psr[:, 0:NS], func=AF.Sigmoid,
                                                                       bias=CV[:, CV_BA + c:CV_BA + c + 1]), r=[bpsr, B_CV], w=[bt0])
                            C.act(lambda e, c=c: e.activation(out=tsm[:, 1, :], in_=tsm[:, 0, :], func=AF.Exp,
                                                              scale=CV[:, CV_CA + c:CV_CA + c + 1]), r=[bt0, B_CV], w=[bt0])
                            C.act(lambda e, psr=psr, c=c: e.activation(out=tsm[:, 2, :], in_=psr[:, NS:2 * NS], func=AF.Sigmoid,
                                                                       bias=CV[:, CV_BX + c:CV_BX + c + 1]), r=[bpsr, B_CV], w=[bt0])
                            C.act(lambda e: e.activation(out=tsm[:, 3, :], in_=tsm[:, 1, :], func=AF.Square), r=[bt0], w=[bt0])
                            C.act(lambda e: e.activation(out=tsm[:, 3, :], in_=tsm[:, 3, :], func=AF.Sqrt, scale=-1.0, bias=1.0),
                                  r=[bt0], w=[bt0])
                            C.dve(lambda e: e.tensor_tensor(out=tsm[:, 2, :], in0=tsm[:, 2, :], in1=tsm[:, 3, :], op=ALU.mult),
                                  r=[bt0], w=[bt0])
                            C.dve(lambda e, c=c: e.tensor_tensor(out=tsm[:, 2, :], in0=tsm[:, 2, :], in1=XCS[:, c, :], op=ALU.mult),
                                  r=[bt0, B_XCS], w=[bt0])
                            C.dve(lambda e, c=c: e.tensor_tensor(out=tsm[:, 1, :], in0=tsm[:, 1, :], in1=H0S[:, c, :], op=ALU.mult),
                                  r=[bt0, B_H0S], w=[bt0])
                            C.dve(lambda e, c=c: e.tensor_tensor(out=S_H[:, c, :], in0=tsm[:, 1, :], in1=tsm[:, 2, :], op=ALU.add),
                                  r=[bt0], w=[B_SH])
                            C.act(lambda e, c=c: e.activation(out=tsm[:, 4, :], in_=USR[:, c, 0:NS], func=AF.Gelu),
                                  r=[B_USR], w=[bt0])
                            C.dve(lambda e, c=c: e.tensor_tensor(out=HGS[:, c, :], in0=S_H[:, c, :], in1=tsm[:, 4, :], op=ALU.mult),
                                  r=[bt0, B_SH], w=[B_HGS])
                C.barrier()
                dump("hl", HL[:], [B_HL], [128, 8]); dump("pcl", PCL[:], [B_PCL], [128, 8])
                dump("pca", PCA[:].rearrange("p c k -> p (c k)"), [B_PCA], [128, 24])
                dump("xcs", XCS[:].rearrange("p c s -> p (c s)"), [B_XCS], [128, 32])
                dump("sh", S_H[:].rearrange("p c s -> p (c s)"), [B_SH], [128, 32])
                dump("h0s", H0S[:].rearrange("p c s -> p (c s)"), [B_H0S], [128, 32])
                chk(1)
                R2.reset(); R3.reset()
                RH = Region(arena, 32 * KB, 64 * KB)
                RES = R2.alloc([128, NB, D], F32)
                B_RES = [Buf("RES%d" % tb) for tb in range(NB)]
                HF = C.sb(top, "HF", [128, 8], F32)
                B_HF = Buf("HF")
                C.dma(lambda e: e.dma_start(out=e1s.ap(), in_=HL[:]), r=[B_HL], w=[B_e["e1s"]], sem=B_ser, serial=True)
                allgather(e1s, e1r, B_e["e1s"], B_e["e1r"])
                C.dma(lambda e: e.dma_start(out=HIN[:], in_=e1r.ap()[0:128, :]), r=[B_e["e1r"]], w=[B_HIN], sem=B_ser, serial=True)
                C.dve(lambda e: e.tensor_scalar(out=HIN[:], in0=HIN[:], scalar1=pmask[:, 0:1], scalar2=None, op0=ALU.mult),
                      r=[B_HIN, B_pm], w=[B_HIN])
                C.dve(lambda e: e.tensor_tensor(out=HF[:], in0=PCL[:], in1=HIN[:], op=ALU.mult), r=[B_PCL, B_HIN], w=[B_HF])
                C.dve(lambda e: e.tensor_tensor(out=HF[:], in0=HF[:], in1=HL[:], op=ALU.add), r=[B_HF, B_HL], w=[B_HF])
                dump("hf", HF[:], [B_HF], [128, 8])
                chk(2)
                wo_sb = RH.alloc([128, KC, D], BF16)
                B_wo = Buf("wo_sb")
                wost = C.ring(RH, "wost", 2, [128, D], F32)
                junk = RH.alloc([128, D], BF16)
                bj = Buf("junkB")
                for k in range(KC):
                    sg, bsg = wost.next()
                    C.dma(lambda e, sg=sg, k=k: e.dma_start(out=sg[:], in_=w_out[k * 128:(k + 1) * 128, :]), w=[bsg], sem=bsg)
                    C.pool(lambda e, sg=sg, k=k: e.tensor_copy(out=wo_sb[:, k, :], in_=sg[:]), r=[bsg], w=[B_wo])
                ps, bps = sample_proj(wo_sb, B_wo, KC, lambda k: HGS[:, k, :], B_HGS)
                C.dve(lambda e, ps=ps: e.tensor_tensor(out=RST[:], in0=XST[:, :, 0:NS],
                                                       in1=ps[:, 0:KC * NS].rearrange("p (c s) -> p c s", s=NS), op=ALU.add),
                      r=[bps, B_XST], w=[B_RST])
                sample_norm()
                hgr = C.ring(R3, "hg", 2, [128, KC, 512], BF16)
                xnr = C.ring(R3, "xnB", 2, [128, D], BF16)
                ssr = C.ring(R3, "ssB", 2, [128, 4], F32)
                for tt in range(4):
                    sl = slice(tt * 512, (tt + 1) * 512)
                    hg, bhg = hgr.next()
                    for c in range(KC):
                        C.dve(lambda e, hg=hg, c=c, sl=sl: e.scalar_tensor_tensor(
                            out=hg[:, c, :], in0=PG[:, c, sl], scalar=HIN[:, c:c + 1], in1=BIGB[:, c, sl],
                            op0=ALU.mult, op1=ALU.add), r=[B_PG[tt], B_HIN, B_BIG[tt]], w=[bhg])
                    for tbl in range(4):
                        tb = tt * 4 + tbl
                        C.dma(lambda e, tb=tb: e.dma_start(out=RES[:, tb, :], in_=xp[tb * 128:(tb + 1) * 128, :]),
                              w=[B_RES[tb]], sem=B_ser, serial=True)
                        for half in range(2):
                            ps, bps = psf.next()
                            for c in range(KC):
                                C.pe(lambda e, ps=ps, hg=hg, c=c, tbl=tbl, half=half: e.matmul(
                                    ps[:], lhsT=hg[:, c, tbl * 128:(tbl + 1) * 128], rhs=wo_sb[:, c, half * 512:(half + 1) * 512],
                                    start=(c == 0), stop=(c == KC - 1)), r=[bhg, B_wo], w=[bps])
                            C.dve(lambda e, ps=ps, tb=tb, half=half: e.tensor_tensor(
                                out=RES[:, tb, half * 512:(half + 1) * 512], in0=RES[:, tb, half * 512:(half + 1) * 512],
                                in1=ps[:], op=ALU.add), r=[bps, B_RES[tb]], w=[B_RES[tb]])
                C.barrier()
                norm_pass()
                dump("res1", RES, B_RES, [128, NB, D])
                chk(3)
                HALO = [C.sb(top, "HALO%d" % l, [128, KC, 2], BF16) for l in range(2)]
                B_HALO = [Buf("HALO0"), Buf("HALO1")]

                def halo_exchange(l, es, er, nes, ner):
                    C.dma(lambda e: e.dma_start(out=es.ap().rearrange("p (c x) -> p c x", c=KC), in_=BIGB[:, :, NT - 2:NT]),
                          r=[B_BIG[3]], w=[B_e[nes]], sem=B_ser, serial=True)
                    allgather(es, er, B_e[nes], B_e[ner])
                    C.dma(lambda e: e.dma_start(out=HALO[l][:], in_=er.ap()[0:128, :].rearrange("p (c x) -> p c x", c=KC)),
                          r=[B_e[ner]], w=[B_HALO[l]], sem=B_ser, serial=True)
                    C.dve(lambda e: e.tensor_scalar(out=HALO[l][:], in0=HALO[l][:], scalar1=pmask[:, 0:1], scalar2=None,
                                                    op0=ALU.mult), r=[B_HALO[l], B_pm], w=[B_HALO[l]])
                halo_exchange(0, e2s, e2r, "e2s", "e2r")
                C.barrier()
                dump("halo0", HALO[0][:].rearrange("p c x -> p (c x)"), [B_HALO[0]], [128, 16])
                chk(4)

            RA = Region(arena, 32 * KB, 96 * KB)
            acts_r = C.ring(top, "actsS", 2, [128, 4, NS], BF16)

            import os
            FFN_NG = int(os.environ.get("FFN_NG", FC // 4))
            FFN_HALO = int(os.environ.get("FFN_HALO", 1))
            FFN_DOWN = int(os.environ.get("FFN_DOWN", 1))

            def ffn(l):
                RA.reset(); R3.reset()
                actr = C.ring(RA, "act", 2, [128, 4, NT], BF16)
                FS4 = FS[:].rearrange("p l c (s k) -> p l c s k", k=2)
                wdnr = C.ring(RA, "wdn", 2, [128, 4, D], BF16)
                acc = RA.alloc([128, NT], F32); b_acc = Buf("acc%d" % l)
                gel = RA.alloc([128, NT], BF16); b_gel = Buf("gel%d" % l)
                wdst = C.ring(RA, "wdst", 1, [128, 1, D], F32)
                gsbr = C.ring(R3, "gsb", 2, [128, 2 + NT + 6], F32)
                wst = C.ring(R3, "wstF", 3, [128, KC, 128], F32)
                wbf = C.ring(R3, "wbfF", 3, [128, KC, 128], BF16)
                wup3 = w_up[l].rearrange("(k p) n -> p k n", p=128)
                gcol = CV_NF0 + 8 * l
                cvf = CV_F[l]
                pend_v = None

                def do_v(ci, act_t, bact, acts, bacts):
                    wv, bwv = load_w(wup3[:, :, DFF + ci * 128:DFF + (ci + 1) * 128], KC, 128, gcol, wst, wbf)
                    pss, bpss = psf.next()
                    for k in range(KC):
                        C.pe(lambda e, pss=pss, wv=wv, k=k: e.matmul(pss[:, 0:NS], lhsT=wv[:, k, :], rhs=HNS[:, k, 0:NS],
                                                                      start=(k == 0), stop=(k == KC - 1)), r=[bwv, B_HNS], w=[bpss])
                    C.dve(lambda e, pss=pss, ci=ci, acts=acts: e.tensor_tensor(out=acts[:, ci % 4, :], in0=GELS[:, ci, :],
                                                                             in1=pss[:, 0:NS], op=ALU.mult),
                          r=[bpss, B_GELS], w=[bacts])
                    for tt in range(4):
                        sl = slice(tt * 512, (tt + 1) * 512)
                        ps, bps = psf.next()
                        for k in range(KC):
                            C.pe(lambda e, ps=ps, wv=wv, k=k, sl=sl: e.matmul(ps[:], lhsT=wv[:, k, :], rhs=BIGB[:, k, sl],
                                                                            start=(k == 0), stop=(k == KC - 1)),
                                 r=[bwv, B_BIG[tt]], w=[bps])
                        C.dve(lambda e, ps=ps, ci=ci, sl=sl, act_t=act_t: e.tensor_tensor(
                            out=act_t[:, ci % 4, sl], in0=gel[:, sl], in1=ps[:], op=ALU.mult), r=[bps, b_gel], w=[bact])

                def do_down(G, act_t, bact, wdn, bwdn, acts, bacts):
                    ps, bps = sample_proj(wdn, bwdn, 4, lambda k: acts[:, k, :], bacts)
                    C.dve(lambda e, ps=ps: e.tensor_tensor(out=RST[:], in0=RST[:],
                                                           in1=ps[:, 0:KC * NS].rearrange("p (c s) -> p c s", s=NS), op=ALU.add),
                          r=[bps, B_RST], w=[B_RST])
                    for tb in range(NB if FFN_DOWN else 0):
                        for half in range(2):
                            ps, bps = psf.next()
                            for ci in range(4):
                                C.pe(lambda e, ps=ps, ci=ci, tb=tb, half=half, act_t=act_t, wdn=wdn: e.matmul(
                                    ps[:], lhsT=act_t[:, ci, tb * 128:(tb + 1) * 128], rhs=wdn[:, ci, half * 512:(half + 1) * 512],
                                    start=(ci == 0), stop=(ci == 3)), r=[bact, bwdn], w=[bps])
                            C.dve(lambda e, ps=ps, tb=tb, half=half: e.tensor_tensor(
                                out=RES[:, tb, half * 512:(half + 1) * 512], in0=RES[:, tb, half * 512:(half + 1) * 512],
                                in1=ps[:], op=ALU.add), r=[bps, B_RES[tb]], w=[B_RES[tb]])

                pend_down = None
                for G in range(FFN_NG):
                    (act_t, bact), (wdn, bwdn), (acts, bacts) = actr.next(), wdnr.next(), acts_r.next()
                    for cl in range(4):
                        ci = G * 4 + cl
                        wg, bwg = load_w(wup3[:, :, ci * 128:(ci + 1) * 128], KC, 128, gcol, wst, wbf)
                        gsb, bgsb = gsbr.next()
                        for tt in range(4):
                            sl = slice(tt * 512, (tt + 1) * 512)
                            ps, bps = psf.next()
                            for k in range(KC):
                                C.pe(lambda e, ps=ps, wg=wg, k=k, sl=sl: e.matmul(ps[:], lhsT=wg[:, k, :], rhs=BIGB[:, k, sl],
                                                                                start=(k == 0), stop=(k == KC - 1)),
                                     r=[bwg, B_BIG[tt]], w=[bps])
                            C.act(lambda e, ps=ps, gsb=gsb, tt=tt: e.copy(out=gsb[:, 2 + tt * 512:2 + (tt + 1) * 512], in_=ps[:]),
                                  r=[bps], w=[bgsb])
                        ps, bps = psf.next()
                        for k in range(KC if FFN_HALO else 0):
                            C.pe(lambda e, ps=ps, wg=wg, k=k: e.matmul(ps[:, 0:2], lhsT=wg[:, k, :], rhs=HALO[l][:, k, :],
                                                                        start=(k == 0), stop=(k == KC - 1)),
                                 r=[bwg, B_HALO[l]], w=[bps])
                        C.act(lambda e, ps=ps, gsb=gsb: e.copy(out=gsb[:, 0:2], in_=ps[:, 0:2]), r=[bps], w=[bgsb])
                        C.act(lambda e, gsb=gsb, ci=ci: e.copy(out=PF[:, l, ci, :], in_=gsb[:, NT:NT + 2]), r=[bgsb], w=[B_PF])
                        pss, bpss = psf.next()
                        for k in range(KC):
                            C.pe(lambda e, pss=pss, wg=wg, k=k: e.matmul(pss[:, 0:NS], lhsT=wg[:, k, :], rhs=HNS[:, k, 0:NS],
                                                                          start=(k == 0), stop=(k == KC - 1)), r=[bwg, B_HNS], w=[bpss])
                        bt1 = B_tsm[1]
                        C.act(lambda e, pss=pss: e.copy(out=tsm[:, 6, :], in_=pss[:, 0:NS]), r=[bpss], w=[bt1])
                        C.act(lambda e, ci=ci: e.activation(out=tsm[:, 5, :], in_=FS4[:, l, ci, :, 0], func=AF.Identity,
                                                            scale=CV[:, cvf + ci:cvf + ci + 1],
                                                            bias=CV[:, cvf + 72 + ci:cvf + 72 + ci + 1]), r=[B_FS, B_CV, bt1], w=[bt1])
                        C.dve(lambda e, ci=ci: e.scalar_tensor_tensor(out=tsm[:, 5, :], in0=FS4[:, l, ci, :, 1],
                                                                      scalar=CV[:, cvf + 24 + ci:cvf + 24 + ci + 1],
                                                                      in1=tsm[:, 5, :], op0=ALU.mult, op1=ALU.add),
                              r=[B_FS, B_CV, bt1], w=[bt1])
                        C.dve(lambda e, ci=ci: e.scalar_tensor_tensor(out=tsm[:, 5, :], in0=tsm[:, 6, :],
                                                                      scalar=CV[:, cvf + 48 + ci:cvf + 48 + ci + 1],
                                                                      in1=tsm[:, 5, :], op0=ALU.mult, op1=ALU.add),
                              r=[B_CV, bt1], w=[bt1])
                        C.act(lambda e, ci=ci: e.activation(out=GELS[:, ci, :], in_=tsm[:, 5, :], func=AF.Gelu), r=[bt1], w=[B_GELS])
                        C.dve(lambda e, ci=ci: e.tensor_copy(out=FS4[:, l, ci, :, 0], in_=FS4[:, l, ci, :, 1]), r=[B_FS, bt1], w=[B_FS])
                        C.dve(lambda e, ci=ci: e.tensor_copy(out=FS4[:, l, ci, :, 1], in_=tsm[:, 6, :]), r=[B_FS, bt1], w=[B_FS])
                        if pend_v is not None:
                            do_v(*pend_v)
                        C.act(lambda e, gsb=gsb, ci=ci: e.activation(
                            out=acc[:], in_=gsb[:, 0:NT], func=AF.Identity, scale=CV[:, cvf + ci:cvf + ci + 1],
                            bias=CV[:, cvf + 72 + ci:cvf + 72 + ci + 1]), r=[bgsb, B_CV, b_gel], w=[b_acc])
                        for k in (1, 2):
                            C.dve(lambda e, gsb=gsb, ci=ci, k=k: e.scalar_tensor_tensor(
                                out=acc[:], in0=gsb[:, k:k + NT], scalar=CV[:, cvf + 24 * k + ci:cvf + 24 * k + ci + 1],
                                in1=acc[:], op0=ALU.mult, op1=ALU.add), r=[bgsb, B_CV, b_acc], w=[b_acc])
                        C.act(lambda e: e.activation(out=gel[:], in_=acc[:], func=AF.Gelu), r=[b_acc], w=[b_gel])
                        pend_v = (ci, act_t, bact, acts, bacts)
                        sg, bsg = wdst.next()
                        C.dma(lambda e, sg=sg, ci=ci: e.dma_start(out=sg[:, 0, :], in_=w_down[l][ci * 128:(ci + 1) * 128, :]),
                              w=[bsg], sem=bsg)
                        drip(1)
                        C.pool(lambda e, sg=sg, wdn=wdn, cl=cl: e.tensor_copy(out=wdn[:, cl, :], in_=sg[:, 0, :]),
                               r=[bsg], w=[bwdn])
                    if pend_down is not None:
                        do_down(*pend_down)
                    pend_down = None
                    do_v(*pend_v)
                    pend_v = None
                    pend_down = (G, act_t, bact, wdn, bwdn, acts, bacts)
                do_down(*pend_down)
                C.barrier()

            ffn(0)
            dump("pf", PF[:].rearrange("p l c k -> p (l c k)"), [B_PF], [128, 2 * FC * 2])
            dump("res2", RES, B_RES, [128, NB, D])
            chk(5)
            def sls(start, stride, n=128):
                return slice(start, start + (n - 1) * stride + 1, stride)

            norm_pass()
            RA.reset(); R3.reset()
            WKV = RA.alloc([128, KC, 3072], BF16)
            B_WKVs = [Buf("WKV%d" % i) for i in range(6)]
            B_WKV = None
            wst = C.ring(R3, "wstD", 3, [128, KC, 128], F32)
            wbf = C.ring(R3, "wbfD", 3, [128, KC, 128], BF16)
            rowr = C.ring(R3, "rowD", 2, [128, NT], BF16)
            vbr = C.ring(R3, "vbD", 2, [128, 512], BF16)
            vfr = C.ring(R3, "vfD", 2, [128, 512], F32)
            w_kv3 = w_kv.rearrange("(k p) n -> p k n", p=128)
            w_q3 = w_q.rearrange("(k p) n -> p k n", p=128)
            sample_norm()
            sstg = C.ring(R3, "sstg", 2, [NS, 512], F32)

            def sample_rows(rhs_fn, brhs, ncol, dst):
                ps, bps = psf.next()
                for k in range(KC):
                    C.pe(lambda e, ps=ps, k=k: e.matmul(ps[0:NS, 0:ncol], lhsT=HNS[:, k, 0:NS], rhs=rhs_fn(k),
                                                        start=(k == 0), stop=(k == KC - 1)), r=[B_HNS, brhs], w=[bps])
                stg, bstg = sstg.next()
                C.act(lambda e, ps=ps, stg=stg: e.copy(out=stg[0:NS, 0:ncol], in_=ps[0:NS, 0:ncol]), r=[bps], w=[bstg])
                C.dma(lambda e, stg=stg: e.dma_start(out=dst, in_=stg[0:NS, 0:ncol]), r=[bstg], sem=bstg)
            for s in range(24):
                sg, bsg = wst.next()
                C.dma(lambda e, sg=sg, s=s: e.dma_start(out=sg[:], in_=w_kv3[:, :, s * 128:(s + 1) * 128]), w=[bsg], sem=bsg)
                drip(1)
                C.pool(lambda e, sg=sg, s=s: e.tensor_tensor(out=WKV[:, :, s * 128:(s + 1) * 128], in0=sg[:],
                                                             in1=bc_last(CV[:, CV_KVN:CV_KVN + KC], 128), op=ALU.mult),
                       r=[bsg, B_CV], w=[B_WKVs[s // 4]])
            def kview(t):
                return t.ap()[0:28].rearrange("(e c) (p x) -> e c p x", c=4, x=128)

            def vview(t):
                return t.ap()[0:28].rearrange("(e q) (k x) -> e (q k) x", q=4, x=512)
            xs_k4 = [kview(t) for t in xs_k]
            xs_v3 = [vview(t) for t in xs_v]

            def featmajor(g, cc, lhs_fn, bw, scr, is_k):
                row, brow = rowr.next()
                for tt in range(4):
                    ps, bps = psf.next()
                    for k in range(KC):
                        C.pe(lambda e, ps=ps, k=k, tt=tt: e.matmul(ps[:], lhsT=lhs_fn(k), rhs=BIGB[:, k, tt * 512:(tt + 1) * 512],
                                                                   start=(k == 0), stop=(k == KC - 1)), r=[bw, B_BIG[tt]], w=[bps])
                    if g == 0:
                        o_ap, i_ap = row[:, tt * 512:(tt + 1) * 512], ps[:]
                    elif g == 1:
                        o_ap = row[:].rearrange("p (r x) -> p r x", r=4)[:, :, tt * 128:(tt + 1) * 128]
                        i_ap = ps[:].rearrange("p (q r) -> p r q", r=4)
                    else:
                        o_ap = row[:].rearrange("p (r x) -> p r x", r=16)[:, :, tt * 32:(tt + 1) * 32]
                        i_ap = ps[:].rearrange("p (q r) -> p r q", r=16)
                    if tt % 2 == 0:
                        C.act(lambda e, o_ap=o_ap, i_ap=i_ap: e.copy(out=o_ap, in_=i_ap), r=[bps], w=[brow])
                    else:
                        C.dve(lambda e, o_ap=o_ap, i_ap=i_ap: e.tensor_copy(out=o_ap, in_=i_ap), r=[bps], w=[brow])
                C.dma(lambda e, row=row: e.dma_start(out=scr.ap()[g, cc], in_=row[:]), r=[brow], sem=brow)
                if is_k:
                    if g == 2:
                        for j, (e0, e1) in enumerate(((0, 7), (7, 14), (14, 16))):
                            C.dma(lambda e, row=row, j=j, e0=e0, e1=e1: e.dma_start(
                                out=xs_k4[j][0:e1 - e0, cc].rearrange("e p x -> p e x"),
                                in_=row[:].rearrange("p (e x) -> p e x", e=16)[:, e0:e1, :]), r=[brow], sem=brow)
                    elif g == 1:
                        C.dma(lambda e, row=row: e.dma_start(
                            out=xs_k4[2][2:6, cc].rearrange("e p x -> p e x"),
                            in_=row[:].rearrange("p (r n x) -> p r n x", r=4, n=4)[:, :, 3, :]), r=[brow], sem=brow)
                    else:
                        C.dma(lambda e, row=row: e.dma_start(out=xs_k4[2][6, cc], in_=row[:, 1920:2048]), r=[brow], sem=brow)

            D_K = int(os.environ.get("D_K", 3)); D_Q = int(os.environ.get("D_Q", 3)); D_TOK = int(os.environ.get("D_TOK", 3))
            D_CC = int(os.environ.get("D_CC", 1))
            for g in range(D_K):
                for cc in range(4):
                    col0 = g * 512 + cc * 128
                    featmajor(g, cc, lambda k, col0=col0: WKV[:, k, col0:col0 + 128], B_WKVs[g], kt_scr, True)
            for s6 in range(6):
                g_ = s6 % 3
                dst = (sk if s6 < 3 else sv)[g_][:, WG[g_] - 1, :]
                sample_rows(lambda k, s6=s6: WKV[:, k, s6 * 512:(s6 + 1) * 512], B_WKVs[s6], 512, dst)

            for g in range(D_TOK):
                for u in range(16):
                    t0_, stp = unit_tok(g, u)
                    slot = unit_slot(g, u)
                    if not int(os.environ.get("TOK_SLOT", 1)):
                        slot = None
                    if g == 0:
                        orow = (lambda o: o[0:128, :])
                    elif g == 1:
                        orow = (lambda o, u=u: o.rearrange("(i r) d -> r i d", r=4)[u // 4])
                    else:
                        orow = (lambda o, u=u: o.rearrange("(i r) d -> r i d", r=16)[u])
                    for which in ("v", "k"):
                        if which == "k" and slot is None:
                            continue
                        cbase = 1536 + g * 512 if which == "v" else g * 512
                        ps, bps = psf.next()
                        for k in range(KC):
                            C.pe(lambda e, ps=ps, k=k, t0_=t0_, stp=stp, cbase=cbase: e.matmul(
                                ps[:], lhsT=BIGB[:, k, sls(t0_, stp)], rhs=WKV[:, k, cbase:cbase + 512],
                                start=(k == 0), stop=(k == KC - 1)), r=B_BIG + [B_WKVs[cbase // 512]], w=[bps])
                        vf = bvf = None
                        if slot is not None:
                            vf, bvf = vfr.next()
                            C.dve(lambda e, vf=vf, ps=ps: e.tensor_copy(out=vf[:], in_=ps[:]), r=[bps], w=[bvf])
                            dst = orow((o_v if which == "v" else o_k)[g])
                            C.dma(lambda e, vf=vf, dst=dst: e.dma_start(out=dst, in_=vf[:]), r=[bvf], sem=bvf)
                        if which == "v":
                            vb, bvb = vbr.next()
                            if vf is None:
                                C.act(lambda e, vb=vb, ps=ps: e.copy(out=vb[:], in_=ps[:]), r=[bps], w=[bvb])
                            else:
                                C.act(lambda e, vb=vb, vf=vf: e.copy(out=vb[:], in_=vf[:]), r=[bvf], w=[bvb])
                            C.dma(lambda e, vb=vb, g=g, u=u: e.dma_start(out=v_scr.ap()[g, u], in_=vb[:]), r=[bvb], sem=bvb)
                            if slot is not None:
                                C.dma(lambda e, vb=vb, slot=slot: e.dma_start(out=xs_v3[slot // 7][slot % 7], in_=vb[:]), r=[bvb], sem=bvb)
            C.barrier()
            if D_CC:
                for j in range(3):
                    allgather(xs_k[j], xr_k[j], B_xsk, B_xrk)
                    allgather(xs_v[j], xr_v[j], B_xsv, B_xrv)
            for g in range(D_Q):
                for cc in range(4):
                    col0 = g * 512 + cc * 128
                    wq, bwq = load_w(w_q3[:, :, col0:col0 + 128], KC, 128, CV_NM1, wst, wbf, eng="dve")
                    featmajor(g, cc, lambda k, wq=wq: wq[:, k, :], bwq, qt_scr, False)
                    sample_rows(lambda k, wq=wq: wq[:, k, :], bwq, 128, qs_scr.ap()[:, col0:col0 + 128])
            C.barrier()
            chk(6)

            R0.reset(); RA.reset(); R3.reset()
            ACC = RA.alloc([128, 4, 2, NT], F32)
            B_ACC = [Buf("ACC%d" % cc) for cc in range(4)]
            OT = R3.alloc([128, 4, NT], BF16)
            B_OT = [Buf("OT%d" % i) for i in range(4)]
            WO = R3.alloc([128, 4, D], BF16)
            B_WO = Buf("WO")
            wost = C.ring(R3, "wostE", 2, [128, D], F32)
            for cc in range(4):
                sg, bsg = wost.next()
                C.dma(lambda e, sg=sg, cc=cc: e.dma_start(out=sg[:], in_=w_o[cc * 128:(cc + 1) * 128, :]), w=[bsg], sem=bsg)
                C.pool(lambda e, sg=sg, cc=cc: e.tensor_copy(out=WO[:, cc, :], in_=sg[:]), r=[bsg], w=[B_WO])
            NU = 2
            qzr = C.ring(R0, "QZ", NU, [128, 4, 2, 128], BF16)
            ktr = C.ring(R0, "KT", NU, [128, 4, 2, 128], BF16)
            vrr = C.ring(R0, "VR", NU, [128, 2, 4, 2, 64], BF16)
            vzr = C.ring(R0, "VZ", NU, [128, 2, 4, 2, 128], BF16)
            ptr_ = C.ring(R0, "PT", 4, [128, 512], BF16)
            for (t, bt) in qzr.items + vzr.items:
                C.pool(lambda e, t=t: e.memset(t[:], 0.0), w=[bt])
            xr_k4 = [kview(t) for t in xr_k]
            xr_v3 = [vview(t) for t in xr_v]
            def setup_unit(g, u):
                t0_, stp = unit_tok(g, u)
                pk, pi = unit_prev(g, u)
                (QZ, bqz), (KT, bkt), (VR, bvr), (VZ, bvz) = qzr.next(), ktr.next(), vrr.next(), vzr.next()
                cols = slice(u * 128, (u + 1) * 128)
                for par in range(2):
                    C.dma(lambda e, QZ=QZ, par=par, cols=cols, g=g: e.dma_start(
                        out=QZ[par * 64:(par + 1) * 64, :, par, :],
                        in_=qt_scr.ap()[g, :, par * 64:(par + 1) * 64, cols].rearrange("c p x -> p c x")),
                        w=[bqz], sem=bqz)
                C.dma(lambda e, KT=KT, cols=cols, g=g: e.dma_start(
                    out=KT[:, :, 1, :], in_=kt_scr.ap()[g, :, :, cols].rearrange("c p x -> p c x")), w=[bkt], sem=bkt)
                C.dma(lambda e, VR=VR, g=g, u=u: e.dma_start(out=VR[:, 1].rearrange("p c t d -> p (c t d)"),
                                                             in_=v_scr.ap()[g, u]), w=[bvr], sem=bvr)
                if pk == "own":
                    pcols = slice(pi * 128, (pi + 1) * 128)
                    ksrc = kt_scr.ap()[g, :, :, pcols].rearrange("c p x -> p c x")
                    vsrc = v_scr.ap()[g, pi]
                    mprev = 0
                else:
                    ksrc = xr_k4[pi // 7][pi % 7].rearrange("c p x -> p c x")
                    vsrc = xr_v3[pi // 7][pi % 7]
                    mprev = 1
                C.dma(lambda e, KT=KT, ksrc=ksrc: e.dma_start(out=KT[:, :, 0, :], in_=ksrc), w=[bkt], sem=bkt)
                C.dma(lambda e, VR=VR, vsrc=vsrc: e.dma_start(out=VR[:, 0].rearrange("p c t d -> p (c t d)"), in_=vsrc),
                      w=[bvr], sem=bvr)
                for par in range(2):
                    C.pool(lambda e, VZ=VZ, VR=VR, par=par: e.tensor_copy(
                        out=VZ[:, :, :, par, par * 64:(par + 1) * 64], in_=VR[:, :, :, par, :]), r=[bvr], w=[bvz])
                return dict(QZ=QZ, bqz=bqz, KT=KT, bkt=bkt, VZ=VZ, bvz=bvz, mprev=mprev, t0_=t0_, stp=stp, g=g)

            def stageS(cx, cc):
                QZ, bqz, KT, bkt, VZ, bvz, mprev, t0_, stp, g = (cx[k_] for k_ in ('QZ', 'bqz', 'KT', 'bkt', 'VZ', 'bvz', 'mprev', 't0_', 'stp', 'g'))
                pss, bpss = psf.next()
                for blk, mi in ((0, mprev), (1, 2)):
                    C.pe(lambda e, pss=pss, KT=KT, QZ=QZ, cc=cc, blk=blk: e.matmul(
                        pss[:, blk * 256:(blk + 1) * 256], lhsT=KT[:, cc, blk, :],
                        rhs=QZ[:, cc, :, :].rearrange("p a q -> p (a q)"), start=True, stop=False), r=[bkt, bqz], w=[bpss])
                    C.pe(lambda e, pss=pss, blk=blk, mi=mi: e.matmul(
                        pss[:, blk * 256:(blk + 1) * 256], lhsT=ident_b[:], rhs=bc_mid(maskb[:, mi, :], 2),
                        start=False, stop=True), r=[B_ident, B_maskb], w=[bpss])
                PT, bpt = ptr_.next()
                C.act(lambda e, PT=PT, pss=pss: e.activation(out=PT[:], in_=pss[:], func=AF.Exp, scale=0.125),
                      r=[bpss], w=[bpt])
                return PT, bpt

            def stageO(cx, cc, PT, bpt):
                QZ, bqz, KT, bkt, VZ, bvz, mprev, t0_, stp, g = (cx[k_] for k_ in ('QZ', 'bqz', 'KT', 'bkt', 'VZ', 'bvz', 'mprev', 't0_', 'stp', 'g'))
                pso, bpso = psf.next()
                idx = 0
                for par in range(2):
                    for blk in range(2):
                        C.pe(lambda e, pso=pso, VZ=VZ, cc=cc, par=par, blk=blk, PT=PT, idx=idx: e.matmul(
                            pso[:, 0:128], lhsT=VZ[:, blk, cc, par, :], rhs=PT[:, (blk * 2 + par) * 128:(blk * 2 + par + 1) * 128],
                            start=(idx == 0), stop=(idx == 3)), r=[bvz, bpt], w=[bpso])
                        idx += 1
                idx = 0
                for par in range(2):
                    for blk in range(2):
                        C.pe(lambda e, pso=pso, par=par, blk=blk, PT=PT, idx=idx: e.matmul(
                            pso[:, 128:256], lhsT=esel[:, par, :], rhs=PT[:, (blk * 2 + par) * 128:(blk * 2 + par + 1) * 128],
                            start=(idx == 0), stop=(idx == 3)), r=[B_esel, bpt], w=[bpso])
                        idx += 1
                dsta = ACC[:, cc, :, sls(t0_, stp)]
                srca = pso[:, 0:256].rearrange("p (a x) -> p a x", a=2)
                if g == 0:
                    C.dve(lambda e, dsta=dsta, srca=srca: e.tensor_copy(out=dsta, in_=srca), r=[bpso], w=[B_ACC[cc]])
                else:
                    C.dve(lambda e, dsta=dsta, srca=srca: e.tensor_tensor(out=dsta, in0=dsta, in1=srca, op=ALU.add),
                          r=[bpso, B_ACC[cc]], w=[B_ACC[cc]])

            pending = None
            for g in range(3):
                for u in range(16):
                    cx = setup_unit(g, u)
                    for cc in range(4):
                        PT, bpt = stageS(cx, cc)
                        if pending is not None:
                            stageO(*pending)
                        pending = (cx, cc, PT, bpt)
            stageO(*pending)
            tmpr = C.ring(R3, "tE", 2, [128, 512], F32)
            for tt in range(4):
                sl = slice(tt * 512, (tt + 1) * 512)
                for cc in range(4):
                    tm, btm = tmpr.next()
                    C.act(lambda e, tm=tm, cc=cc, sl=sl: e.activation(out=tm[:], in_=ACC[:, cc, 1, sl], func=AF.Ln),
                          r=[B_ACC[cc]], w=[btm])
                    C.act(lambda e, tm=tm: e.activation(out=tm[:], in_=tm[:], func=AF.Exp, scale=-1.0), r=[btm], w=[btm])
                    C.dve(lambda e, tm=tm, cc=cc, sl=sl: e.tensor_tensor(out=OT[:, cc, sl], in0=ACC[:, cc, 0, sl], in1=tm[:],
                                                                        op=ALU.mult), r=[btm, B_ACC[cc]], w=[B_OT[tt]])
                for tbl in range(4):
                    tb = tt * 4 + tbl
                    for half in range(2):
                        ps, bps = psf.next()
                        for cc in range(4):
                            C.pe(lambda e, ps=ps, cc=cc, tb=tb, half=half: e.matmul(
                                ps[:], lhsT=OT[:, cc, tb * 128:(tb + 1) * 128], rhs=WO[:, cc, half * 512:(half + 1) * 512],
                                start=(cc == 0), stop=(cc == 3)), r=[B_OT[tt], B_WO], w=[bps])
                        C.dve(lambda e, ps=ps, tb=tb, half=half: e.tensor_tensor(
                            out=RES[:, tb, half * 512:(half + 1) * 512], in0=RES[:, tb, half * 512:(half + 1) * 512],
                            in1=ps[:], op=ALU.add), r=[bps, B_RES[tb]], w=[B_RES[tb]])
            C.barrier()
            R0.reset()
            kcr = C.ring(R0, "kcS", 2, [128, 512], F32)
            vcr = C.ring(R0, "vcS", 2, [128, 512], F32)
            qbr = C.ring(R0, "qbS", 2, [128, 512], F32)
            knr = C.ring(R0, "knS", 2, [1, 2, 512], F32)
            prod = R0.alloc([128, 512], F32); b_prod = Buf("prodS")
            OACC = R0.alloc([1, NS, 512], F32); b_oacc = Buf("OACC")
            smr = C.ring(R0, "smS", 2, [128, 4, 8], F32)
            DACC = C.sb(top, "DACC", [1, NS * 8], F32); b_dacc = Buf("DACC")
            OTS = C.sb(top, "OTS", [128, 4, NS], BF16); b_ots = Buf("OTS")
            DIL = (1, 4, 16)
            for s in range(NS):
                for g in range(3):
                    (Kc, bkc), (Vc, bvc), (qb, bqb), (kn, bkn), (sm, bsm) = kcr.next(), vcr.next(), qbr.next(), knr.next(), smr.next()
                    C.dma(lambda e, Kc=Kc, s=s, g=g: e.dma_start(out=Kc[:], in_=ck[g][s].rearrange("(i d) x -> d i x", d=DIL[g])[0]),
                          w=[bkc], sem=bkc)
                    C.dma(lambda e, Vc=Vc, s=s, g=g: e.dma_start(out=Vc[:], in_=cv[g][s].rearrange("(i d) x -> d i x", d=DIL[g])[0]),
                          w=[bvc], sem=bvc)
                    C.dma(lambda e, qb=qb, s=s, g=g: e.dma_start(out=qb[:], in_=qs_scr.ap()[s, g * 512:(g + 1) * 512].partition_broadcast(128)),
                          w=[bqb], sem=bqb)
                    C.dma(lambda e, kn=kn, s=s, g=g: e.dma_start(out=kn[0:1, 0, :], in_=sk[g][s, WG[g] - 1:WG[g], :]), w=[bkn], sem=bkn)
                    C.dma(lambda e, kn=kn, s=s, g=g: e.dma_start(out=kn[0:1, 1, :], in_=sv[g][s, WG[g] - 1:WG[g], :]), w=[bkn], sem=bkn)
                    C.dve(lambda e, Kc=Kc, qb=qb: e.tensor_tensor(out=prod[:], in0=Kc[:], in1=qb[:], op=ALU.mult), r=[bkc, bqb], w=[b_prod])
                    C.dve(lambda e, sm=sm: e.tensor_reduce(out=sm[:, 0, :], in_=prod[:].rearrange("p (h d) -> p h d", d=64),
                                                           axis=mybir.AxisListType.X, op=ALU.add), r=[b_prod], w=[bsm])
                    C.act(lambda e, sm=sm: e.activation(out=sm[:, 1, :], in_=sm[:, 0, :], func=AF.Exp, scale=0.125), r=[bsm], w=[bsm])
                    C.dve(lambda e, kn=kn, qb=qb: e.tensor_tensor(out=prod[0:1, :], in0=kn[0:1, 0, :], in1=qb[0:1, :], op=ALU.mult),
                          r=[bkn, bqb, b_prod], w=[b_prod])
                    C.dve(lambda e, sm=sm: e.tensor_reduce(out=sm[0:1, 2, :], in_=prod[0:1, :].rearrange("p (h d) -> p h d", d=64),
                                                           axis=mybir.AxisListType.X, op=ALU.add), r=[b_prod, bsm], w=[bsm])
                    C.act(lambda e, sm=sm: e.activation(out=sm[0:1, 3, :], in_=sm[0:1, 2, :], func=AF.Exp, scale=0.125), r=[bsm], w=[bsm])
                    pso, bpso = psf.next()
                    for h in range(8):
                        C.pe(lambda e, pso=pso, sm=sm, Vc=Vc, h=h: e.matmul(pso[0:1, h * 64:(h + 1) * 64], lhsT=sm[:, 1, h:h + 1],
                                                                           rhs=Vc[:, h * 64:(h + 1) * 64], start=True, stop=False),
                             r=[bsm, bvc], w=[bpso])
                        C.pe(lambda e, pso=pso, sm=sm, kn=kn, h=h: e.matmul(pso[0:1, h * 64:(h + 1) * 64], lhsT=sm[0:1, 3, h:h + 1],
                                                                           rhs=kn[0:1, 1, h * 64:(h + 1) * 64], start=False, stop=True),
                             r=[bsm, bkn], w=[bpso])
                    psd, bpsd = psf.next()
                    C.pe(lambda e, psd=psd, sm=sm: e.matmul(psd[0:1, 0:8], lhsT=ones_f[:, 0:1], rhs=sm[:, 1, :], start=True, stop=False),
                         r=[bsm, B_ones], w=[bpsd])
                    C.pe(lambda e, psd=psd, sm=sm: e.matmul(psd[0:1, 0:8], lhsT=ones_f[0:1, 0:1], rhs=sm[0:1, 3, :], start=False, stop=True),
                         r=[bsm, B_ones], w=[bpsd])
                    if g == 0:
                        C.dve(lambda e, pso=pso, s=s: e.tensor_copy(out=OACC[0:1, s, :], in_=pso[0:1, :]), r=[bpso], w=[b_oacc])
                        C.dve(lambda e, psd=psd, s=s: e.tensor_copy(out=DACC[0:1, s * 8:(s + 1) * 8], in_=psd[0:1, 0:8]), r=[bpsd], w=[b_dacc])
                    else:
                        C.dve(lambda e, pso=pso, s=s: e.tensor_tensor(out=OACC[0:1, s, :], in0=OACC[0:1, s, :], in1=pso[0:1, :], op=ALU.add),
                              r=[bpso, b_oacc], w=[b_oacc])
                        C.dve(lambda e, psd=psd, s=s: e.tensor_tensor(out=DACC[0:1, s * 8:(s + 1) * 8], in0=DACC[0:1, s * 8:(s + 1) * 8],
                                                                      in1=psd[0:1, 0:8], op=ALU.add), r=[bpsd, b_dacc], w=[b_dacc])
            C.dve(lambda e: e.reciprocal(out=DACC[:], in_=DACC[:]), r=[b_dacc], w=[b_dacc])
            OA3 = OACC[0:1].rearrange("p s (h d) -> p (s h) d", d=64)
            C.dve(lambda e: e.tensor_tensor(out=OA3, in0=OA3, in1=bc_last(DACC[0:1, :], 64), op=ALU.mult), r=[b_oacc, b_dacc], w=[b_oacc])
            pst, bpst = psf.next()
            for cc in range(4):
                for s in range(NS):
                    C.pe(lambda e, pst=pst, cc=cc, s=s: e.transpose(pst[:, cc * NS + s:cc * NS + s + 1], OACC[0:1, s, cc * 128:(cc + 1) * 128],
                                                                    ident_f[0:1, 0:1]), r=[b_oacc, B_ident], w=[bpst])
            C.act(lambda e, pst=pst: e.copy(out=OTS[:], in_=pst[:, 0:4 * NS].rearrange("p (c s) -> p c s", s=NS)), r=[bpst], w=[b_ots])
            ps, bps = sample_proj(WO, B_WO, 4, lambda k: OTS[:, k, :], b_ots)
            C.dve(lambda e, ps=ps: e.tensor_tensor(out=RST[:], in0=RST[:], in1=ps[:, 0:KC * NS].rearrange("p (c s) -> p c s", s=NS),
                                                   op=ALU.add), r=[bps, B_RST], w=[B_RST])
            C.barrier()
            chk(7)
            norm_pass()
            sample_norm()
            halo_exchange(1, e4s, e4r, "e4s", "e4r")
            C.barrier()
            ffn(1)
            R3.reset()
            gfin = R3.alloc([128, D], F32)
            C.dma(lambda e: e.dma_start(out=gfin[:], in_=final_norm.partition_broadcast(128)), w=[B_gfin], sem=B_ser, serial=True)
            yr = C.ring(R3, "yt", 2, [128, D], F32)
            ssr = C.ring(R3, "ssY", 2, [128, 4], F32)
            junk = R3.alloc([128, D], BF16); bj = Buf("junkY")
            ssa = R3.alloc([128, 3, NB], F32); bssa = Buf("ssaY")
            for tb in range(NB):
                C.act(lambda e, tb=tb: e.activation(out=junk[:], in_=RES[:, tb, :], func=AF.Square, accum_out=ssa[:, 0, tb:tb + 1]),
                      r=[B_RES[tb]], w=[bj, bssa])
            C.act(lambda e: e.activation(out=ssa[:, 1, :], in_=ssa[:, 0, :], func=AF.Sqrt, scale=1.0 / D, bias=EPS), r=[bssa], w=[bssa])
            C.dve(lambda e: e.reciprocal(out=ssa[:, 2, :], in_=ssa[:, 1, :]), r=[bssa], w=[bssa])
            for tb in range(NB):
                yt, byt = yr.next()
                C.dve(lambda e, yt=yt, tb=tb: e.scalar_tensor_tensor(
                    out=yt[:], in0=RES[:, tb, :], scalar=ssa[:, 2, tb:tb + 1], in1=gfin[:], op0=ALU.mult, op1=ALU.mult),
                    r=[B_RES[tb], bssa, B_gfin], w=[byt])
                C.dma(lambda e, yt=yt, tb=tb: e.dma_start(out=y_out[tb * 128:(tb + 1) * 128, :], in_=yt[:]), r=[byt], sem=byt)
            drip(len(drip_list))
            R3.reset()
            so = R3.alloc([64, 3072], F32); bso = Buf("so")

            def rows_out(src_fn, nchunk, nrow, dst):
                nonlocal_ps = psf.next()
                ps, bps = nonlocal_ps
                done = 0
                while done < nchunk:
                    n = min(4, nchunk - done)
                    ps, bps = psf.next()
                    for i in range(n):
                        C.pe(lambda e, ps=ps, i=i, c=done + i: e.transpose(ps[0:nrow, i * 128:(i + 1) * 128], src_fn(c), ident_f[:]),
                             r=[B_ident, B_HF, B_PCA, B_PF, B_sq, B_SH, B_CS, B_FS], w=[bps])
                    C.dve(lambda e, ps=ps, n=n, done=done: e.tensor_copy(out=so[0:nrow, done * 128:(done + n) * 128],
                                                                        in_=ps[0:nrow, 0:n * 128]), r=[bps], w=[bso])
                    done += n
                C.dma(lambda e: e.dma_start(out=dst, in_=so[0:nrow, 0:nchunk * 128]), r=[bso], sem=B_ser, serial=True)
            sample_rstd()
            C.dve(lambda e: e.tensor_tensor(out=sq_s[:], in0=RST[:], in1=bc_mid(st_s[:, 1, :], KC), op=ALU.mult),
                  r=[B_RST, B_sts, B_sq], w=[B_sq])
            C.dve(lambda e: e.tensor_tensor(out=sq_s[:], in0=sq_s[:], in1=bc_last(CV[:, CV_FIN:CV_FIN + KC], NS), op=ALU.mult),
                  r=[B_sq, B_CV], w=[B_sq])
            rows_out(lambda c: sq_s[:, c, :], 8, NS, ys_out)
            rows_out(lambda c: S_H[:, c, :], 8, NS, o_slru)
            rows_out(lambda c: CS[:, c, :], 8, NS * 3, o_sconva)
            for l in range(2):
                rows_out(lambda c, l=l: FS[:, l, c, :], FC, NS * 2, o_sffn[l])
            rows_out(lambda c: HF[:, c:c + 1], 8, 1, o_lru.rearrange("c p -> (c p)").rearrange("(o n) -> o n", o=1))
            rows_out(lambda c: PCA[:, c, :], 8, 3, o_conva)
            for l in range(2):
                rows_out(lambda c, l=l: PF[:, l, c, :], FC, 2, o_ffn[l])
        except StopBuild:
            pass
        P.emit(nc, top, block)
    return nc


WEIGHT_NAMES = ["a_w_in", "a_conv_w", "a_conv_b", "a_gate_a_w", "a_gate_a_b", "a_gate_x_w", "a_gate_x_b", "a_lambda",
                "a_w_out", "kv_norm", "w_kv", "b_w_q", "b_w_o", "norm_mix", "norm_ffn", "ffn_w_up", "ffn_conv_w",
                "ffn_conv_b", "ffn_w_down", "final_norm"]
KEEP_LEAD = ("norm_mix", "norm_ffn", "ffn_w_up", "ffn_conv_w", "ffn_conv_b", "ffn_w_down")


def make_in_maps(inp):
    f32 = np.float32
    shared = {}
    for n in WEIGHT_NAMES:
        a = np.asarray(inp[n], dtype=f32)
        if a.shape[0] == 1 and n not in KEEP_LEAD:
            a = a[0]
        shared[n] = np.ascontiguousarray(a)
    kq = np.arange(128)
    m_prev = np.where(kq[:, None] >= kq[None, :], 0.0, NEG).astype(f32)
    m_cur = np.where(kq[:, None] <= kq[None, :], 0.0, NEG).astype(f32)
    xpr = np.asarray(inp["x_prompt"], dtype=f32)
    xsa = np.asarray(inp["x_sample"], dtype=f32)
    lru = np.asarray(inp["state_lru_h"], dtype=f32)
    cva = np.asarray(inp["state_conv_a"], dtype=f32)
    ffs = np.asarray(inp["state_ffn_conv"], dtype=f32)
    maps = []
    for c in range(8):
        b, half = c // 2, c % 2
        sl = slice(NS * c, NS * (c + 1))
        m = dict(shared)
        m["xp"] = np.ascontiguousarray(xpr[b, half * NT:(half + 1) * NT])
        xsm = np.zeros((8, D), f32)
        xsm[0:NS] = xsa[sl, 0]
        if half == 1:
            xsm[4:7] = xpr[b, NT - 3:NT]
        m["xsm"] = xsm
        cm = np.empty((128, 3, 128), f32)
        cm[:, 0] = m_prev
        cm[:, 1] = m_prev if half == 1 else NEG
        cm[:, 2] = m_cur
        m["cmask"] = cm
        m["pmask"] = np.full((128, 1), float(half), f32)
        m["s_lru0"] = np.ascontiguousarray(lru[0, sl])
        m["s_conva0"] = np.ascontiguousarray(cva[0, sl].reshape(NS * 3, D))
        m["s_ffn0"] = np.ascontiguousarray(ffs[:, sl].reshape(2, NS * 2, DFF))
        for g, (kn, vn) in enumerate((("cache_k0", "cache_v0"), ("cache_k1", "cache_v1"), ("cache_k2", "cache_v2"))):
            m["ck%d" % g] = np.ascontiguousarray(np.asarray(inp[kn], dtype=f32)[sl].reshape(NS, -1, 512))
            m["cv%d" % g] = np.ascontiguousarray(np.asarray(inp[vn], dtype=f32)[sl].reshape(NS, -1, 512))
        maps.append(m)
    return maps


_NC_CACHE = {}


def kernel(**inputs):
    if "nc" not in _NC_CACHE:
        _NC_CACHE["nc"] = build_program()
    nc = _NC_CACHE["nc"]
    maps = make_in_maps(inputs)
    res = run_bass_kernel_spmd(nc, maps, core_ids=list(range(8)))
    R = res.results
    f32 = np.float32

    def cat(name, axis=0):
        return np.concatenate([np.asarray(R[c][name], dtype=f32) for c in range(8)], axis=axis)
    y_prompt = np.stack([np.concatenate([R[2 * b]["y"], R[2 * b + 1]["y"]], axis=0) for b in range(4)]).astype(f32)
    y_sample = cat("ys").reshape(32, 1, D)
    p_lru = np.stack([R[2 * b + 1]["o_lru"].reshape(D) for b in range(4)])[None].astype(f32)
    s_lru = cat("o_slru").reshape(1, 32, D)
    p_conva = np.stack([R[2 * b + 1]["o_conva"] for b in range(4)])[None].astype(f32)
    s_conva = cat("o_sconva").reshape(1, 32, 3, D)
    p_ffn = np.stack([R[2 * b + 1]["o_ffn"] for b in range(4)], axis=1).astype(f32)
    s_ffn = np.concatenate([np.asarray(R[c]["o_sffn"], dtype=f32).reshape(2, NS, 2, DFF) for c in range(8)], axis=1)
    outs = [y_prompt, y_sample, p_lru, s_lru, p_conva, s_conva, p_ffn, s_ffn]
    for g in range(3):
        pk = np.stack([R[2 * b + 1]["o_k%d" % g].reshape(-1, 8, 64) for b in range(4)]).astype(f32)
        pv = np.stack([R[2 * b + 1]["o_v%d" % g].reshape(-1, 8, 64) for b in range(4)]).astype(f32)
        skg = cat("sk%d" % g).reshape(32, -1, 8, 64)
        svg = cat("sv%d" % g).reshape(32, -1, 8, 64)
        outs += [pk, pv, skg, svg]
    return tuple(np.ascontiguousarray(o, dtype=f32) for o in outs)
```
